# Optimizing a Trainium2 kernel written in Bass

```python
import jax, jax.numpy as jnp
from jax import lax
import numpy as np

D_MODEL = 1024
BATCH = 4
SEQ = 4096
DEPTH = 2
DEC_BATCH = 128
DEC_SEQ = 1
PAST_LEN = 8192
PAGE_SIZE = 128

RET_HEADS = 4
RET_DK = 64
RET_DV = 128
RET_CHUNK = 128
ATT_HEADS = 4
KV_HEADS = 2
HEAD_DIM = 64
WINDOW = 128
ATT_BLOCK = 128
CONV_CH = 256
CONV_K = 31
FFN_DIM = 2816
EPS = 1e-6
NEG_INF = -1e30

SPLITS = (RET_HEADS * RET_DK, RET_HEADS * RET_DK, RET_HEADS * RET_DV, RET_HEADS * RET_DV,
          ATT_HEADS * HEAD_DIM, KV_HEADS * HEAD_DIM, KV_HEADS * HEAD_DIM, 2 * CONV_CH)
IN_DIM = sum(SPLITS)
MIX_DIM = RET_HEADS * RET_DV + ATT_HEADS * HEAD_DIM + CONV_CH

kernel_name = 'hymba_style_retention_swa_conformer_step'


def rms_norm(x, g):
    xf = x.astype(jnp.float32)
    y = xf * lax.rsqrt(jnp.mean(xf * xf, axis=-1, keepdims=True) + EPS)
    return (y * g.astype(jnp.float32)).astype(x.dtype)


def layer_norm(x, g, b):
    xf = x.astype(jnp.float32)
    mu = jnp.mean(xf, axis=-1, keepdims=True)
    var = jnp.mean(jnp.square(xf - mu), axis=-1, keepdims=True)
    y = (xf - mu) * lax.rsqrt(var + EPS) * g.astype(jnp.float32) + b.astype(jnp.float32)
    return y.astype(x.dtype)


def swiglu_ffn(x, wg, wu, wd):
    return (jax.nn.silu(x @ wg) * (x @ wu)) @ wd


def retention_log_decay():
    return jnp.log1p(-jnp.exp2(-5.0 - jnp.arange(RET_HEADS, dtype=jnp.float32)))


def alibi_slopes():
    return jnp.exp2(-8.0 * (jnp.arange(ATT_HEADS, dtype=jnp.float32) + 1.0) / ATT_HEADS)


def retention(q, k, v, s0):
    B, L = q.shape[:2]
    c = RET_CHUNK if L % RET_CHUNK == 0 else L
    n = L // c
    lg = retention_log_decay()
    pos = jnp.arange(c, dtype=jnp.float32)
    diff = pos[:, None] - pos[None, :]
    inner_decay = jnp.where(diff >= 0, jnp.exp(lg[:, None, None] * jnp.maximum(diff, 0.0)), 0.0)
    q_decay = jnp.exp(lg[:, None] * (pos + 1.0))
    k_decay = jnp.exp(lg[:, None] * (c - 1.0 - pos))
    chunk_decay = jnp.exp(lg * c)

    def to_chunks(t):
        return t.astype(jnp.float32).reshape(B, n, c, RET_HEADS, t.shape[-1]).transpose(1, 0, 3, 2, 4)

    qc = to_chunks(q)
    kc = to_chunks(k * (RET_DK ** -0.5))
    vc = to_chunks(v)

    def step(s, inp):
        qi, ki, vi = inp
        a = jnp.einsum('bhid,bhjd->bhij', qi, ki) * inner_decay
        o = (jnp.einsum('bhij,bhjv->bhiv', a, vi)
             + jnp.einsum('bhid,bhdv->bhiv', qi, s) * q_decay[:, :, None])
        s = s * chunk_decay[:, None, None] + jnp.einsum('bhjd,bhjv->bhdv', ki * k_decay[:, :, None], vi)
        return s, o

    s_final, o = lax.scan(step, s0.astype(jnp.float32), (qc, kc, vc))
    o = o.transpose(1, 0, 3, 2, 4).reshape(B, L, RET_HEADS, RET_DV)
    return o, s_final


def swa_attention(q, k, v, k_buf, v_buf, sinks, base_pos):
    B, L = q.shape[:2]
    P = k_buf.shape[1]
    G = ATT_HEADS // KV_HEADS
    k_all = jnp.concatenate([k_buf.astype(k.dtype), k], axis=1)
    v_all = jnp.concatenate([v_buf.astype(v.dtype), v], axis=1)
    bq = ATT_BLOCK if L % ATT_BLOCK == 0 else L
    nb = L // bq
    kidx = jnp.arange(nb)[:, None] * bq + jnp.arange(bq + P)[None, :]
    kb = jnp.take(k_all, kidx, axis=1)
    vb = jnp.take(v_all, kidx, axis=1)
    qb = q.reshape(B, nb, bq, KV_HEADS, G, HEAD_DIM)
    s = jnp.einsum('bnqhgd,bnshd->bnhgqs', qb, kb).astype(jnp.float32)
    qpos = base_pos + jnp.arange(L).reshape(nb, bq)
    kpos = base_pos - P + kidx
    dist = qpos[:, :, None] - kpos[:, None, :]
    valid = (dist >= 0) & (dist < WINDOW) & (kpos[:, None, :] >= 0)
    slopes = alibi_slopes().reshape(KV_HEADS, G)
    bias = -slopes[None, :, :, None, None] * dist[:, None, None].astype(jnp.float32)
    s = jnp.where(valid[:, None, None], s + bias, NEG_INF)
    sink = jnp.broadcast_to(sinks.astype(jnp.float32).reshape(KV_HEADS, G)[None, None, :, :, None, None],
                            s.shape[:-1] + (1,))
    p = jax.nn.softmax(jnp.concatenate([s, sink], axis=-1), axis=-1)[..., :-1]
    o = jnp.einsum('bnhgqs,bnshd->bnqhgd', p.astype(v.dtype), vb).reshape(B, L, ATT_HEADS * HEAD_DIM)
    return o, k_all[:, -WINDOW:], v_all[:, -WINDOW:]


def conformer_conv(a, gate, buf, w, b, ln_g, ln_b, pw):
    u = a * jax.nn.sigmoid(gate)
    u_all = jnp.concatenate([buf.astype(u.dtype), u], axis=1)
    y = lax.conv_general_dilated(u_all, w[:, None, :].astype(u.dtype), (1,), 'VALID',
                                 dimension_numbers=('NWC', 'WIO', 'NWC'),
                                 feature_group_count=CONV_CH) + b
    y = jax.nn.silu(layer_norm(y, ln_g, ln_b))
    return y @ pw, u_all[:, -(CONV_K - 1):]


def layer(x, ret_s, k_buf, v_buf, conv_buf, base_pos, p):
    B, L, _ = x.shape
    x = x + 0.5 * swiglu_ffn(rms_norm(x, p['ffn1_norm']), p['ffn1_wg'], p['ffn1_wu'], p['ffn1_wd'])
    h = rms_norm(x, p['mix_norm'])
    z = h @ p['w_in']
    rq, rk, rv, rg, aq, ak, av, cu = jnp.split(z, np.cumsum(SPLITS)[:-1].tolist(), axis=-1)
    o_r, ret_new = retention(rq.reshape(B, L, RET_HEADS, RET_DK), rk.reshape(B, L, RET_HEADS, RET_DK),
                             rv.reshape(B, L, RET_HEADS, RET_DV), ret_s)
    o_r = rms_norm(o_r, p['ret_norm_g']).astype(x.dtype).reshape(B, L, RET_HEADS * RET_DV) * jax.nn.silu(rg)
    q = rms_norm(aq.reshape(B, L, ATT_HEADS, HEAD_DIM), p['q_norm_g']) * (HEAD_DIM ** -0.5)
    k = rms_norm(ak.reshape(B, L, KV_HEADS, HEAD_DIM), p['k_norm_g'])
    v = av.reshape(B, L, KV_HEADS, HEAD_DIM)
    o_a, k_new, v_new = swa_attention(q, k, v, k_buf, v_buf, p['sinks'], base_pos)
    ca, cg = jnp.split(cu, 2, axis=-1)
    o_c, conv_new = conformer_conv(ca, cg, conv_buf, p['conv_w'], p['conv_b'], p['conv_ln_g'],
                                   p['conv_ln_b'], p['conv_pw'])
    y = jnp.concatenate([o_r, o_a.astype(x.dtype), o_c.astype(x.dtype)], axis=-1) @ p['w_out']
    x = x + y
    x = x + 0.5 * swiglu_ffn(rms_norm(x, p['ffn2_norm']), p['ffn2_wg'], p['ffn2_wu'], p['ffn2_wd'])
    return x, ret_new, k_new, v_new, conv_new


def setup_inputs(seed: int = 0) -> dict:
    key = jax.random.key(seed)
    ks = iter(jax.random.split(key, 32))
    f32 = jnp.float32

    def nrm(shape, scale):
        return jax.random.normal(next(ks), shape, f32) * scale

    def gain(shape):
        return 1.0 + nrm(shape, 0.1)

    swa_buf = min(WINDOW, PAST_LEN)
    return {
        'x_prompt': nrm((BATCH, SEQ, D_MODEL), 1.0),
        'x_sample': nrm((DEC_BATCH, DEC_SEQ, D_MODEL), 1.0),
        'state_ret': nrm((DEPTH, DEC_BATCH, RET_HEADS, RET_DK, RET_DV), 0.5),
        'cache_k_win': nrm((DEPTH, DEC_BATCH, swa_buf, KV_HEADS, HEAD_DIM), 1.0),
        'cache_v_win': nrm((DEPTH, DEC_BATCH, swa_buf, KV_HEADS, HEAD_DIM), 1.0),
        'state_conv': nrm((DEPTH, DEC_BATCH, CONV_K - 1, CONV_CH), 0.5),
        'ffn1_norm': gain((DEPTH, D_MODEL)),
        'ffn1_wg': nrm((DEPTH, D_MODEL, FFN_DIM), D_MODEL ** -0.5),
        'ffn1_wu': nrm((DEPTH, D_MODEL, FFN_DIM), D_MODEL ** -0.5),
        'ffn1_wd': nrm((DEPTH, FFN_DIM, D_MODEL), FFN_DIM ** -0.5),
        'mix_norm': gain((DEPTH, D_MODEL)),
        'w_in': nrm((DEPTH, D_MODEL, IN_DIM), D_MODEL ** -0.5),
        'ret_norm_g': gain((DEPTH, RET_HEADS, RET_DV)),
        'q_norm_g': gain((DEPTH, HEAD_DIM)),
        'k_norm_g': gain((DEPTH, HEAD_DIM)),
        'sinks': nrm((DEPTH, ATT_HEADS), 0.5),
        'conv_w': nrm((DEPTH, CONV_K, CONV_CH), CONV_K ** -0.5),
        'conv_b': nrm((DEPTH, CONV_CH), 0.02),
        'conv_ln_g': gain((DEPTH, CONV_CH)),
        'conv_ln_b': nrm((DEPTH, CONV_CH), 0.02),
        'conv_pw': nrm((DEPTH, CONV_CH, CONV_CH), CONV_CH ** -0.5),
        'w_out': nrm((DEPTH, MIX_DIM, D_MODEL), MIX_DIM ** -0.5),
        'ffn2_norm': gain((DEPTH, D_MODEL)),
        'ffn2_wg': nrm((DEPTH, D_MODEL, FFN_DIM), D_MODEL ** -0.5),
        'ffn2_wu': nrm((DEPTH, D_MODEL, FFN_DIM), D_MODEL ** -0.5),
        'ffn2_wd': nrm((DEPTH, FFN_DIM, D_MODEL), FFN_DIM ** -0.5),
    }


def reference(x_prompt, x_sample, state_ret, cache_k_win, cache_v_win, state_conv,
              ffn1_norm, ffn1_wg, ffn1_wu, ffn1_wd, mix_norm, w_in, ret_norm_g, q_norm_g, k_norm_g,
              sinks, conv_w, conv_b, conv_ln_g, conv_ln_b, conv_pw, w_out,
              ffn2_norm, ffn2_wg, ffn2_wu, ffn2_wd):
    B = x_prompt.shape[0]
    dt = x_prompt.dtype
    hp, hs = x_prompt, x_sample
    ret_p, ret_s, kp, ks_, vp, vs, cp, cs = [], [], [], [], [], [], [], []
    for l in range(DEPTH):
        p = dict(ffn1_norm=ffn1_norm[l], ffn1_wg=ffn1_wg[l], ffn1_wu=ffn1_wu[l], ffn1_wd=ffn1_wd[l],
                 mix_norm=mix_norm[l], w_in=w_in[l], ret_norm_g=ret_norm_g[l], q_norm_g=q_norm_g[l],
                 k_norm_g=k_norm_g[l], sinks=sinks[l], conv_w=conv_w[l], conv_b=conv_b[l],
                 conv_ln_g=conv_ln_g[l], conv_ln_b=conv_ln_b[l], conv_pw=conv_pw[l], w_out=w_out[l],
                 ffn2_norm=ffn2_norm[l], ffn2_wg=ffn2_wg[l], ffn2_wu=ffn2_wu[l], ffn2_wd=ffn2_wd[l])
        hp, r1, k1, v1, c1 = layer(
            hp,
            jnp.zeros((B, RET_HEADS, RET_DK, RET_DV), jnp.float32),
            jnp.zeros((B, WINDOW, KV_HEADS, HEAD_DIM), dt),
            jnp.zeros((B, WINDOW, KV_HEADS, HEAD_DIM), dt),
            jnp.zeros((B, CONV_K - 1, CONV_CH), dt),
            0, p)
        hs, r2, k2, v2, c2 = layer(hs, state_ret[l], cache_k_win[l], cache_v_win[l], state_conv[l],
                                   PAST_LEN, p)
        ret_p.append(r1); ret_s.append(r2)
        kp.append(k1); ks_.append(k2)
        vp.append(v1); vs.append(v2)
        cp.append(c1); cs.append(c2)
    return (hp, hs, jnp.stack(ret_p), jnp.stack(ret_s), jnp.stack(kp), jnp.stack(ks_),
            jnp.stack(vp), jnp.stack(vs), jnp.stack(cp), jnp.stack(cs))
```

```python
import numpy as np
from contextlib import ExitStack
import concourse.bass as bass
import concourse.mybir as mybir
from concourse.bass_utils import run_bass_kernel_spmd

F32 = mybir.dt.float32
BF16 = mybir.dt.bfloat16
ALU = mybir.AluOpType
AF = mybir.ActivationFunctionType
AX = mybir.AxisListType

D = 1024
FF = 2816
DEPTH = 2
SEQ = 4096
BATCH = 4
DECB = 128
IN_DIM = 2560
EPS = 1e-6
NCORES = 8
ACTIVE = 4
NSEG = 4
TP = 1024
NS = DECB // ACTIVE
NT = TP + NS
SEM_LIMIT = 30000
ACT_CORES = [0, 2, 4, 6]
DBG = {}


class Sched:
    ENGS = ("pe", "act", "dve", "pool", "sp")

    def __init__(self, nc, es):
        self.nc = nc
        self.es = es
        self.ops = []
        self.last_write = {}
        self.readers = {}
        self.out_dma_ops = []
        self.bar_deps = set()
        self.bar_pending = set()

    def op(self, eng, fn, reads=(), writes=(), dma=False, semkey=None, final=False, nobar=False):
        idx = len(self.ops)
        raw = set()
        other = set()
        for k in reads:
            w = self.last_write.get(k)
            if w is not None:
                raw.add(w)
        for k in writes:
            w = self.last_write.get(k)
            if w is not None:
                other.add(w)
            for r in self.readers.get(k, {}).values():
                other.add(r)
        for k in writes:
            self.last_write[k] = idx
            self.readers[k] = {}
        rk = ("dma", semkey) if dma else eng
        for k in reads:
            self.readers.setdefault(k, {})[rk] = idx
        if eng in self.bar_pending and not nobar and eng != "pe":
            if any(not self.persistent(k) for k in list(reads) + list(writes)):
                other |= self.bar_deps
                self.bar_pending.discard(eng)
        deps = set()
        for d in raw | other:
            if d == idx:
                continue
            p = self.ops[d]
            if (not p["dma"]) and p["eng"] == eng:
                if eng == "pe" or d not in raw:
                    continue
            deps.add(d)
        self.ops.append(dict(eng=eng, fn=fn, deps=deps, dma=dma, semkey=semkey, sig=False, ev=None))
        if final:
            self.out_dma_ops.append(idx)
        return idx

    def barrier(self):
        last = {}
        for i, o in enumerate(self.ops):
            if o["fn"] is None:
                continue
            if o["dma"]:
                last[("dma", o["semkey"])] = i
            else:
                last[o["eng"]] = i
        self.bar_deps = set(last.values())
        self.bar_pending = set(self.ENGS)
        self.last_write = {k: v for k, v in self.last_write.items() if self.persistent(k)}
        self.readers = {k: v for k, v in self.readers.items() if self.persistent(k)}

    PERSIST = frozenset(("x", "h", "o", "ps", "WG", "WD", "stage", "Sst", "kTh", "Vh", "uh", "ident", "identb", "onesv",
                         "epsb", "rdm", "qdec", "kdec", "g128", "g1", "amask", "smask", "eyeN", "hsel", "n1g", "mng",
                         "n2g", "retg", "qg8", "kgs", "esink", "convw", "convb", "lng", "lnb", "oT"))

    def persistent(self, k):
        return (k[0] if isinstance(k, tuple) else k) in self.PERSIST

    def emit(self):
        nc = self.nc
        ops = self.ops
        if self.out_dma_ops:
            ops.append(dict(eng="sp", fn=None, deps=set(self.out_dma_ops), dma=False, semkey=None,
                            sig=False, ev=None))
        for o in ops:
            for d in o["deps"]:
                ops[d]["sig"] = True
        eng_sems = {e: [] for e in self.ENGS}
        eng_cnt = {e: 0 for e in self.ENGS}
        dma_sems = {}
        dma_cnt = {}
        nsem = [0]

        def new_sem():
            nsem[0] += 1
            return self.es.enter_context(nc.semaphore("s%d" % nsem[0]))

        for o in ops:
            if o["fn"] is None:
                continue
            if o["dma"]:
                k = o["semkey"]
                if k not in dma_sems or dma_cnt[k] + 16 >= SEM_LIMIT:
                    dma_sems[k] = new_sem()
                    dma_cnt[k] = 0
                dma_cnt[k] += 16
                o["ev"] = (dma_sems[k], dma_cnt[k])
            elif o["sig"]:
                e = o["eng"]
                if not eng_sems[e] or eng_cnt[e] >= SEM_LIMIT:
                    eng_sems[e].append(new_sem())
                    eng_cnt[e] = 0
                eng_cnt[e] += 1
                o["ev"] = (eng_sems[e][-1], eng_cnt[e])
        nwaits = [0]

        def run_engine(e, eng):
            seen = {}
            for o in ops:
                if o["eng"] != e:
                    continue
                need = {}
                for d in o["deps"]:
                    sem, val = ops[d]["ev"]
                    key = id(sem)
                    if seen.get(key, 0) >= val:
                        continue
                    if key not in need or need[key][1] < val:
                        need[key] = (sem, val)
                for key, (sem, val) in need.items():
                    eng.wait_ge(sem, val)
                    seen[key] = val
                    nwaits[0] += 1
                if o["fn"] is None:
                    continue
                ins = o["fn"](eng)
                if o["ev"] is not None:
                    ins.then_inc(o["ev"][0], 16 if o["dma"] else 1)

        with nc.Block() as block:
            @block.tensor
            def _(eng):
                run_engine("pe", eng)

            @block.scalar
            def _(eng):
                run_engine("act", eng)

            @block.vector
            def _(eng):
                run_engine("dve", eng)

            @block.gpsimd
            def _(eng):
                run_engine("pool", eng)

            @block.sync
            def _(eng):
                run_engine("sp", eng)
        self.stats = dict(n_ops=len(ops), n_waits=nwaits[0], n_sems=nsem[0])


def const_tables():
    lg = np.log1p(-np.exp2(-5.0 - np.arange(4, dtype=np.float64)))
    slopes = np.exp2(-8.0 * (np.arange(4, dtype=np.float64) + 1.0) / 4.0)
    j = np.arange(128)[:, None].astype(np.float64)
    i = np.arange(128)[None, :].astype(np.float64)
    t = {}
    rdm = np.zeros((128, 4, 128), np.float64)
    for h in range(4):
        rdm[:, h, :] = np.where(i >= j, np.exp(lg[h] * np.maximum(i - j, 0.0)), 0.0)
    t["c_rdm"] = rdm
    qdec = np.zeros((128, 2, 128), np.float64)
    kdec = np.zeros((128, 4), np.float64)
    g128 = np.zeros((128, 2), np.float64)
    g1 = np.zeros((128, 2), np.float64)
    for pr in range(2):
        for hh in range(2):
            h = 2 * pr + hh
            qdec[64 * hh:64 * hh + 64, pr, :] = np.exp(lg[h] * (np.arange(128) + 1.0))[None, :]
            g128[64 * hh:64 * hh + 64, pr] = np.exp(lg[h] * 128.0)
            g1[64 * hh:64 * hh + 64, pr] = np.exp(lg[h])
    for h in range(4):
        kdec[:, h] = np.exp(lg[h] * (127.0 - np.arange(128))) * 0.125
    t["c_qdec"] = qdec
    t["c_kdec"] = kdec
    t["c_g128"] = g128
    t["c_g1"] = g1
    am = np.zeros((128, 2, 2, 2, 128), np.float64)
    for g in range(2):
        for hh in range(2):
            h = 2 * g + hh
            dist_prev = 128.0 + i - j
            am[:, g, hh, 0, :] = np.where(j > i, np.exp(-slopes[h] * dist_prev), 0.0)
            am[:, g, hh, 1, :] = np.where(i >= j, np.exp(-slopes[h] * (i - j)), 0.0)
    t["c_amask"] = am
    sm = np.zeros((128, 4), np.float64)
    p = np.arange(128, dtype=np.float64)
    for h in range(4):
        sm[:, h] = np.where(p >= 1, np.exp(-slopes[h] * (128.0 - p)), 0.0)
    t["c_smask"] = sm
    t["c_eye"] = np.eye(NS)
    hs = np.zeros((128, 2, 128), np.float64)
    hs[0:64, 0, :] = 1.0
    hs[64:128, 1, :] = 1.0
    t["c_hsel"] = hs
    return {k: np.ascontiguousarray(v, dtype=np.float32) for k, v in t.items()}, lg


def build():
    ctab, lg = const_tables()
    gam = [float(np.exp(x)) for x in lg]
    nc = bass.Bass("TRN2", target_bir_lowering=False)

    def din(name, shape):
        return nc.dram_tensor(name, list(shape), F32, kind="ExternalInput").ap()

    def dout(name, shape):
        return nc.dram_tensor(name, list(shape), F32, kind="ExternalOutput").ap()

    xp = din("xp", (NSEG * TP, D))
    xs = din("xs", (NS, D))
    sret = din("sret", (DEPTH, NS, 4, 64, 128))
    ck = din("ck", (DEPTH, NS, 128, 128))
    cv = din("cv", (DEPTH, NS, 128, 128))
    sconv = din("sconv", (DEPTH, NS, 30, 256))
    W = {}
    for nm, shp in (("ffn1_norm", (DEPTH, D)), ("ffn1_wg", (DEPTH, D, FF)), ("ffn1_wu", (DEPTH, D, FF)),
                    ("ffn1_wd", (DEPTH, FF, D)), ("mix_norm", (DEPTH, D)), ("w_in", (DEPTH, D, IN_DIM)),
                    ("ret_norm_g", (DEPTH, 4, 128)), ("q_norm_g", (DEPTH, 64)), ("k_norm_g", (DEPTH, 64)),
                    ("sinks", (DEPTH, 4)), ("conv_w", (DEPTH, 31, 256)), ("conv_b", (DEPTH, 256)),
                    ("conv_ln_g", (DEPTH, 256)), ("conv_ln_b", (DEPTH, 256)), ("conv_pw", (DEPTH, 256, 256)),
                    ("w_out", (DEPTH, D, D)), ("ffn2_norm", (DEPTH, D)), ("ffn2_wg", (DEPTH, D, FF)),
                    ("ffn2_wu", (DEPTH, D, FF)), ("ffn2_wd", (DEPTH, FF, D))):
        W[nm] = din(nm, shp)
    C = {k: din(k, v.shape) for k, v in ctab.items()}
    yp = dout("yp", (NSEG * TP, D))
    ys = dout("ys", (NS, D))
    retp = dout("retp", (DEPTH, 4, 64, 128))
    rets = dout("rets", (DEPTH, NS, 4, 64, 128))
    kwp = dout("kwp", (DEPTH, 128, 128))
    kws = dout("kws", (DEPTH, NS, 128, 128))
    vwp = dout("vwp", (DEPTH, 128, 128))
    vws = dout("vws", (DEPTH, NS, 128, 128))
    cvp = dout("cvp", (DEPTH, 30, 256))
    cvs = dout("cvs", (DEPTH, NS, 30, 256))

    es = ExitStack()
    with es:
        def sb(name, shape, dt):
            return es.enter_context(nc.sbuf_tensor(name, list(shape), dt))

        S = Sched(nc, es)
        ps = [es.enter_context(nc.psum_tensor("ps%d" % i, [128, 512], F32)) for i in range(8)]
        pctr = [0]

        def pb():
            i = pctr[0] % 8
            pctr[0] += 1
            return ps[i], ("ps", i)

        def MM(out, lhsT, rhs, st, sp, r, w):
            S.op("pe", lambda e: e.matmul(out, lhsT=lhsT, rhs=rhs, start=st, stop=sp), r, w)

        def TR(out, in_, idn, r, w):
            S.op("pe", lambda e: e.transpose(out, in_, idn), r, w)

        def ACT(out, in_, func, r, w, bias=None, scale=1.0):
            if bias is None:
                S.op("act", lambda e: e.activation(out=out, in_=in_, func=func, scale=scale), r, w)
            else:
                S.op("act", lambda e: e.activation(out=out, in_=in_, func=func, bias=bias, scale=scale), r, w)

        def STT(out, in0, scalar, in1, op0, op1, r, w, eng="dve"):
            S.op(eng, lambda e: e.scalar_tensor_tensor(out=out, in0=in0, scalar=scalar, in1=in1, op0=op0, op1=op1), r, w)

        def TT(out, in0, in1, op, r, w, eng="dve"):
            S.op(eng, lambda e: e.tensor_tensor(out=out, in0=in0, in1=in1, op=op), r, w)

        def TS(out, in0, s1, op0, r, w, s2=None, op1=None, eng="dve"):
            if op1 is None:
                S.op(eng, lambda e: e.tensor_scalar(out=out, in0=in0, scalar1=s1, scalar2=None, op0=op0), r, w)
            else:
                S.op(eng, lambda e: e.tensor_scalar(out=out, in0=in0, scalar1=s1, scalar2=s2, op0=op0, op1=op1), r, w)

        def CP(out, in_, r, w, eng="dve"):
            if DBG.get("noactcp") and eng == "act":
                eng = "dve"
            if eng == "act":
                S.op("act", lambda e: e.activation(out=out, in_=in_, func=AF.Identity), r, w)
            else:
                S.op(eng, lambda e: e.tensor_copy(out, in_), r, w)

        def RCP(out, in_, r, w):
            S.op("dve", lambda e: e.reciprocal(out, in_), r, w)

        def MSET(ap, val, w, eng="dve"):
            S.op(eng, lambda e: e.memset(ap, val), (), w)

        def DMA(q, out, in_, r, w, key, final=False, slow=False):
            if slow:
                S.op(q, lambda e: e.dma_start(out=out, in_=in_, allow_slow_non_contiguous=True), r, w, dma=True,
                     semkey=key, final=final)
            else:
                S.op(q, lambda e: e.dma_start(out=out, in_=in_), r, w, dma=True, semkey=key, final=final)

        xT = sb("xT", [128, 8, NT], F32)
        hT = sb("hT", [128, 8, NT], BF16)
        oT = sb("oT", [128, 10, NT], BF16)
        ident = sb("ident", [128, 128], F32)
        identb = sb("identb", [128, 128], BF16)
        onesv = sb("onesv", [128, 5, 128], BF16)
        epsb = sb("epsb", [128, 1], F32)
        stage = sb("stage", [128, 2, D], F32)
        rdm = sb("rdm", [128, 4, 128], F32)
        qdec = sb("qdec", [128, 2, 128], F32)
        kdec = sb("kdec", [128, 4], F32)
        g128 = sb("g128", [128, 2], F32)
        g1 = sb("g1", [128, 2], F32)
        amask = sb("amask", [128, 2, 512], BF16)
        smask = sb("smask", [128, 4], F32)
        eyeN = sb("eyeN", [NS, NS], BF16)
        hsel = sb("hsel", [128, 2, 128], BF16)
        n1g = sb("n1g", [128, DEPTH, 8], F32)
        mng = sb("mng", [128, DEPTH, 8], F32)
        n2g = sb("n2g", [128, DEPTH, 8], F32)
        retg = sb("retg", [128, DEPTH, 4], F32)
        qg8 = sb("qg8", [64, DEPTH], F32)
        kgs = sb("kgs", [64, DEPTH], F32)
        esink = sb("esink", [64, DEPTH * 4], F32)
        convw = sb("convw", [128, DEPTH, 2, 31], F32)
        convb = sb("convb", [128, DEPTH, 2], F32)
        lng = sb("lng", [128, DEPTH, 2], F32)
        lnb = sb("lnb", [128, DEPTH, 2], F32)
        Sst = sb("Sst", [128, DEPTH, 2, 128], F32)
        kTh = sb("kTh", [64, DEPTH, 2, 128], BF16)
        Vh = sb("Vh", [128, DEPTH, 128], BF16)
        uh = sb("uh", [128, DEPTH, 2, 30], BF16)
        NA = 16640
        arena = sb("arena", [128, NA], F32)
        aoff = [0]
        WG = [sb("WG%d" % i, [128, 8192], BF16) for i in range(2)]
        WD = [sb("WD%d" % i, [128, 4096], BF16) for i in range(2)]
        wgc = [0]
        wdc = [0]

        def wg_slot():
            i = wgc[0] % 2
            wgc[0] += 1
            return WG[i], ("WG", i)

        def wd_slot():
            i = wdc[0] % 2
            wdc[0] += 1
            return WD[i], ("WD", i)

        def wview(Wt, off, a, b, parts=128):
            return Wt[0:parts, off:off + a * b].rearrange("p (a b) -> p a b", a=a, b=b)

        def WDMA(out, in_, key):
            S.op("pool", lambda e: e.dma_start(out=out, in_=in_), (), [key], dma=True, semkey=key, nobar=True)

        def areset():
            aoff[0] = 0

        def aalloc(shape, dt, parts=128):
            n = int(np.prod(shape))
            words = (n + 1) // 2 if dt == BF16 else n
            o = aoff[0]
            aoff[0] += words
            assert aoff[0] <= NA, ("arena overflow", aoff[0])
            a = arena[0:parts, o:o + words]
            if dt == BF16:
                a = a.bitcast(BF16)
                if n % 2:
                    a = a[:, 0:n]
            if len(shape) == 2:
                return a.rearrange("p (a b) -> p a b", a=shape[0], b=shape[1])
            if len(shape) == 3:
                return a.rearrange("p (a b c) -> p a b c", a=shape[0], b=shape[1], c=shape[2])
            return a

        MSET(ident[:], 0.0, ["ident"], eng="pool")
        S.op("pool", lambda e: e.affine_select(out=ident[:], in_=ident[:], pattern=[[-1, 128]],
                                               compare_op=ALU.not_equal, fill=1.0, base=0, channel_multiplier=1),
             ["ident"], ["ident"])
        CP(identb[:], ident[:], ["ident"], ["identb"])
        for i, v in enumerate((1.0 / 1024, 1.0 / 64, 1.0 / 128, 1.0 / 256, 1.0)):
            MSET(onesv[:, i, :], v, ["onesv"])
        MSET(epsb[:], EPS, ["epsb"])
        MSET(Sst[:], 0.0, ["Sst"])
        MSET(kTh[:], 0.0, ["kTh"])
        MSET(Vh[:], 0.0, ["Vh"])
        MSET(uh[:], 0.0, ["uh"])
        if DBG.get("zero_o"):
            MSET(oT[:], 0.0, ["oT"])
        DMA("sp", rdm[:], C["c_rdm"], (), ["rdm"], "c_rdm")
        DMA("sp", qdec[:], C["c_qdec"], (), ["qdec"], "c_qdec")
        DMA("sp", kdec[:], C["c_kdec"], (), ["kdec"], "c_kdec")
        DMA("sp", g128[:], C["c_g128"], (), ["g128"], "c_g128")
        DMA("sp", g1[:], C["c_g1"], (), ["g1"], "c_g1")
        DMA("sp", smask[:], C["c_smask"], (), ["smask"], "c_smask")
        if not DBG.get("nopoolc"):
          DMA("pool", amask[:], C["c_amask"].rearrange("p a b c d -> p a (b c d)"), (), ["amask"], "c_amask")
          DMA("pool", eyeN[:], C["c_eye"], (), ["eyeN"], "c_eyeN")
          DMA("pool", hsel[:], C["c_hsel"], (), ["hsel"], "c_hsel")
        for l in range(DEPTH if not DBG.get("noparams") else 0):
            DMA("sp", n1g[:, l, :], W["ffn1_norm"][l].rearrange("(c p) -> p c", p=128), (), ["n1g"], "c_n1g", slow=True)
            DMA("sp", mng[:, l, :], W["mix_norm"][l].rearrange("(c p) -> p c", p=128), (), ["mng"], "c_mng", slow=True)
            DMA("sp", n2g[:, l, :], W["ffn2_norm"][l].rearrange("(c p) -> p c", p=128), (), ["n2g"], "c_n2g", slow=True)
            DMA("sp", retg[:, l, :], W["ret_norm_g"][l].rearrange("h p -> p h"), (), ["retg"], "c_retg", slow=True)
            DMA("sp", qg8[:, l:l + 1], W["q_norm_g"][l].rearrange("(p o) -> p o", o=1), (), ["qg8"], "c_qg8", slow=True)
            DMA("sp", kgs[:, l:l + 1], W["k_norm_g"][l].rearrange("(p o) -> p o", o=1), (), ["kgs"], "c_kgs", slow=True)
            DMA("sp", esink[:, l * 4:(l + 1) * 4], W["sinks"][l:l + 1, :].partition_broadcast(64), (), ["esink"],
                "c_esink", slow=True)
            for c in range(2):
                DMA("sp", convw[:, l, c, :], W["conv_w"][l][:, c * 128:(c + 1) * 128].rearrange("j p -> p j"), (),
                    ["convw"], "c_convw", slow=True)
            DMA("sp", convb[:, l, :], W["conv_b"][l].rearrange("(c p) -> p c", p=128), (), ["convb"], "c_convb", slow=True)
            DMA("sp", lng[:, l, :], W["conv_ln_g"][l].rearrange("(c p) -> p c", p=128), (), ["lng"], "c_lng", slow=True)
            DMA("sp", lnb[:, l, :], W["conv_ln_b"][l].rearrange("(c p) -> p c", p=128), (), ["lnb"], "c_lnb", slow=True)
        S.barrier()
        TS(qg8[:], qg8[:], 0.125, ALU.mult, ["qg8"], ["qg8"])
        ACT(esink[:], esink[:], AF.Exp, ["esink"], ["esink"])
        S.barrier()

        def tiles_of(si):
            t = [(0, 512), (512, 512)]
            if si == 0:
                t.append((TP, NS))
            return t

        def load_x(si):
            chunks = [(xp[si * TP + n * 128: si * TP + (n + 1) * 128, :], n * 128, 128) for n in range(TP // 128)]
            if si == 0 and not DBG.get("nosample"):
                chunks.append((xs[:, :], TP, NS))
            if DBG.get("noload"):
                chunks = []
            for ci, (src, t0, n) in enumerate(chunks):
                sl = ci % 2
                DMA("sp", stage[0:n, sl, :], src, (), [("stage", sl)], ("stage", sl))
                for half in range(2):
                    p, pk = pb()
                    for q in range(4):
                        c = half * 4 + q
                        TR(p[:, q * 128:q * 128 + n], stage[0:n, sl, c * 128:(c + 1) * 128], ident[0:n, 0:n],
                           [("stage", sl), "ident"], [pk])
                    for q in range(4):
                        c = half * 4 + q
                        CP(xT[:, c, t0:t0 + n], p[:, q * 128:q * 128 + n], [pk],
                           [("x", c, (t0 // 512) * 512 if t0 < TP else TP)], eng="act" if half else "dve")

        def store_y(si):
            chunks = [(yp[si * TP + n * 128: si * TP + (n + 1) * 128, :], n * 128, 128) for n in range(TP // 128)]
            if si == 0 and not DBG.get("nosample"):
                chunks.append((ys[:, :], TP, NS))
            if DBG.get("nostore"):
                chunks = chunks[:1]
            for ci, (dst, t0, n) in enumerate(chunks):
                sl = ci % 2
                tk = (t0 // 512) * 512 if t0 < TP else TP
                for half in range(2):
                    p, pk = pb()
                    for q in range(4):
                        c = half * 4 + q
                        TR(p[0:n, q * 128:(q + 1) * 128], xT[:, c, t0:t0 + n], ident[:, :],
                           [("x", c, tk), "ident"], [pk])
                    CP(stage[0:n, sl, half * 512:(half + 1) * 512], p[0:n, :], [pk], [("stage", sl)],
                       eng="act" if half else "dve")
                DMA("sp", dst, stage[0:n, sl, :], [("stage", sl)], (), ("stage", sl), final=True)

        def xkey(c, t0):
            return ("x", c, t0)

        def xkeys(c, t0, n):
            return [("x", c, t0)]

        def rstd_from(psmean, pk, n, parts, rbuf, rk):
            ACT(rbuf[0:parts, 0:n], psmean, AF.Ln, [pk, "epsb"], [rk], bias=epsb[0:parts, 0:1])
            ACT(rbuf[0:parts, 0:n], rbuf[0:parts, 0:n], AF.Exp, [rk], [rk], scale=-0.5)

        def rmsnorm(si, gain, l):
            assert aoff[0] == 0
            sq = aalloc([8, 512], BF16)
            rb = aalloc([2, 512], F32)
            for ti, (t0, n) in enumerate(tiles_of(si)):
                p, pk = pb()
                for c in range(8):
                    ACT(sq[:, c, 0:n], xT[:, c, t0:t0 + n], AF.Square, [("x", c, t0)], [("sq", c)])
                for c in range(8):
                    MM(p[:, 0:n], onesv[:, 0, :], sq[:, c, 0:n], c == 0, c == 7, [("sq", c), "onesv"], [pk])
                rk = ("rstd", ti % 2)
                rstd_from(p[:, 0:n], pk, n, 128, rb[:, ti % 2, :], rk)
                for c in range(8):
                    STT(hT[:, c, t0:t0 + n], xT[:, c, t0:t0 + n], gain[:, l, c:c + 1], rb[:, ti % 2, 0:n],
                        ALU.mult, ALU.mult, [("x", c, t0), rk], [("h", t0)])

        def fix_xkeys(si):
            pass

        def ffn(si, l, which):
            areset()
            wg_d = W["ffn%d_wg" % which][l]
            wu_d = W["ffn%d_wu" % which][l]
            wd_d = W["ffn%d_wd" % which][l]
            gain = n1g if which == 1 else n2g
            rmsnorm(si, gain, l)
            act = [aalloc([4, NT], BF16) for _ in range(2)]
            sg = aalloc([2, 512], F32)
            groups = [(g * 512, 4) for g in range(5)] + [(2560, 2)]
            tl = tiles_of(si)
            gslots = {}

            def gateup(j):
                f0, nfc = groups[j]
                sl = j % 2
                Wg, gk = wg_slot()
                Wd, dk = wd_slot()
                wgv = wview(Wg, 0, 8, 512)
                wuv = wview(Wg, 4096, 8, 512)
                wdv = wview(Wd, 0, 4, D)
                gslots[j] = (wdv, dk)
                WDMA(wgv[:, :, 0:nfc * 128], wg_d[:, f0:f0 + nfc * 128].rearrange("(k p) f -> p k f", p=128), gk)
                WDMA(wuv[:, :, 0:nfc * 128], wu_d[:, f0:f0 + nfc * 128].rearrange("(k p) f -> p k f", p=128), gk)
                WDMA(wdv[:, 0:nfc, :], wd_d[f0:f0 + nfc * 128, :].rearrange("(k p) d -> p k d", p=128), dk)
                cnt = 0
                for (t0, n) in tl:
                    for fc in range(nfc):
                        pg, pgk = pb()
                        pu, puk = pb()
                        for k in range(8):
                            MM(pg[:, 0:n], wgv[:, k, fc * 128:(fc + 1) * 128], hT[:, k, t0:t0 + n], k == 0, k == 7,
                               [gk, ("h", t0)], [pgk])
                        for k in range(8):
                            MM(pu[:, 0:n], wuv[:, k, fc * 128:(fc + 1) * 128], hT[:, k, t0:t0 + n], k == 0, k == 7,
                               [gk, ("h", t0)], [puk])
                        s2 = cnt % 2
                        cnt += 1
                        ACT(sg[:, s2, 0:n], pg[:, 0:n], AF.Silu, [pgk], [("sg", s2)])
                        TT(act[sl][:, fc, t0:t0 + n], sg[:, s2, 0:n], pu[:, 0:n], ALU.mult, [("sg", s2), puk],
                           [("act", sl, t0)])

            def down(j):
                f0, nfc = groups[j]
                sl = j % 2
                wdv, dk = gslots[j]
                for (t0, n) in tl:
                    for c in range(8):
                        p, pk = pb()
                        for fc in range(nfc):
                            MM(p[:, 0:n], wdv[:, fc, c * 128:(c + 1) * 128], act[sl][:, fc, t0:t0 + n], fc == 0,
                               fc == nfc - 1, [dk, ("act", sl, t0)], [pk])
                        STT(xT[:, c, t0:t0 + n], p[:, 0:n], 0.5, xT[:, c, t0:t0 + n], ALU.mult, ALU.add,
                            [pk, ("x", c, t0)], [("x", c, t0)])

            for j in range(len(groups) + 1):
                if j < len(groups):
                    gateup(j)
                if j >= 1:
                    down(j - 1)

        def ret_epilogue(Oin, Okey, n, l, h, hh, sgT, t0, scr, ebank=None):
            Osb, sqb, rb = scr
            CP(Osb[:, 0:n], Oin, [Okey], ["Osb"], eng="act")
            ACT(sqb[:, 0:n], Oin, AF.Square, [Okey], ["sqb"])
            p, pk = pb() if ebank is None else (ps[ebank], ("ps", ebank))
            MM(p[:, 0:n], onesv[:, 2, :], sqb[:, 0:n], True, True, ["sqb", "onesv"], [pk])
            rstd_from(p[:, 0:n], pk, n, 128, rb, "rb")
            STT(Osb[:, 0:n], Osb[:, 0:n], retg[:, l, h:h + 1], rb[:, 0:n], ALU.mult, ALU.mult, ["Osb", "rb"], ["Osb"])
            TT(oT[:, h, t0:t0 + n], Osb[:, 0:n], sgT[:, hh, t0:t0 + n], ALU.mult, ["Osb", ("sgT", t0)], [("o", h, t0)])

        def retention(si, l, pr, last):
            areset()
            win = W["w_in"][l]
            Wg, gk = wg_slot()
            wqk = wview(Wg, 0, 8, 256)
            wv = wview(Wg, 2048, 8, 256)
            wgt = wview(Wg, 4096, 8, 256)
            Kp = aalloc([8, 128], BF16)
            Vt = aalloc([8, 256], BF16)
            qT = aalloc([1, NT], BF16)[:, 0, :]
            kT = aalloc([1, NT], BF16)[:, 0, :]
            qdT = aalloc([1, TP], BF16)[:, 0, :]
            sgT = aalloc([2, NT], BF16)
            AT = [aalloc([4, 128], BF16) for _ in range(2)]
            Sbf = aalloc([1, 128], BF16)[:, 0, :]
            Osb = aalloc([1, 512], F32)[:, 0, :]
            sqb = aalloc([1, 512], BF16)[:, 0, :]
            rb = aalloc([1, 512], F32)[:, 0, :]
            scr = (Osb, sqb, rb)
            wv3 = win.rearrange("(k p) f -> p k f", p=128)
            WDMA(wqk[:, :, 0:128], wv3[:, :, pr * 128:(pr + 1) * 128], gk)
            WDMA(wqk[:, :, 128:256], wv3[:, :, 256 + pr * 128:256 + (pr + 1) * 128], gk)
            WDMA(wv[:], wv3[:, :, 512 + pr * 256:512 + (pr + 1) * 256], gk)
            WDMA(wgt[:], wv3[:, :, 1024 + pr * 256:1024 + (pr + 1) * 256], gk)
            Sf = Sst[:, l, pr, :]
            CP(Sbf, Sf, ["Sst"], ["Sbf"])
            for n in range(8):
                p, pk = pb()
                p2, pk2 = pb()
                for k in range(8):
                    MM(p[:, 0:128], hT[:, k, n * 128:(n + 1) * 128], wqk[:, k, 128:256], k == 0, k == 7,
                       [("h", (n // 4) * 512), gk], [pk])
                for k in range(8):
                    MM(p2[:, 0:256], hT[:, k, n * 128:(n + 1) * 128], wv[:, k, :], k == 0, k == 7,
                       [("h", (n // 4) * 512), gk], [pk2])
                for hh in range(2):
                    TS(Kp[:, n, hh * 64:(hh + 1) * 64], p[:, hh * 64:(hh + 1) * 64], kdec[:, 2 * pr + hh:2 * pr + hh + 1],
                       ALU.mult, [pk, "kdec"], [("Kp", n)])
                CP(Vt[:, n, :], p2[:, 0:256], [pk2], [("Vt", n)], eng="act")
            for (t0, n) in tiles_of(si):
                p, pk = pb()
                for k in range(8):
                    MM(p[:, 0:n], wqk[:, k, 0:128], hT[:, k, t0:t0 + n], k == 0, k == 7, [gk, ("h", t0)], [pk])
                CP(qT[:, t0:t0 + n], p[:, 0:n], [pk], [("qT", t0)], eng="dve")
                if t0 < TP:
                    TT(qdT[:, t0:t0 + n].rearrange("p (a b) -> p a b", a=4), p[:, 0:n].rearrange("p (a b) -> p a b", a=4),
                       qdec[:, pr:pr + 1, :].to_broadcast([128, 4, 128]), ALU.mult, [pk, "qdec"], [("qdT", t0)])
                p, pk = pb()
                for k in range(8):
                    MM(p[:, 0:n], wqk[:, k, 128:256], hT[:, k, t0:t0 + n], k == 0, k == 7, [gk, ("h", t0)], [pk])
                ACT(kT[:, t0:t0 + n], p[:, 0:n], AF.Identity, [pk], [("kT", t0)], scale=0.125)
                if t0 >= TP:
                    for hh in range(2):
                        p, pk = pb()
                        for k in range(8):
                            MM(p[:, 0:n], wgt[:, k, hh * 128:(hh + 1) * 128], hT[:, k, t0:t0 + n], k == 0, k == 7,
                               [gk, ("h", t0)], [pk])
                        ACT(sgT[:, hh, t0:t0 + n], p[:, 0:n], AF.Silu, [pk], [("sgT", t0)])
            Sall = aalloc([8, 128], BF16)
            AT2 = [[AT[0], AT[1]], [aalloc([4, 128], BF16), aalloc([4, 128], BF16)]]
            for n in range(8):
                b = n // 2
                off = (n % 2) * 256
                MM(ps[b][:, off:off + 128], Kp[:, n, :], Vt[:, n, 0:128], True, True, [("Kp", n), ("Vt", n)], [("ps", b)])
                MM(ps[b][:, off + 128:off + 256], Kp[:, n, :], Vt[:, n, 128:256], True, True, [("Kp", n), ("Vt", n)],
                   [("ps", b)])
            for ti in range(2):
                t0 = ti * 512
                for hh in range(2):
                    bank = 4 + 2 * ti + hh
                    lo, hi = 64 * hh, 64 * hh + 64
                    for nn in range(4):
                        c0 = t0 + nn * 128
                        MM(ps[bank][:, nn * 128:(nn + 1) * 128], kT[lo:hi, c0:c0 + 128], qT[lo:hi, c0:c0 + 128], True, True,
                           [("kT", t0), ("qT", t0)], [("ps", bank)])
                    TT(AT2[ti][hh][:], ps[bank][:, :].rearrange("p (a b) -> p a b", a=4),
                       rdm[:, 2 * pr + hh:2 * pr + hh + 1, :].to_broadcast([128, 4, 128]), ALU.mult, [("ps", bank), "rdm"],
                       [("AT", ti, hh)])
            for ti in range(2):
                t0 = ti * 512
                for hh in range(2):
                    bank = 4 + 2 * ti + hh
                    for k in range(8):
                        MM(ps[bank][:, :], wgt[:, k, hh * 128:(hh + 1) * 128], hT[:, k, t0:t0 + 512], k == 0, k == 7,
                           [gk, ("h", t0)], [("ps", bank)])
                    ACT(sgT[:, hh, t0:t0 + 512], ps[bank][:, :], AF.Silu, [("ps", bank)], [("sgT", t0)])
            for n in range(8):
                b = n // 2
                off = (n % 2) * 256
                CP(Sall[:, n, :], Sst[:, l, pr, :], ["Sst"], [("Sall", n)])
                for hh in range(2):
                    lo, hi = 64 * hh, 64 * hh + 64
                    STT(Sst[lo:hi, l, pr, :], Sst[lo:hi, l, pr, :], gam[2 * pr + hh] ** 128,
                        ps[b][lo:hi, off + hh * 128:off + (hh + 1) * 128], ALU.mult, ALU.add, [("ps", b), "Sst"], ["Sst"])
            for ti in range(2):
                t0 = ti * 512
                for hh in range(2):
                    bank = 2 * ti + hh
                    lo, hi = 64 * hh, 64 * hh + 64
                    for nn in range(4):
                        n = ti * 4 + nn
                        c0 = t0 + nn * 128
                        MM(ps[bank][:, nn * 128:(nn + 1) * 128], Vt[:, n, hh * 128:(hh + 1) * 128], AT2[ti][hh][:, nn, :],
                           True, False, [("Vt", n), ("AT", ti, hh)], [("ps", bank)])
                        MM(ps[bank][:, nn * 128:(nn + 1) * 128], Sall[lo:hi, n, :], qdT[lo:hi, c0:c0 + 128], False, True,
                           [("Sall", n), ("qdT", t0)], [("ps", bank)])
                for hh in range(2):
                    bank = 2 * ti + hh
                    ret_epilogue(ps[bank][:, :], ("ps", bank), 512, l, 2 * pr + hh, hh, sgT, t0, scr, ebank=4 + 2 * ti + hh)
            pctr[0] = 0
            if last:
                for hh in range(2):
                    DMA("sp", retp[l, 2 * pr + hh], Sst[64 * hh:64 * hh + 64, l, pr, :], ["Sst"], (), "retp", final=True)
            if si == 0:
                t0 = TP
                S0f = aalloc([NS, 128], F32)
                S0b = aalloc([NS, 128], BF16)
                Ks = aalloc([1, 128], BF16, parts=NS)[:, 0, :]
                Vs = aalloc([1, 256], BF16, parts=NS)[:, 0, :]
                Vbd1 = aalloc([NS, 128], BF16, parts=NS)
                Vbd = [Vbd1, Vbd1]
                vTs = aalloc([2, NS], F32)
                prod = aalloc([1, NS], BF16)[:, 0, :]
                tmp = aalloc([1, NS], F32)[:, 0, :]
                Os = aalloc([1, NS], F32)[:, 0, :]
                Sn = aalloc([2, 512], F32)
                DMA("sp", S0f[:], sret[l, :, 2 * pr:2 * pr + 2].rearrange("b h d v -> (h d) b v"), (), ["S0f"], "S0f")
                CP(S0b[:], S0f[:], ["S0f"], ["S0b"])
                p, pk = pb()
                for k in range(8):
                    MM(p[0:NS, 0:128], hT[:, k, t0:t0 + NS], wqk[:, k, 128:256], k == 0, k == 7, [("h", t0), gk], [pk])
                for k in range(8):
                    MM(p[0:NS, 128:384], hT[:, k, t0:t0 + NS], wv[:, k, :], k == 0, k == 7, [("h", t0), gk], [pk])
                ACT(Ks, p[0:NS, 0:128], AF.Identity, [pk], ["Ks"], scale=0.125)
                CP(Vs, p[0:NS, 128:384], [pk], ["Vs"], eng="act")
                for hh in range(2):
                    p, pk = pb()
                    for k in range(8):
                        MM(p[:, 0:NS], wv[:, k, hh * 128:(hh + 1) * 128], hT[:, k, t0:t0 + NS], k == 0, k == 7,
                           [gk, ("h", t0)], [pk])
                    CP(vTs[:, hh, :], p[:, 0:NS], [pk], [("vTs", hh)], eng="act")
                TT(prod, qT[:, t0:t0 + NS], kT[:, t0:t0 + NS], ALU.mult, [("qT", t0), ("kT", t0)], ["prod"])
                for hh in range(2):
                    lo, hi = 64 * hh, 64 * hh + 64
                    h = 2 * pr + hh
                    pt, ptk = pb()
                    for b in range(NS):
                        MM(pt[:, b:b + 1], S0b[lo:hi, b, :], qT[lo:hi, t0 + b:t0 + b + 1], True, True,
                           ["S0b", ("qT", t0)], [ptk])
                    pq, pqk = pb()
                    MM(pq[:, 0:NS], hsel[:, hh, :], prod, True, True, ["hsel", "prod"], [pqk])
                    TT(tmp, pq[:, 0:NS], vTs[:, hh, :], ALU.mult, [pqk, ("vTs", hh)], ["tmp"])
                    STT(Os, pt[:, 0:NS], gam[h], tmp, ALU.mult, ALU.add, [ptk, "tmp"], ["Os"])
                    ret_epilogue(Os, "Os", NS, l, h, hh, sgT, t0, scr)
                    TT(Vbd[hh][:], Vs[:, hh * 128:(hh + 1) * 128].unsqueeze(1).to_broadcast([NS, NS, 128]),
                       eyeN[:, :].unsqueeze(2).to_broadcast([NS, NS, 128]), ALU.mult, ["Vs", "eyeN"], ["Vbd"])
                    for q in range(NS // 4):
                        pn, pnk = pb()
                        MM(pn[:, :], Ks[:, :], Vbd[hh][:, 4 * q:4 * q + 4, :].rearrange("p a b -> p (a b)"), True, True, ["Ks", "Vbd"], [pnk])
                        s2 = q % 2
                        STT(Sn[lo:hi, s2, :], S0f[lo:hi, 4 * q:4 * q + 4, :].rearrange("p a b -> p (a b)"), gam[h], pn[lo:hi, :], ALU.mult, ALU.add,
                            [pnk, "S0f"], [("Sn", s2)])
                        DMA("sp", rets[l, 4 * q:4 * q + 4, h].rearrange("b d v -> d b v"),
                            Sn[lo:hi, s2, :].rearrange("p (b v) -> p b v", b=4), [("Sn", s2)], (), ("Sn", s2), final=True)
            S.barrier()

        def attention(si, l, last):
            areset()
            wv3 = W["w_in"][l].rearrange("(k p) f -> p k f", p=128)
            Wg, gk = wg_slot()
            wa = wview(Wg, 0, 8, 512)
            Va = aalloc([9, 128], BF16)
            qn = [aalloc([1, NT], BF16, parts=64)[:, 0, :] for _ in range(4)]
            kn = [aalloc([1, 128 + NT], BF16, parts=64)[:, 0, :] for _ in range(2)]
            qsb = aalloc([1, 512], F32, parts=64)[:, 0, :]
            sqb = aalloc([1, 512], BF16, parts=64)[:, 0, :]
            rb = aalloc([1, 512], F32, parts=64)[:, 0, :]
            knf = aalloc([2, 128], F32, parts=64)
            vlast = aalloc([1, 128], F32)[:, 0, :]
            E = [aalloc([1, 512], BF16)[:, 0, :] for _ in range(4)]
            PT = [aalloc([1, 512], BF16)[:, 0, :] for _ in range(4)]
            den = aalloc([1, 512], F32, parts=64)[:, 0, :]
            WDMA(wa[:], wv3[:, :, 1536:2048], gk)
            CP(Va[:, 0, :], Vh[:, l, :], ["Vh"], [("Va", 0)])
            for g in range(2):
                CP(kn[g][:, 0:128], kTh[:, l, g, :], ["kTh"], [("kn", g, -1)])
            for n in range(8):
                p, pk = pb()
                for k in range(8):
                    MM(p[:, 0:128], hT[:, k, n * 128:(n + 1) * 128], wa[:, k, 384:512], k == 0, k == 7,
                       [("h", (n // 4) * 512), gk], [pk])
                CP(Va[:, n + 1, :], p[:, 0:128], [pk], [("Va", n + 1)], eng="act")
                if n == 7:
                    if last:
                        CP(vlast, p[:, 0:128], [pk], ["vlast"], eng="act")
                        DMA("sp", vwp[l], vlast, ["vlast"], (), "vwp", final=True)
                    CP(Vh[:, l, :], p[:, 0:128], [pk], ["Vh"], eng="act")
            sqb2 = [sqb, aalloc([1, 512], BF16, parts=64)[:, 0, :]]
            rb2 = [rb, aalloc([1, 512], F32, parts=64)[:, 0, :]]
            items = []
            for (t0, n) in tiles_of(si):
                for h in range(4):
                    items.append(("q", h, t0, n))
                for g in range(2):
                    items.append(("k", g, t0, n))
            state = {}

            def b_proj(i):
                kind, idx, t0, n = items[i]
                wcol = idx * 64 if kind == "q" else 256 + idx * 64
                p, pk = pb()
                for k in range(8):
                    MM(p[0:64, 0:n], wa[:, k, wcol:wcol + 64], hT[:, k, t0:t0 + n], k == 0, k == 7, [gk, ("h", t0)], [pk])
                state[i] = (p, pk)

            def b_finish(i):
                kind, idx, t0, n = items[i]
                p, pk = state.pop(i)
                s2 = i % 2
                ACT(sqb2[s2][:, 0:n], p[0:64, 0:n], AF.Square, [pk], [("sqbA", s2)])
                p2, pk2 = pb()
                MM(p2[0:64, 0:n], onesv[0:64, 1, 0:64], sqb2[s2][:, 0:n], True, True, [("sqbA", s2), "onesv"], [pk2])
                rstd_from(p2[0:64, 0:n], pk2, n, 64, rb2[s2], ("rbA", s2))
                if kind == "q":
                    STT(qn[idx][:, t0:t0 + n], p[0:64, 0:n], qg8[:, l:l + 1], rb2[s2][:, 0:n], ALU.mult, ALU.mult,
                        [pk, ("rbA", s2)], [("qn", idx, t0)])
                else:
                    g = idx
                    STT(kn[g][:, 128 + t0:128 + t0 + n], p[0:64, 0:n], kgs[:, l:l + 1], rb2[s2][:, 0:n], ALU.mult, ALU.mult,
                        [pk, ("rbA", s2)], [("kn", g, t0)])
                    if t0 == 512:
                        STT(knf[:, g, :], p[0:64, 384:512], kgs[:, l:l + 1], rb2[s2][:, 384:512], ALU.mult, ALU.mult,
                            [pk, ("rbA", s2)], [("knf", g)])
                        if last:
                            DMA("sp", kwp[l, :, g * 64:(g + 1) * 64].rearrange("t d -> d t"), knf[:, g, :], [("knf", g)],
                                (), "kwp", final=True, slow=True)
                        CP(kTh[:, l, g, :], kn[g][:, 128 + 896:128 + 1024], [("kn", g, 512)], ["kTh"])

            b_proj(0)
            for i in range(len(items)):
                if i + 1 < len(items):
                    b_proj(i + 1)
                b_finish(i)
            lnd = qsb
            it = 0
            for g in range(2):
                for ti in range(2):
                    t0 = ti * 512
                    base = 4 * (it % 2)
                    oth = 4 - base
                    it += 1
                    pO = [(ps[base + 0], ("ps", base + 0)), (ps[base + 1], ("ps", base + 1))]
                    pD = [(ps[base + 2], ("ps", base + 2)), (ps[base + 3], ("ps", base + 3))]
                    for nn in range(4):
                        n = ti * 4 + nn
                        c0 = n * 128
                        pS, pSk = ps[oth + nn], ("ps", oth + nn)
                        prevk = ("kn", g, ((c0 - 128) // 512) * 512) if c0 >= 128 else ("kn", g, -1)
                        for hh in range(2):
                            h = 2 * g + hh
                            MM(pS[:, hh * 256:hh * 256 + 128], kn[g][:, c0:c0 + 128], qn[h][:, c0:c0 + 128], True, True,
                               [prevk, ("qn", h, t0)], [pSk])
                            MM(pS[:, hh * 256 + 128:hh * 256 + 256], kn[g][:, 128 + c0:256 + c0], qn[h][:, c0:c0 + 128],
                               True, True, [("kn", g, t0), ("qn", h, t0)], [pSk])
                    for nn in range(4):
                        pS, pSk = ps[oth + nn], ("ps", oth + nn)
                        ACT(E[nn], pS[:, :], AF.Exp, [pSk], [("E", nn)])
                        TT(PT[nn], E[nn], amask[:, g, :], ALU.mult, [("E", nn), "amask"], [("PT", nn)])
                    for nn in range(4):
                        n = ti * 4 + nn
                        skip_prev = (si == 0 and n == 0)
                        s2 = nn
                        for hh in range(2):
                            cs = slice(nn * 128, (nn + 1) * 128)
                            prevP = PT[s2][:, hh * 256:hh * 256 + 128]
                            curP = PT[s2][:, hh * 256 + 128:hh * 256 + 256]
                            if not skip_prev:
                                MM(pO[hh][0][0:64, cs], Va[:, n, g * 64:(g + 1) * 64], prevP, True, False,
                                   [("Va", n), ("PT", s2)], [pO[hh][1]])
                            MM(pO[hh][0][0:64, cs], Va[:, n + 1, g * 64:(g + 1) * 64], curP, skip_prev, True,
                               [("Va", n + 1), ("PT", s2)], [pO[hh][1]])
                            if not skip_prev:
                                MM(pD[hh][0][0:64, cs], onesv[:, 4, 0:64], prevP, True, False, [("PT", s2), "onesv"],
                                   [pD[hh][1]])
                            MM(pD[hh][0][0:64, cs], onesv[:, 4, 0:64], curP, skip_prev, True, [("PT", s2), "onesv"],
                               [pD[hh][1]])
                    for hh in range(2):
                        h = 2 * g + hh
                        ACT(lnd, pD[hh][0][0:64, :], AF.Ln, [pD[hh][1], "esink"], ["qsb"],
                            bias=esink[:, l * 4 + h:l * 4 + h + 1])
                        ACT(den, lnd, AF.Exp, ["qsb"], ["den"], scale=-1.0)
                        TT(oT[0:64, 4 + h, t0:t0 + 512], pO[hh][0][0:64, :], den[:, :], ALU.mult, [pO[hh][1], "den"],
                           [("o", 4 + h, t0)])
            pctr[0] = 0
            if si == 0:
                t0 = TP
                HB = NS // 2
                kcf = aalloc([HB, 128], F32)
                vcb = aalloc([NS, 128], BF16)
                kcT = aalloc([NS, 128], BF16, parts=64)
                vnT = aalloc([2, NS], F32, parts=64)
                knS = aalloc([2, NS], F32, parts=64)
                Es = aalloc([1, 4 * NS], F32)[:, 0, :]
                Ps = aalloc([4, NS], BF16)
                prod = aalloc([1, NS], BF16, parts=64)[:, 0, :]
                pn = aalloc([4, NS], F32, parts=64)
                num = aalloc([4, NS], F32, parts=64)
                dn = aalloc([4, NS], F32, parts=64)
                DMA("pool", vcb[:], cv[l].rearrange("b k f -> k b f"), (), ["vcb"], "vcb")
                DMA("sp", kws[l, :, 0:127, :], ck[l, :, 1:128, :], (), (), "kws", final=True)
                DMA("sp", vws[l, :, 0:127, :], cv[l, :, 1:128, :], (), (), "vws", final=True)
                for g in range(2):
                    p, pk = pb()
                    for k in range(8):
                        MM(p[0:64, 0:NS], wa[:, k, 384 + g * 64:384 + (g + 1) * 64], hT[:, k, t0:t0 + NS], k == 0, k == 7,
                           [gk, ("h", t0)], [pk])
                    CP(vnT[:, g, :], p[0:64, 0:NS], [pk], [("vnT", g)])
                    DMA("sp", vws[l, :, 127, g * 64:(g + 1) * 64].rearrange("b d -> d b"), vnT[:, g, :], [("vnT", g)], (),
                        "vws2", final=True, slow=True)
                    p, pk = pb()
                    for k in range(8):
                        MM(p[0:64, 0:NS], wa[:, k, 256 + g * 64:256 + (g + 1) * 64], hT[:, k, t0:t0 + NS], k == 0, k == 7,
                           [gk, ("h", t0)], [pk])
                    CP(qsb[:, 0:NS], p[0:64, 0:NS], [pk], ["qsb"], eng="act")
                    ACT(sqb[:, 0:NS], p[0:64, 0:NS], AF.Square, [pk], ["sqbA"])
                    p2, pk2 = pb()
                    MM(p2[0:64, 0:NS], onesv[0:64, 1, 0:64], sqb[:, 0:NS], True, True, ["sqbA", "onesv"], [pk2])
                    rstd_from(p2[0:64, 0:NS], pk2, NS, 64, rb, "rbA")
                    STT(knS[:, g, :], qsb[:, 0:NS], kgs[:, l:l + 1], rb[:, 0:NS], ALU.mult, ALU.mult, ["qsb", "rbA"],
                        [("knS", g)])
                    DMA("sp", kws[l, :, 127, g * 64:(g + 1) * 64].rearrange("b d -> d b"), knS[:, g, :], [("knS", g)], (),
                        "kws2", final=True, slow=True)
                pS, pSk = ps[7], ("ps", 7)
                for g in range(2):
                    for hb in range(2):
                        DMA("sp", kcf[:], ck[l, hb * HB:(hb + 1) * HB].rearrange("b k f -> k b f"), (), ["kcf"], "kcf")
                        for q in range(HB // 4):
                            p, pk = ps[q % 4], ("ps", q % 4)
                            for bb in range(4):
                                TR(p[0:64, bb * 128:(bb + 1) * 128], kcf[:, 4 * q + bb, g * 64:(g + 1) * 64], ident[:, :],
                                   ["kcf", "ident"], [pk])
                            b0 = hb * HB + 4 * q
                            CP(kcT[:, b0:b0 + 4, :], p[0:64, :].rearrange("p (a b) -> p a b", a=4), [pk],
                               ["kcT"], eng="act" if q % 2 else "dve")
                    for hh in range(2):
                        h = 2 * g + hh
                        for b in range(NS):
                            MM(pS[:, h * NS + b:h * NS + b + 1], kcT[:, b, :], qn[h][:, t0 + b:t0 + b + 1], True, True,
                               ["kcT", ("qn", h, t0)], [pSk])
                pctr[0] = 0
                ACT(Es, pS[:, 0:4 * NS], AF.Exp, [pSk], ["Es"])
                TT(Ps[:], Es.rearrange("p (h b) -> p h b", h=4), smask[:, :].unsqueeze(2).to_broadcast([128, 4, NS]),
                   ALU.mult, ["Es", "smask"], ["Ps"])
                pO2, pO2k = pb()
                for h in range(4):
                    g = h // 2
                    for b in range(NS):
                        MM(pO2[0:64, h * NS + b:h * NS + b + 1], vcb[:, b, g * 64:(g + 1) * 64], Ps[:, h, b:b + 1], True, True,
                           ["vcb", "Ps"], [pO2k])
                pD2, pD2k = pb()
                MM(pD2[0:64, 0:4 * NS], onesv[:, 4, 0:64], Ps[:].rearrange("p h b -> p (h b)"), True, True,
                   ["Ps", "onesv"], [pD2k])
                pN, pNk = pb()
                for h in range(4):
                    g = h // 2
                    TT(prod, qn[h][:, t0:t0 + NS], kn[g][:, 128 + t0:128 + t0 + NS], ALU.mult,
                       [("qn", h, t0), ("kn", g, t0)], ["prodA"])
                    MM(pN[0:64, h * NS:(h + 1) * NS], onesv[0:64, 4, 0:64], prod, True, True, ["prodA", "onesv"], [pNk])
                ACT(pn[:].rearrange("p h b -> p (h b)"), pN[0:64, 0:4 * NS], AF.Exp, [pNk], ["pn"])
                for h in range(4):
                    g = h // 2
                    TT(num[:, h, :], pn[:, h, :], vnT[:, g, :], ALU.mult, ["pn", ("vnT", g)], ["num"])
                TT(num[:].rearrange("p h b -> p (h b)"), num[:].rearrange("p h b -> p (h b)"), pO2[0:64, 0:4 * NS], ALU.add,
                   ["num", pO2k], ["num"])
                TT(dn[:].rearrange("p h b -> p (h b)"), pn[:].rearrange("p h b -> p (h b)"), pD2[0:64, 0:4 * NS], ALU.add,
                   ["pn", pD2k], ["dn"])
                for h in range(4):
                    TS(dn[:, h, :], dn[:, h, :], esink[:, l * 4 + h:l * 4 + h + 1], ALU.add, ["dn", "esink"], ["dn"])
                RCP(dn[:].rearrange("p h b -> p (h b)"), dn[:].rearrange("p h b -> p (h b)"), ["dn"], ["dn"])
                for h in range(4):
                    TT(oT[0:64, 4 + h, t0:t0 + NS], num[:, h, :], dn[:, h, :], ALU.mult, ["num", "dn"], [("o", 4 + h, t0)])
            S.barrier()

        def conv(si, l, last):
            areset()
            wv3 = W["w_in"][l].rearrange("(k p) f -> p k f", p=128)
            Wg, gk = wg_slot()
            wc = wview(Wg, 0, 8, 512)
            wpw = wview(Wg, 4096, 2, 256)
            Dg = aalloc([2, 31, 128], BF16)
            uT = aalloc([2, 30 + TP], BF16)
            sig = aalloc([2, 512], F32)
            ulast = aalloc([2, 30], F32)
            yb = aalloc([2, 512], F32)
            ybf = aalloc([2, 512], BF16)
            ysq = aalloc([2, 512], BF16)
            msb = aalloc([1, 512], F32)[:, 0, :]
            var = aalloc([1, 512], F32)[:, 0, :]
            rb = aalloc([1, 512], F32)[:, 0, :]
            dd = aalloc([2, 512], F32)
            ysl = aalloc([2, 512], BF16)
            WDMA(wc[:], wv3[:, :, 2048:2560], gk)
            WDMA(wpw[:], W["conv_pw"][l].rearrange("(k p) o -> p k o", p=128), gk)
            if si == 0:
                wcv = wview(Wg, 4608, 1, 256, parts=30)[:, 0, :]
                WDMA(wcv, W["conv_w"][l, 0:30, :], gk)
            for c in range(2):
                for j in range(31):
                    TS(Dg[:, c, j, :], identb[:, :], convw[:, l, c, j:j + 1], ALU.mult, ["identb", "convw"], ["Dg"],
                       eng="pool")
                CP(uT[:, c, 0:30], uh[:, l, c, :], ["uh"], [("uT", c, -1)])

            def ln_epilogue(n, t0):
                pm, pmk = pb()
                pq, pqk = pb()
                for c in range(2):
                    MM(pm[:, 0:n], onesv[:, 3, :], ybf[:, c, 0:n], c == 0, c == 1, [("ybf", c), "onesv"], [pmk])
                for c in range(2):
                    MM(pq[:, 0:n], onesv[:, 3, :], ysq[:, c, 0:n], c == 0, c == 1, [("ysq", c), "onesv"], [pqk])
                CP(msb[:, 0:n], pm[:, 0:n], [pmk], ["msb"], eng="act")
                STT(var[:, 0:n], msb[:, 0:n], -1.0, msb[:, 0:n], ALU.mult, ALU.mult, ["msb"], ["var"])
                TT(var[:, 0:n], var[:, 0:n], pq[:, 0:n], ALU.add, ["var", pqk], ["var"])
                ACT(rb[:, 0:n], var[:, 0:n], AF.Ln, ["var", "epsb"], ["rbC"], bias=epsb[:, 0:1])
                ACT(rb[:, 0:n], rb[:, 0:n], AF.Exp, ["rbC"], ["rbC"], scale=-0.5)
                for c in range(2):
                    TT(dd[:, c, 0:n], yb[:, c, 0:n], msb[:, 0:n], ALU.subtract, [("yb", c), "msb"], [("dd", c)])
                    TT(dd[:, c, 0:n], dd[:, c, 0:n], rb[:, 0:n], ALU.mult, [("dd", c), "rbC"], [("dd", c)])
                    ACT(ysl[:, c, 0:n], dd[:, c, 0:n], AF.Silu, [("dd", c), "lng", "lnb"], [("ysl", c)],
                        bias=lnb[:, l, c:c + 1], scale=lng[:, l, c:c + 1])
                for oc in range(2):
                    p, pk = pb()
                    for c in range(2):
                        MM(p[:, 0:n], wpw[:, c, oc * 128:(oc + 1) * 128], ysl[:, c, 0:n], c == 0, c == 1,
                           [gk, ("ysl", c)], [pk])
                    CP(oT[:, 8 + oc, t0:t0 + n], p[:, 0:n], [pk], [("o", 8 + oc, t0)], eng="act" if oc else "dve")

            uS = aalloc([2, NS], F32)
            for (t0, n) in tiles_of(si):
                for c in range(2):
                    pa, pak = pb()
                    pg, pgk = pb()
                    for k in range(8):
                        MM(pa[:, 0:n], wc[:, k, c * 128:(c + 1) * 128], hT[:, k, t0:t0 + n], k == 0, k == 7,
                           [gk, ("h", t0)], [pak])
                    for k in range(8):
                        MM(pg[:, 0:n], wc[:, k, 256 + c * 128:256 + (c + 1) * 128], hT[:, k, t0:t0 + n], k == 0, k == 7,
                           [gk, ("h", t0)], [pgk])
                    ACT(sig[:, c, 0:n], pg[:, 0:n], AF.Sigmoid, [pgk], [("sig", c)])
                    if t0 < TP:
                        TT(uT[:, c, 30 + t0:30 + t0 + n], pa[:, 0:n], sig[:, c, 0:n], ALU.mult, [pak, ("sig", c)],
                           [("uT", c, t0)])
                        if t0 == 512:
                            TT(ulast[:, c, :], pa[:, 482:512], sig[:, c, 482:512], ALU.mult, [pak, ("sig", c)],
                               [("ulast", c)])
                            if last:
                                DMA("sp", cvp[l, :, c * 128:(c + 1) * 128].rearrange("t p -> p t"), ulast[:, c, :],
                                    [("ulast", c)], (), "cvp", final=True, slow=True)
                            CP(uh[:, l, c, :], uT[:, c, TP:TP + 30], [("uT", c, 512)], ["uh"])
                    else:
                        TT(uS[:, c, :], pa[:, 0:n], sig[:, c, 0:n], ALU.mult, [pak, ("sig", c)], [("uS", c)])
            wst = wout_prepare(l)
            taps = {}
            for ti in range(2):
                t0 = ti * 512
                for c in range(2):
                    p, pk = pb()
                    rk = [("uT", c, t0), ("uT", c, t0 - 512 if t0 else -1), "Dg"]
                    for j in range(31):
                        MM(p[:, :], Dg[:, c, j, :], uT[:, c, t0 + j:t0 + j + 512], j == 0, j == 30, rk, [pk])
                    taps[(ti, c)] = (p, pk)
            def evac(ti):
                for c in range(2):
                    p, pk = taps[(ti, c)]
                    ACT(yb[:, c, :], p[:, :], AF.Identity, [pk, "convb"], [("yb", c)], bias=convb[:, l, c:c + 1])
                    ACT(ysq[:, c, :], p[:, :], AF.Square, [pk, "convb"], [("ysq", c)], bias=convb[:, l, c:c + 1])
                    CP(ybf[:, c, :], yb[:, c, :], [("yb", c)], [("ybf", c)])

            evac(0)
            ln_epilogue(512, 0)
            evac(1)
            wout_tile(wst, 0, 512)
            ln_epilogue(512, 512)
            if si == 0:
                t0 = TP
                cbb = aalloc([NS, 256], BF16, parts=30)
                msk = aalloc([4, 128], F32)
                y0 = aalloc([2, NS], F32)
                DMA("pool", cbb[:], sconv[l].rearrange("b j c -> j b c"), (), ["cbb"], "cbb")
                DMA("sp", cvs[l, :, 0:29, :], sconv[l, :, 1:30, :], (), (), "cvs", final=True)
                for c in range(2):
                    DMA("sp", cvs[l, :, 29, c * 128:(c + 1) * 128].rearrange("b p -> p b"), uS[:, c, :], [("uS", c)], (),
                        "cvs2", final=True, slow=True)
                    for q in range(NS // 4):
                        p, pk = pb()
                        MM(p[:, :], wcv[:, c * 128:(c + 1) * 128], cbb[:, 4 * q:4 * q + 4, c * 128:(c + 1) * 128], True, True,
                           [gk, "cbb"], [pk])
                        TT(msk[:], p[:, :].rearrange("p (a b) -> p a b", a=4),
                           ident[:, :].unsqueeze(1).to_broadcast([128, 4, 128]), ALU.mult, [pk, "ident"], ["msk"])
                        S.op("dve", (lambda o_, i_: (lambda e: e.reduce_sum(o_, i_, axis=AX.X)))(y0[:, c, 4 * q:4 * q + 4],
                                                                                                  msk[:]),
                             ["msk"], [("y0", c)])
                    STT(yb[:, c, 0:NS], uS[:, c, :], convw[:, l, c, 30:31], y0[:, c, :], ALU.mult, ALU.add,
                        [("uS", c), ("y0", c), "convw"], [("yb", c)])
                    TS(yb[:, c, 0:NS], yb[:, c, 0:NS], convb[:, l, c:c + 1], ALU.add, [("yb", c), "convb"], [("yb", c)])
                    CP(ybf[:, c, 0:NS], yb[:, c, 0:NS], [("yb", c)], [("ybf", c)])
                    ACT(ysq[:, c, 0:NS], yb[:, c, 0:NS], AF.Square, [("yb", c)], [("ysq", c)])
                ln_epilogue(NS, t0)
            S.barrier()
            wout_tile(wst, 512, 512)
            if si == 0:
                wout_tile(wst, TP, NS)

        def wout_prepare(l):
            Wg, gk = wg_slot()
            Wd, dk = wd_slot()
            wo8 = wview(Wg, 0, 8, D)
            wo2 = wview(Wd, 0, 2, D)
            wod = W["w_out"][l]
            WDMA(wo8[:, 0:4, :], wod[0:512, :].rearrange("(k p) d -> p k d", p=128), gk)
            WDMA(wo8[0:64, 4:8, :], wod[512:768, :].rearrange("(k p) d -> p k d", p=64), gk)
            WDMA(wo2[:, :, :], wod[768:1024, :].rearrange("(k p) d -> p k d", p=128), dk)
            return wo8, wo2, gk, dk

        def wout_tile(wst, t0, n):
            wo8, wo2, gk, dk = wst
            for c in range(8):
                p, pk = pb()
                for k in range(10):
                    kp = 64 if 4 <= k < 8 else 128
                    wsrc = wo8[0:kp, k, c * 128:(c + 1) * 128] if k < 8 else wo2[:, k - 8, c * 128:(c + 1) * 128]
                    MM(p[:, 0:n], wsrc, oT[0:kp, k, t0:t0 + n], k == 0, k == 9, [gk, dk, ("o", k, t0)], [pk])
                TT(xT[:, c, t0:t0 + n], xT[:, c, t0:t0 + n], p[:, 0:n], ALU.add, [pk, ("x", c, t0)], [("x", c, t0)])

        PH = DBG.get("phases")

        def on(name):
            return PH is None or name in PH

        for si in range(DBG.get("nseg", NSEG)):
            last = si == DBG.get("nseg", NSEG) - 1
            load_x(si)
            for l in range(DBG.get("depth", DEPTH)):
                if on("ffn1"):
                    ffn(si, l, 1)
                if on("mixnorm"):
                    areset()
                    rmsnorm(si, mng, l)
                    S.barrier()
                if on("ret"):
                    for pr in range(2):
                        retention(si, l, pr, last)
                if on("att"):
                    attention(si, l, last)
                if on("conv"):
                    conv(si, l, last)
                if on("ffn2"):
                    ffn(si, l, 2)
            store_y(si)
        S.emit()
        print("sched stats", S.stats, flush=True)
    return nc


_CACHE = {}


def kernel(**inputs):
    f32 = lambda a: np.ascontiguousarray(np.asarray(a), dtype=np.float32)
    inp = {k: f32(v) for k, v in inputs.items()}
    if "nc" not in _CACHE:
        _CACHE["nc"] = build()
    nc = _CACHE["nc"]
    ctab, _ = const_tables()
    wnames = ["ffn1_norm", "ffn1_wg", "ffn1_wu", "ffn1_wd", "mix_norm", "w_in", "ret_norm_g", "q_norm_g", "k_norm_g",
              "sinks", "conv_w", "conv_b", "conv_ln_g", "conv_ln_b", "conv_pw", "w_out", "ffn2_norm", "ffn2_wg",
              "ffn2_wu", "ffn2_wd"]
    in_maps = []
    zx = np.zeros((NSEG * TP, D), np.float32)
    for core in range(NCORES):
        m = {k: inp[k] for k in wnames}
        m.update(ctab)
        if core in ACT_CORES:
            c = ACT_CORES.index(core)
            sl = slice(c * NS, (c + 1) * NS)
            m["xp"] = inp["x_prompt"][c]
            m["xs"] = np.ascontiguousarray(inp["x_sample"][sl, 0, :])
            m["sret"] = np.ascontiguousarray(inp["state_ret"][:, sl])
            m["ck"] = np.ascontiguousarray(inp["cache_k_win"][:, sl].reshape(DEPTH, NS, 128, 128))
            m["cv"] = np.ascontiguousarray(inp["cache_v_win"][:, sl].reshape(DEPTH, NS, 128, 128))
            m["sconv"] = np.ascontiguousarray(inp["state_conv"][:, sl])
        else:
            m["xp"] = zx
            m["xs"] = np.zeros((NS, D), np.float32)
            m["sret"] = np.zeros((DEPTH, NS, 4, 64, 128), np.float32)
            m["ck"] = np.zeros((DEPTH, NS, 128, 128), np.float32)
            m["cv"] = np.zeros((DEPTH, NS, 128, 128), np.float32)
            m["sconv"] = np.zeros((DEPTH, NS, 30, 256), np.float32)
        in_maps.append(m)
    res = run_bass_kernel_spmd(nc, in_maps, core_ids=list(range(NCORES)))
    R = res.results
    A = ACT_CORES
    y_p = np.stack([R[c]["yp"] for c in A]).reshape(BATCH, SEQ, D)
    y_s = np.concatenate([R[c]["ys"] for c in A]).reshape(DECB, 1, D)
    ret_p = np.stack([R[c]["retp"] for c in A], axis=1)
    ret_s = np.concatenate([R[c]["rets"] for c in A], axis=1)
    kw_p = np.stack([R[c]["kwp"] for c in A], axis=1).reshape(DEPTH, BATCH, 128, 2, 64)
    kw_s = np.concatenate([R[c]["kws"] for c in A], axis=1).reshape(DEPTH, DECB, 128, 2, 64)
    vw_p = np.stack([R[c]["vwp"] for c in A], axis=1).reshape(DEPTH, BATCH, 128, 2, 64)
    vw_s = np.concatenate([R[c]["vws"] for c in A], axis=1).reshape(DEPTH, DECB, 128, 2, 64)
    cv_p = np.stack([R[c]["cvp"] for c in A], axis=1)
    cv_s = np.concatenate([R[c]["cvs"] for c in A], axis=1)
    outs = (y_p, y_s, ret_p, ret_s, kw_p, kw_s, vw_p, vw_s, cv_p, cv_s)
    return tuple(np.ascontiguousarray(o, dtype=np.float32) for o in outs)
```

```python
import numpy as np
from contextlib import ExitStack
import concourse.bass as bass
import concourse.mybir as mybir
from concourse.bass_utils import run_bass_kernel_spmd

F32 = mybir.dt.float32
BF16 = mybir.dt.bfloat16
ALU = mybir.AluOpType
AF = mybir.ActivationFunctionType
AX = mybir.AxisListType

D = 1024
FF = 2816
DEPTH = 2
SEQ = 4096
BATCH = 4
DECB = 128
IN_DIM = 2560
EPS = 1e-6
NCORES = 8
ACTIVE = 4
NSEG = 4
TP = 1024
NS = DECB // ACTIVE
NT = TP + NS
SEM_LIMIT = 30000
ACT_CORES = [0, 2, 4, 6]
DBG = {}


class Sched:
    ENGS = ("pe", "act", "dve", "pool", "sp")

    def __init__(self, nc, es):
        self.nc = nc
        self.es = es
        self.ops = []
        self.last_write = {}
        self.readers = {}
        self.out_dma_ops = []
        self.bar_deps = set()
        self.bar_pending = set()

    def op(self, eng, fn, reads=(), writes=(), dma=False, semkey=None, final=False, nobar=False):
        idx = len(self.ops)
        raw = set()
        other = set()
        for k in reads:
            w = self.last_write.get(k)
            if w is not None:
                raw.add(w)
        for k in writes:
            w = self.last_write.get(k)
            if w is not None:
                other.add(w)
            for r in self.readers.get(k, {}).values():
                other.add(r)
        for k in writes:
            self.last_write[k] = idx
            self.readers[k] = {}
        rk = ("dma", semkey) if dma else eng
        for k in reads:
            self.readers.setdefault(k, {})[rk] = idx
        if eng in self.bar_pending and not nobar and eng != "pe":
            if any(not self.persistent(k) for k in list(reads) + list(writes)):
                other |= self.bar_deps
                self.bar_pending.discard(eng)
        deps = set()
        for d in raw | other:
            if d == idx:
                continue
            p = self.ops[d]
            if (not p["dma"]) and p["eng"] == eng:
                if eng == "pe":
                    continue
            deps.add(d)
        self.ops.append(dict(eng=eng, fn=fn, deps=deps, dma=dma, semkey=semkey, sig=False, ev=None))
        if final:
            self.out_dma_ops.append(idx)
        return idx

    def barrier(self):
        last = {}
        for i, o in enumerate(self.ops):
            if o["fn"] is None:
                continue
            if o["dma"]:
                last[("dma", o["semkey"])] = i
            else:
                last[o["eng"]] = i
        self.bar_deps = set(last.values())
        self.bar_pending = set(self.ENGS)
        self.last_write = {k: v for k, v in self.last_write.items() if self.persistent(k)}
        self.readers = {k: v for k, v in self.readers.items() if self.persistent(k)}

    PERSIST = frozenset(("x", "h", "o", "ps", "WG", "WD", "stage", "Sst", "kTh", "Vh", "uh", "ident", "identb", "onesv",
                         "epsb", "rdm", "qdec", "kdec", "g128", "g1", "amask", "smask", "eyeN", "hsel", "n1g", "mng",
                         "n2g", "retg", "qg8", "kgs", "esink", "convw", "convb", "lng", "lnb", "oT"))

    def persistent(self, k):
        return (k[0] if isinstance(k, tuple) else k) in self.PERSIST

    def emit(self):
        nc = self.nc
        ops = self.ops
        if self.out_dma_ops:
            ops.append(dict(eng="sp", fn=None, deps=set(self.out_dma_ops), dma=False, semkey=None,
                            sig=False, ev=None))
        for o in ops:
            for d in o["deps"]:
                ops[d]["sig"] = True
        eng_sems = {e: [] for e in self.ENGS}
        eng_cnt = {e: 0 for e in self.ENGS}
        dma_sems = {}
        dma_cnt = {}
        nsem = [0]

        def new_sem():
            nsem[0] += 1
            return self.es.enter_context(nc.semaphore("s%d" % nsem[0]))

        for o in ops:
            if o["fn"] is None:
                continue
            if o["dma"]:
                k = o["semkey"]
                if k not in dma_sems or dma_cnt[k] + 16 >= SEM_LIMIT:
                    dma_sems[k] = new_sem()
                    dma_cnt[k] = 0
                dma_cnt[k] += 16
                o["ev"] = (dma_sems[k], dma_cnt[k])
            elif o["sig"]:
                e = o["eng"]
                if not eng_sems[e] or eng_cnt[e] >= SEM_LIMIT:
                    eng_sems[e].append(new_sem())
                    eng_cnt[e] = 0
                eng_cnt[e] += 1
                o["ev"] = (eng_sems[e][-1], eng_cnt[e])
        nwaits = [0]

        def run_engine(e, eng):
            seen = {}
            for o in ops:
                if o["eng"] != e:
                    continue
                need = {}
                for d in o["deps"]:
                    sem, val = ops[d]["ev"]
                    key = id(sem)
                    if seen.get(key, 0) >= val:
                        continue
                    if key not in need or need[key][1] < val:
                        need[key] = (sem, val)
                for key, (sem, val) in need.items():
                    eng.wait_ge(sem, val)
                    seen[key] = val
                    nwaits[0] += 1
                if o["fn"] is None:
                    continue
                ins = o["fn"](eng)
                if o["ev"] is not None:
                    ins.then_inc(o["ev"][0], 16 if o["dma"] else 1)

        with nc.Block() as block:
            @block.tensor
            def _(eng):
                run_engine("pe", eng)

            @block.scalar
            def _(eng):
                run_engine("act", eng)

            @block.vector
            def _(eng):
                run_engine("dve", eng)

            @block.gpsimd
            def _(eng):
                run_engine("pool", eng)

            @block.sync
            def _(eng):
                run_engine("sp", eng)
        self.stats = dict(n_ops=len(ops), n_waits=nwaits[0], n_sems=nsem[0])


def const_tables():
    lg = np.log1p(-np.exp2(-5.0 - np.arange(4, dtype=np.float64)))
    slopes = np.exp2(-8.0 * (np.arange(4, dtype=np.float64) + 1.0) / 4.0)
    j = np.arange(128)[:, None].astype(np.float64)
    i = np.arange(128)[None, :].astype(np.float64)
    t = {}
    rdm = np.zeros((128, 4, 128), np.float64)
    for h in range(4):
        rdm[:, h, :] = np.where(i >= j, np.exp(lg[h] * np.maximum(i - j, 0.0)), 0.0)
    t["c_rdm"] = rdm
    qdec = np.zeros((128, 2, 128), np.float64)
    kdec = np.zeros((128, 4), np.float64)
    g128 = np.zeros((128, 2), np.float64)
    g1 = np.zeros((128, 2), np.float64)
    for pr in range(2):
        for hh in range(2):
            h = 2 * pr + hh
            qdec[64 * hh:64 * hh + 64, pr, :] = np.exp(lg[h] * (np.arange(128) + 1.0))[None, :]
            g128[64 * hh:64 * hh + 64, pr] = np.exp(lg[h] * 128.0)
            g1[64 * hh:64 * hh + 64, pr] = np.exp(lg[h])
    for h in range(4):
        kdec[:, h] = np.exp(lg[h] * (127.0 - np.arange(128))) * 0.125
    t["c_qdec"] = qdec
    t["c_kdec"] = kdec
    t["c_g128"] = g128
    t["c_g1"] = g1
    am = np.zeros((128, 2, 2, 2, 128), np.float64)
    for g in range(2):
        for hh in range(2):
            h = 2 * g + hh
            dist_prev = 128.0 + i - j
            am[:, g, hh, 0, :] = np.where(j > i, np.exp(-slopes[h] * dist_prev), 0.0)
            am[:, g, hh, 1, :] = np.where(i >= j, np.exp(-slopes[h] * (i - j)), 0.0)
    t["c_amask"] = am
    sm = np.zeros((128, 4), np.float64)
    p = np.arange(128, dtype=np.float64)
    for h in range(4):
        sm[:, h] = np.where(p >= 1, np.exp(-slopes[h] * (128.0 - p)), 0.0)
    t["c_smask"] = sm
    t["c_eye"] = np.eye(NS)
    hs = np.zeros((128, 2, 128), np.float64)
    hs[0:64, 0, :] = 1.0
    hs[64:128, 1, :] = 1.0
    t["c_hsel"] = hs
    return {k: np.ascontiguousarray(v, dtype=np.float32) for k, v in t.items()}, lg


def build():
    ctab, lg = const_tables()
    gam = [float(np.exp(x)) for x in lg]
    nc = bass.Bass("TRN2", target_bir_lowering=False)

    def din(name, shape):
        return nc.dram_tensor(name, list(shape), F32, kind="ExternalInput").ap()

    def dout(name, shape):
        return nc.dram_tensor(name, list(shape), F32, kind="ExternalOutput").ap()

    xp = din("xp", (NSEG * TP, D))
    xs = din("xs", (NS, D))
    sret = din("sret", (DEPTH, NS, 4, 64, 128))
    ck = din("ck", (DEPTH, NS, 128, 128))
    cv = din("cv", (DEPTH, NS, 128, 128))
    sconv = din("sconv", (DEPTH, NS, 30, 256))
    W = {}
    for nm, shp in (("ffn1_norm", (DEPTH, D)), ("ffn1_wg", (DEPTH, D, FF)), ("ffn1_wu", (DEPTH, D, FF)),
                    ("ffn1_wd", (DEPTH, FF, D)), ("mix_norm", (DEPTH, D)), ("w_in", (DEPTH, D, IN_DIM)),
                    ("ret_norm_g", (DEPTH, 4, 128)), ("q_norm_g", (DEPTH, 64)), ("k_norm_g", (DEPTH, 64)),
                    ("sinks", (DEPTH, 4)), ("conv_w", (DEPTH, 31, 256)), ("conv_b", (DEPTH, 256)),
                    ("conv_ln_g", (DEPTH, 256)), ("conv_ln_b", (DEPTH, 256)), ("conv_pw", (DEPTH, 256, 256)),
                    ("w_out", (DEPTH, D, D)), ("ffn2_norm", (DEPTH, D)), ("ffn2_wg", (DEPTH, D, FF)),
                    ("ffn2_wu", (DEPTH, D, FF)), ("ffn2_wd", (DEPTH, FF, D))):
        W[nm] = din(nm, shp)
    C = {k: din(k, v.shape) for k, v in ctab.items()}
    yp = dout("yp", (NSEG * TP, D))
    ys = dout("ys", (NS, D))
    retp = dout("retp", (DEPTH, 4, 64, 128))
    rets = dout("rets", (DEPTH, NS, 4, 64, 128))
    kwp = dout("kwp", (DEPTH, 128, 128))
    kws = dout("kws", (DEPTH, NS, 128, 128))
    vwp = dout("vwp", (DEPTH, 128, 128))
    vws = dout("vws", (DEPTH, NS, 128, 128))
    cvp = dout("cvp", (DEPTH, 30, 256))
    cvs = dout("cvs", (DEPTH, NS, 30, 256))

    es = ExitStack()
    with es:
        def sb(name, shape, dt):
            return es.enter_context(nc.sbuf_tensor(name, list(shape), dt))

        S = Sched(nc, es)
        ps = [es.enter_context(nc.psum_tensor("ps%d" % i, [128, 512], F32)) for i in range(8)]
        pctr = [0]

        def pb():
            i = pctr[0] % 8
            pctr[0] += 1
            return ps[i], ("ps", i)

        def MM(out, lhsT, rhs, st, sp, r, w):
            S.op("pe", lambda e: e.matmul(out, lhsT=lhsT, rhs=rhs, start=st, stop=sp), r, w)

        def TR(out, in_, idn, r, w):
            S.op("pe", lambda e: e.transpose(out, in_, idn), r, w)

        def ACT(out, in_, func, r, w, bias=None, scale=1.0):
            if bias is None:
                S.op("act", lambda e: e.activation(out=out, in_=in_, func=func, scale=scale), r, w)
            else:
                S.op("act", lambda e: e.activation(out=out, in_=in_, func=func, bias=bias, scale=scale), r, w)

        def STT(out, in0, scalar, in1, op0, op1, r, w, eng="dve"):
            S.op(eng, lambda e: e.scalar_tensor_tensor(out=out, in0=in0, scalar=scalar, in1=in1, op0=op0, op1=op1), r, w)

        def TT(out, in0, in1, op, r, w, eng="dve"):
            S.op(eng, lambda e: e.tensor_tensor(out=out, in0=in0, in1=in1, op=op), r, w)

        def TS(out, in0, s1, op0, r, w, s2=None, op1=None, eng="dve"):
            if op1 is None:
                S.op(eng, lambda e: e.tensor_scalar(out=out, in0=in0, scalar1=s1, scalar2=None, op0=op0), r, w)
            else:
                S.op(eng, lambda e: e.tensor_scalar(out=out, in0=in0, scalar1=s1, scalar2=s2, op0=op0, op1=op1), r, w)

        def CP(out, in_, r, w, eng="dve"):
            if DBG.get("noactcp") and eng == "act":
                eng = "dve"
            if eng == "act":
                S.op("act", lambda e: e.activation(out=out, in_=in_, func=AF.Identity), r, w)
            else:
                S.op(eng, lambda e: e.tensor_copy(out, in_), r, w)

        def RCP(out, in_, r, w):
            S.op("dve", lambda e: e.reciprocal(out, in_), r, w)

        def MSET(ap, val, w, eng="dve"):
            S.op(eng, lambda e: e.memset(ap, val), (), w)

        def DMA(q, out, in_, r, w, key, final=False, slow=False):
            if slow:
                S.op(q, lambda e: e.dma_start(out=out, in_=in_, allow_slow_non_contiguous=True), r, w, dma=True,
                     semkey=key, final=final)
            else:
                S.op(q, lambda e: e.dma_start(out=out, in_=in_), r, w, dma=True, semkey=key, final=final)

        xT = sb("xT", [128, 8, NT], F32)
        hT = sb("hT", [128, 8, NT], BF16)
        oT = sb("oT", [128, 10, NT], BF16)
        ident = sb("ident", [128, 128], F32)
        identb = sb("identb", [128, 128], BF16)
        onesv = sb("onesv", [128, 5, 128], BF16)
        epsb = sb("epsb", [128, 1], F32)
        stage = sb("stage", [128, 2, D], F32)
        rdm = sb("rdm", [128, 4, 128], F32)
        qdec = sb("qdec", [128, 2, 128], F32)
        kdec = sb("kdec", [128, 4], F32)
        g128 = sb("g128", [128, 2], F32)
        g1 = sb("g1", [128, 2], F32)
        amask = sb("amask", [128, 2, 512], BF16)
        smask = sb("smask", [128, 4], F32)
        eyeN = sb("eyeN", [NS, NS], BF16)
        hsel = sb("hsel", [128, 2, 128], BF16)
        n1g = sb("n1g", [128, DEPTH, 8], F32)
        mng = sb("mng", [128, DEPTH, 8], F32)
        n2g = sb("n2g", [128, DEPTH, 8], F32)
        retg = sb("retg", [128, DEPTH, 4], F32)
        qg8 = sb("qg8", [64, DEPTH], F32)
        kgs = sb("kgs", [64, DEPTH], F32)
        esink = sb("esink", [64, DEPTH * 4], F32)
        convw = sb("convw", [128, DEPTH, 2, 31], F32)
        convb = sb("convb", [128, DEPTH, 2], F32)
        lng = sb("lng", [128, DEPTH, 2], F32)
        lnb = sb("lnb", [128, DEPTH, 2], F32)
        Sst = sb("Sst", [128, DEPTH, 2, 128], F32)
        kTh = sb("kTh", [64, DEPTH, 2, 128], BF16)
        Vh = sb("Vh", [128, DEPTH, 128], BF16)
        uh = sb("uh", [128, DEPTH, 2, 30], BF16)
        NA = 16640
        arena = sb("arena", [128, NA], F32)
        aoff = [0]
        WG = [sb("WG%d" % i, [128, 8192], BF16) for i in range(2)]
        WD = [sb("WD%d" % i, [128, 4096], BF16) for i in range(2)]
        wgc = [0]
        wdc = [0]

        def wg_slot():
            i = wgc[0] % 2
            wgc[0] += 1
            return WG[i], ("WG", i)

        def wd_slot():
            i = wdc[0] % 2
            wdc[0] += 1
            return WD[i], ("WD", i)

        def wview(Wt, off, a, b, parts=128):
            return Wt[0:parts, off:off + a * b].rearrange("p (a b) -> p a b", a=a, b=b)

        def WDMA(out, in_, key):
            S.op("pool", lambda e: e.dma_start(out=out, in_=in_), (), [key], dma=True, semkey=key, nobar=True)

        def areset():
            aoff[0] = 0

        def aalloc(shape, dt, parts=128):
            n = int(np.prod(shape))
            words = (n + 1) // 2 if dt == BF16 else n
            o = aoff[0]
            aoff[0] += words
            assert aoff[0] <= NA, ("arena overflow", aoff[0])
            a = arena[0:parts, o:o + words]
            if dt == BF16:
                a = a.bitcast(BF16)
                if n % 2:
                    a = a[:, 0:n]
            if len(shape) == 2:
                return a.rearrange("p (a b) -> p a b", a=shape[0], b=shape[1])
            if len(shape) == 3:
                return a.rearrange("p (a b c) -> p a b c", a=shape[0], b=shape[1], c=shape[2])
            return a

        MSET(ident[:], 0.0, ["ident"], eng="pool")
        S.op("pool", lambda e: e.affine_select(out=ident[:], in_=ident[:], pattern=[[-1, 128]],
                                               compare_op=ALU.not_equal, fill=1.0, base=0, channel_multiplier=1),
             ["ident"], ["ident"])
        CP(identb[:], ident[:], ["ident"], ["identb"])
        for i, v in enumerate((1.0 / 1024, 1.0 / 64, 1.0 / 128, 1.0 / 256, 1.0)):
            MSET(onesv[:, i, :], v, ["onesv"])
        MSET(epsb[:], EPS, ["epsb"])
        MSET(Sst[:], 0.0, ["Sst"])
        MSET(kTh[:], 0.0, ["kTh"])
        MSET(Vh[:], 0.0, ["Vh"])
        MSET(uh[:], 0.0, ["uh"])
        if DBG.get("zero_o"):
            MSET(oT[:], 0.0, ["oT"])
        DMA("sp", rdm[:], C["c_rdm"], (), ["rdm"], "c_rdm")
        DMA("sp", qdec[:], C["c_qdec"], (), ["qdec"], "c_qdec")
        DMA("sp", kdec[:], C["c_kdec"], (), ["kdec"], "c_kdec")
        DMA("sp", g128[:], C["c_g128"], (), ["g128"], "c_g128")
        DMA("sp", g1[:], C["c_g1"], (), ["g1"], "c_g1")
        DMA("sp", smask[:], C["c_smask"], (), ["smask"], "c_smask")
        if not DBG.get("nopoolc"):
          DMA("pool", amask[:], C["c_amask"].rearrange("p a b c d -> p a (b c d)"), (), ["amask"], "c_amask")
          DMA("pool", eyeN[:], C["c_eye"], (), ["eyeN"], "c_eyeN")
          DMA("pool", hsel[:], C["c_hsel"], (), ["hsel"], "c_hsel")
        for l in range(DEPTH if not DBG.get("noparams") else 0):
            DMA("sp", n1g[:, l, :], W["ffn1_norm"][l].rearrange("(c p) -> p c", p=128), (), ["n1g"], "c_n1g", slow=True)
            DMA("sp", mng[:, l, :], W["mix_norm"][l].rearrange("(c p) -> p c", p=128), (), ["mng"], "c_mng", slow=True)
            DMA("sp", n2g[:, l, :], W["ffn2_norm"][l].rearrange("(c p) -> p c", p=128), (), ["n2g"], "c_n2g", slow=True)
            DMA("sp", retg[:, l, :], W["ret_norm_g"][l].rearrange("h p -> p h"), (), ["retg"], "c_retg", slow=True)
            DMA("sp", qg8[:, l:l + 1], W["q_norm_g"][l].rearrange("(p o) -> p o", o=1), (), ["qg8"], "c_qg8", slow=True)
            DMA("sp", kgs[:, l:l + 1], W["k_norm_g"][l].rearrange("(p o) -> p o", o=1), (), ["kgs"], "c_kgs", slow=True)
            DMA("sp", esink[:, l * 4:(l + 1) * 4], W["sinks"][l:l + 1, :].partition_broadcast(64), (), ["esink"],
                "c_esink", slow=True)
            for c in range(2):
                DMA("sp", convw[:, l, c, :], W["conv_w"][l][:, c * 128:(c + 1) * 128].rearrange("j p -> p j"), (),
                    ["convw"], "c_convw", slow=True)
            DMA("sp", convb[:, l, :], W["conv_b"][l].rearrange("(c p) -> p c", p=128), (), ["convb"], "c_convb", slow=True)
            DMA("sp", lng[:, l, :], W["conv_ln_g"][l].rearrange("(c p) -> p c", p=128), (), ["lng"], "c_lng", slow=True)
            DMA("sp", lnb[:, l, :], W["conv_ln_b"][l].rearrange("(c p) -> p c", p=128), (), ["lnb"], "c_lnb", slow=True)
        S.barrier()
        TS(qg8[:], qg8[:], 0.125, ALU.mult, ["qg8"], ["qg8"])
        ACT(esink[:], esink[:], AF.Exp, ["esink"], ["esink"])
        S.barrier()

        def tiles_of(si):
            t = [(0, 512), (512, 512)]
            if si == 0:
                t.append((TP, NS))
            return t

        def load_x(si):
            chunks = [(xp[si * TP + n * 128: si * TP + (n + 1) * 128, :], n * 128, 128) for n in range(TP // 128)]
            if si == 0 and not DBG.get("nosample"):
                chunks.append((xs[:, :], TP, NS))
            if DBG.get("noload"):
                chunks = []
            for ci, (src, t0, n) in enumerate(chunks):
                sl = ci % 2
                DMA("sp", stage[0:n, sl, :], src, (), [("stage", sl)], ("stage", sl))
                for half in range(2):
                    p, pk = pb()
                    for q in range(4):
                        c = half * 4 + q
                        TR(p[:, q * 128:q * 128 + n], stage[0:n, sl, c * 128:(c + 1) * 128], ident[0:n, 0:n],
                           [("stage", sl), "ident"], [pk])
                    for q in range(4):
                        c = half * 4 + q
                        CP(xT[:, c, t0:t0 + n], p[:, q * 128:q * 128 + n], [pk],
                           [("x", c, (t0 // 512) * 512 if t0 < TP else TP)], eng="act" if half else "dve")

        def store_y(si):
            chunks = [(yp[si * TP + n * 128: si * TP + (n + 1) * 128, :], n * 128, 128) for n in range(TP // 128)]
            if si == 0 and not DBG.get("nosample"):
                chunks.append((ys[:, :], TP, NS))
            if DBG.get("nostore"):
                chunks = chunks[:1]
            for ci, (dst, t0, n) in enumerate(chunks):
                sl = ci % 2
                tk = (t0 // 512) * 512 if t0 < TP else TP
                for half in range(2):
                    p, pk = pb()
                    for q in range(4):
                        c = half * 4 + q
                        TR(p[0:n, q * 128:(q + 1) * 128], xT[:, c, t0:t0 + n], ident[:, :],
                           [("x", c, tk), "ident"], [pk])
                    CP(stage[0:n, sl, half * 512:(half + 1) * 512], p[0:n, :], [pk], [("stage", sl)],
                       eng="act" if half else "dve")
                DMA("sp", dst, stage[0:n, sl, :], [("stage", sl)], (), ("stage", sl), final=True)

        def xkey(c, t0):
            return ("x", c, t0)

        def xkeys(c, t0, n):
            return [("x", c, t0)]

        def rstd_from(psmean, pk, n, parts, rbuf, rk):
            ACT(rbuf[0:parts, 0:n], psmean, AF.Ln, [pk, "epsb"], [rk], bias=epsb[0:parts, 0:1])
            ACT(rbuf[0:parts, 0:n], rbuf[0:parts, 0:n], AF.Exp, [rk], [rk], scale=-0.5)

        def rmsnorm(si, gain, l):
            sq = aalloc([8, 512], BF16)
            rb = aalloc([2, 512], F32)
            for ti, (t0, n) in enumerate(tiles_of(si)):
                p, pk = pb()
                for c in range(8):
                    ACT(sq[:, c, 0:n], xT[:, c, t0:t0 + n], AF.Square, [("x", c, t0)], [("sq", c)])
                for c in range(8):
                    MM(p[:, 0:n], onesv[:, 0, :], sq[:, c, 0:n], c == 0, c == 7, [("sq", c), "onesv"], [pk])
                rk = ("rstd", ti % 2)
                rstd_from(p[:, 0:n], pk, n, 128, rb[:, ti % 2, :], rk)
                for c in range(8):
                    STT(hT[:, c, t0:t0 + n], xT[:, c, t0:t0 + n], gain[:, l, c:c + 1], rb[:, ti % 2, 0:n],
                        ALU.mult, ALU.mult, [("x", c, t0), rk], [("h", t0)])

        def fix_xkeys(si):
            pass

        def ffn(si, l, which):
            areset()
            wg_d = W["ffn%d_wg" % which][l]
            wu_d = W["ffn%d_wu" % which][l]
            wd_d = W["ffn%d_wd" % which][l]
            gain = n1g if which == 1 else n2g
            act = [aalloc([4, NT], BF16) for _ in range(2)]
            sg = aalloc([2, 512], F32)
            rmsnorm(si, gain, l)
            groups = [(g * 512, 4) for g in range(5)] + [(2560, 2)]
            tl = tiles_of(si)
            gslots = {}

            def gateup(j):
                f0, nfc = groups[j]
                sl = j % 2
                Wg, gk = wg_slot()
                Wd, dk = wd_slot()
                wgv = wview(Wg, 0, 8, 512)
                wuv = wview(Wg, 4096, 8, 512)
                wdv = wview(Wd, 0, 4, D)
                gslots[j] = (wdv, dk)
                WDMA(wgv[:, :, 0:nfc * 128], wg_d[:, f0:f0 + nfc * 128].rearrange("(k p) f -> p k f", p=128), gk)
                WDMA(wuv[:, :, 0:nfc * 128], wu_d[:, f0:f0 + nfc * 128].rearrange("(k p) f -> p k f", p=128), gk)
                WDMA(wdv[:, 0:nfc, :], wd_d[f0:f0 + nfc * 128, :].rearrange("(k p) d -> p k d", p=128), dk)
                cnt = 0
                for (t0, n) in tl:
                    for fc in range(nfc):
                        pg, pgk = pb()
                        pu, puk = pb()
                        for k in range(8):
                            MM(pg[:, 0:n], wgv[:, k, fc * 128:(fc + 1) * 128], hT[:, k, t0:t0 + n], k == 0, k == 7,
                               [gk, ("h", t0)], [pgk])
                        for k in range(8):
                            MM(pu[:, 0:n], wuv[:, k, fc * 128:(fc + 1) * 128], hT[:, k, t0:t0 + n], k == 0, k == 7,
                               [gk, ("h", t0)], [puk])
                        s2 = cnt % 2
                        cnt += 1
                        ACT(sg[:, s2, 0:n], pg[:, 0:n], AF.Silu, [pgk], [("sg", s2)])
                        TT(act[sl][:, fc, t0:t0 + n], sg[:, s2, 0:n], pu[:, 0:n], ALU.mult, [("sg", s2), puk],
                           [("act", sl, t0)])

            def down(j):
                f0, nfc = groups[j]
                sl = j % 2
                wdv, dk = gslots[j]
                for (t0, n) in tl:
                    for c in range(8):
                        p, pk = pb()
                        for fc in range(nfc):
                            MM(p[:, 0:n], wdv[:, fc, c * 128:(c + 1) * 128], act[sl][:, fc, t0:t0 + n], fc == 0,
                               fc == nfc - 1, [dk, ("act", sl, t0)], [pk])
                        STT(xT[:, c, t0:t0 + n], p[:, 0:n], 0.5, xT[:, c, t0:t0 + n], ALU.mult, ALU.add,
                            [pk, ("x", c, t0)], [("x", c, t0)])

            for j in range(len(groups) + 1):
                if j < len(groups):
                    gateup(j)
                if j >= 1:
                    down(j - 1)
            if which == 1:
                S.barrier()

        def ret_epilogue(Oin, Okey, n, l, h, hh, sgT, t0, scr, ebank=None):
            Osb, sqb, rb = scr
            CP(Osb[:, 0:n], Oin, [Okey], ["Osb"], eng="act")
            ACT(sqb[:, 0:n], Oin, AF.Square, [Okey], ["sqb"])
            p, pk = pb() if ebank is None else (ps[ebank], ("ps", ebank))
            MM(p[:, 0:n], onesv[:, 2, :], sqb[:, 0:n], True, True, ["sqb", "onesv"], [pk])
            rstd_from(p[:, 0:n], pk, n, 128, rb, "rb")
            STT(Osb[:, 0:n], Osb[:, 0:n], retg[:, l, h:h + 1], rb[:, 0:n], ALU.mult, ALU.mult, ["Osb", "rb"], ["Osb"])
            TT(oT[:, h, t0:t0 + n], Osb[:, 0:n], sgT[:, hh, t0:t0 + n], ALU.mult, ["Osb", ("sgT", t0)], [("o", h, t0)])

        def retention(si, l, pr, last):
            areset()
            win = W["w_in"][l]
            Wg, gk = wg_slot()
            wqk = wview(Wg, 0, 8, 256)
            wv = wview(Wg, 2048, 8, 256)
            wgt = wview(Wg, 4096, 8, 256)
            Kp = aalloc([8, 128], BF16)
            Vt = aalloc([8, 256], BF16)
            qT = aalloc([1, NT], BF16)[:, 0, :]
            kT = aalloc([1, NT], BF16)[:, 0, :]
            qdT = aalloc([1, TP], BF16)[:, 0, :]
            sgT = aalloc([2, NT], BF16)
            AT = [aalloc([4, 128], BF16) for _ in range(2)]
            Sbf = aalloc([1, 128], BF16)[:, 0, :]
            Osb = aalloc([1, 512], F32)[:, 0, :]
            sqb = aalloc([1, 512], BF16)[:, 0, :]
            rb = aalloc([1, 512], F32)[:, 0, :]
            scr = (Osb, sqb, rb)
            wv3 = win.rearrange("(k p) f -> p k f", p=128)
            WDMA(wqk[:, :, 0:128], wv3[:, :, pr * 128:(pr + 1) * 128], gk)
            WDMA(wqk[:, :, 128:256], wv3[:, :, 256 + pr * 128:256 + (pr + 1) * 128], gk)
            WDMA(wv[:], wv3[:, :, 512 + pr * 256:512 + (pr + 1) * 256], gk)
            WDMA(wgt[:], wv3[:, :, 1024 + pr * 256:1024 + (pr + 1) * 256], gk)
            Sf = Sst[:, l, pr, :]
            CP(Sbf, Sf, ["Sst"], ["Sbf"])
            for n in range(8):
                p, pk = pb()
                p2, pk2 = pb()
                for k in range(8):
                    MM(p[:, 0:128], hT[:, k, n * 128:(n + 1) * 128], wqk[:, k, 128:256], k == 0, k == 7,
                       [("h", (n // 4) * 512), gk], [pk])
                for k in range(8):
                    MM(p2[:, 0:256], hT[:, k, n * 128:(n + 1) * 128], wv[:, k, :], k == 0, k == 7,
                       [("h", (n // 4) * 512), gk], [pk2])
                for hh in range(2):
                    TS(Kp[:, n, hh * 64:(hh + 1) * 64], p[:, hh * 64:(hh + 1) * 64], kdec[:, 2 * pr + hh:2 * pr + hh + 1],
                       ALU.mult, [pk, "kdec"], [("Kp", n)])
                CP(Vt[:, n, :], p2[:, 0:256], [pk2], [("Vt", n)], eng="act")
            for (t0, n) in tiles_of(si):
                p, pk = pb()
                for k in range(8):
                    MM(p[:, 0:n], wqk[:, k, 0:128], hT[:, k, t0:t0 + n], k == 0, k == 7, [gk, ("h", t0)], [pk])
                CP(qT[:, t0:t0 + n], p[:, 0:n], [pk], [("qT", t0)], eng="dve")
                if t0 < TP:
                    TT(qdT[:, t0:t0 + n].rearrange("p (a b) -> p a b", a=4), p[:, 0:n].rearrange("p (a b) -> p a b", a=4),
                       qdec[:, pr:pr + 1, :].to_broadcast([128, 4, 128]), ALU.mult, [pk, "qdec"], [("qdT", t0)])
                p, pk = pb()
                for k in range(8):
                    MM(p[:, 0:n], wqk[:, k, 128:256], hT[:, k, t0:t0 + n], k == 0, k == 7, [gk, ("h", t0)], [pk])
                ACT(kT[:, t0:t0 + n], p[:, 0:n], AF.Identity, [pk], [("kT", t0)], scale=0.125)
                if t0 >= TP:
                    for hh in range(2):
                        p, pk = pb()
                        for k in range(8):
                            MM(p[:, 0:n], wgt[:, k, hh * 128:(hh + 1) * 128], hT[:, k, t0:t0 + n], k == 0, k == 7,
                               [gk, ("h", t0)], [pk])
                        ACT(sgT[:, hh, t0:t0 + n], p[:, 0:n], AF.Silu, [pk], [("sgT", t0)])
            Sall = aalloc([8, 128], BF16)
            AT2 = [[AT[0], AT[1]], [aalloc([4, 128], BF16), aalloc([4, 128], BF16)]]
            for n in range(8):
                b = n // 2
                off = (n % 2) * 256
                MM(ps[b][:, off:off + 128], Kp[:, n, :], Vt[:, n, 0:128], True, True, [("Kp", n), ("Vt", n)], [("ps", b)])
                MM(ps[b][:, off + 128:off + 256], Kp[:, n, :], Vt[:, n, 128:256], True, True, [("Kp", n), ("Vt", n)],
                   [("ps", b)])
            for ti in range(2):
                t0 = ti * 512
                for hh in range(2):
                    bank = 4 + 2 * ti + hh
                    lo, hi = 64 * hh, 64 * hh + 64
                    for nn in range(4):
                        c0 = t0 + nn * 128
                        MM(ps[bank][:, nn * 128:(nn + 1) * 128], kT[lo:hi, c0:c0 + 128], qT[lo:hi, c0:c0 + 128], True, True,
                           [("kT", t0), ("qT", t0)], [("ps", bank)])
                    TT(AT2[ti][hh][:], ps[bank][:, :].rearrange("p (a b) -> p a b", a=4),
                       rdm[:, 2 * pr + hh:2 * pr + hh + 1, :].to_broadcast([128, 4, 128]), ALU.mult, [("ps", bank), "rdm"],
                       [("AT", ti, hh)])
            for ti in range(2):
                t0 = ti * 512
                for hh in range(2):
                    bank = 4 + 2 * ti + hh
                    for k in range(8):
                        MM(ps[bank][:, :], wgt[:, k, hh * 128:(hh + 1) * 128], hT[:, k, t0:t0 + 512], k == 0, k == 7,
                           [gk, ("h", t0)], [("ps", bank)])
                    ACT(sgT[:, hh, t0:t0 + 512], ps[bank][:, :], AF.Silu, [("ps", bank)], [("sgT", t0)])
            for n in range(8):
                b = n // 2
                off = (n % 2) * 256
                CP(Sall[:, n, :], Sst[:, l, pr, :], ["Sst"], [("Sall", n)])
                for hh in range(2):
                    lo, hi = 64 * hh, 64 * hh + 64
                    STT(Sst[lo:hi, l, pr, :], Sst[lo:hi, l, pr, :], gam[2 * pr + hh] ** 128,
                        ps[b][lo:hi, off + hh * 128:off + (hh + 1) * 128], ALU.mult, ALU.add, [("ps", b), "Sst"], ["Sst"])
            for ti in range(2):
                t0 = ti * 512
                for hh in range(2):
                    bank = 2 * ti + hh
                    lo, hi = 64 * hh, 64 * hh + 64
                    for nn in range(4):
                        n = ti * 4 + nn
                        c0 = t0 + nn * 128
                        MM(ps[bank][:, nn * 128:(nn + 1) * 128], Vt[:, n, hh * 128:(hh + 1) * 128], AT2[ti][hh][:, nn, :],
                           True, False, [("Vt", n), ("AT", ti, hh)], [("ps", bank)])
                        MM(ps[bank][:, nn * 128:(nn + 1) * 128], Sall[lo:hi, n, :], qdT[lo:hi, c0:c0 + 128], False, True,
                           [("Sall", n), ("qdT", t0)], [("ps", bank)])
                for hh in range(2):
                    bank = 2 * ti + hh
                    ret_epilogue(ps[bank][:, :], ("ps", bank), 512, l, 2 * pr + hh, hh, sgT, t0, scr, ebank=4 + 2 * ti + hh)
            pctr[0] = 0
            if last:
                for hh in range(2):
                    DMA("sp", retp[l, 2 * pr + hh], Sst[64 * hh:64 * hh + 64, l, pr, :], ["Sst"], (), "retp", final=True)
            if si == 0:
                t0 = TP
                S0f = aalloc([NS, 128], F32)
                S0b = aalloc([NS, 128], BF16)
                Ks = aalloc([1, 128], BF16, parts=NS)[:, 0, :]
                Vs = aalloc([1, 256], BF16, parts=NS)[:, 0, :]
                Vbd1 = aalloc([NS, 128], BF16, parts=NS)
                Vbd = [Vbd1, Vbd1]
                vTs = aalloc([2, NS], F32)
                prod = aalloc([1, NS], BF16)[:, 0, :]
                tmp = aalloc([1, NS], F32)[:, 0, :]
                Os = aalloc([1, NS], F32)[:, 0, :]
                Sn = aalloc([2, 512], F32)
                DMA("sp", S0f[:], sret[l, :, 2 * pr:2 * pr + 2].rearrange("b h d v -> (h d) b v"), (), ["S0f"], "S0f")
                CP(S0b[:], S0f[:], ["S0f"], ["S0b"])
                p, pk = pb()
                for k in range(8):
                    MM(p[0:NS, 0:128], hT[:, k, t0:t0 + NS], wqk[:, k, 128:256], k == 0, k == 7, [("h", t0), gk], [pk])
                for k in range(8):
                    MM(p[0:NS, 128:384], hT[:, k, t0:t0 + NS], wv[:, k, :], k == 0, k == 7, [("h", t0), gk], [pk])
                ACT(Ks, p[0:NS, 0:128], AF.Identity, [pk], ["Ks"], scale=0.125)
                CP(Vs, p[0:NS, 128:384], [pk], ["Vs"], eng="act")
                for hh in range(2):
                    p, pk = pb()
                    for k in range(8):
                        MM(p[:, 0:NS], wv[:, k, hh * 128:(hh + 1) * 128], hT[:, k, t0:t0 + NS], k == 0, k == 7,
                           [gk, ("h", t0)], [pk])
                    CP(vTs[:, hh, :], p[:, 0:NS], [pk], [("vTs", hh)], eng="act")
                TT(prod, qT[:, t0:t0 + NS], kT[:, t0:t0 + NS], ALU.mult, [("qT", t0), ("kT", t0)], ["prod"])
                for hh in range(2):
                    lo, hi = 64 * hh, 64 * hh + 64
                    h = 2 * pr + hh
                    pt, ptk = pb()
                    for b in range(NS):
                        MM(pt[:, b:b + 1], S0b[lo:hi, b, :], qT[lo:hi, t0 + b:t0 + b + 1], True, True,
                           ["S0b", ("qT", t0)], [ptk])
                    pq, pqk = pb()
                    MM(pq[:, 0:NS], hsel[:, hh, :], prod, True, True, ["hsel", "prod"], [pqk])
                    TT(tmp, pq[:, 0:NS], vTs[:, hh, :], ALU.mult, [pqk, ("vTs", hh)], ["tmp"])
                    STT(Os, pt[:, 0:NS], gam[h], tmp, ALU.mult, ALU.add, [ptk, "tmp"], ["Os"])
                    ret_epilogue(Os, "Os", NS, l, h, hh, sgT, t0, scr)
                    TT(Vbd[hh][:], Vs[:, hh * 128:(hh + 1) * 128].unsqueeze(1).to_broadcast([NS, NS, 128]),
                       eyeN[:, :].unsqueeze(2).to_broadcast([NS, NS, 128]), ALU.mult, ["Vs", "eyeN"], ["Vbd"])
                    for q in range(NS // 4):
                        pn, pnk = pb()
                        MM(pn[:, :], Ks[:, :], Vbd[hh][:, 4 * q:4 * q + 4, :].rearrange("p a b -> p (a b)"), True, True, ["Ks", "Vbd"], [pnk])
                        s2 = q % 2
                        STT(Sn[lo:hi, s2, :], S0f[lo:hi, 4 * q:4 * q + 4, :].rearrange("p a b -> p (a b)"), gam[h], pn[lo:hi, :], ALU.mult, ALU.add,
                            [pnk, "S0f"], [("Sn", s2)])
                        DMA("sp", rets[l, 4 * q:4 * q + 4, h].rearrange("b d v -> d b v"),
                            Sn[lo:hi, s2, :].rearrange("p (b v) -> p b v", b=4), [("Sn", s2)], (), ("Sn", s2), final=True)
            S.barrier()

        def attention(si, l, last):
            areset()
            wv3 = W["w_in"][l].rearrange("(k p) f -> p k f", p=128)
            Wg, gk = wg_slot()
            wa = wview(Wg, 0, 8, 512)
            Va = aalloc([9, 128], BF16)
            qn = [aalloc([1, NT], BF16, parts=64)[:, 0, :] for _ in range(4)]
            kn = [aalloc([1, 128 + NT], BF16, parts=64)[:, 0, :] for _ in range(2)]
            qsb = aalloc([1, 512], F32, parts=64)[:, 0, :]
            sqb = aalloc([1, 512], BF16, parts=64)[:, 0, :]
            rb = aalloc([1, 512], F32, parts=64)[:, 0, :]
            knf = aalloc([2, 128], F32, parts=64)
            vlast = aalloc([1, 128], F32)[:, 0, :]
            E = [aalloc([1, 512], BF16)[:, 0, :] for _ in range(4)]
            PT = [aalloc([1, 512], BF16)[:, 0, :] for _ in range(4)]
            den = aalloc([1, 512], F32, parts=64)[:, 0, :]
            WDMA(wa[:], wv3[:, :, 1536:2048], gk)
            CP(Va[:, 0, :], Vh[:, l, :], ["Vh"], [("Va", 0)])
            for g in range(2):
                CP(kn[g][:, 0:128], kTh[:, l, g, :], ["kTh"], [("kn", g, -1)])
            for n in range(8):
                p, pk = pb()
                for k in range(8):
                    MM(p[:, 0:128], hT[:, k, n * 128:(n + 1) * 128], wa[:, k, 384:512], k == 0, k == 7,
                       [("h", (n // 4) * 512), gk], [pk])
                CP(Va[:, n + 1, :], p[:, 0:128], [pk], [("Va", n + 1)], eng="act")
                if n == 7:
                    if last:
                        CP(vlast, p[:, 0:128], [pk], ["vlast"], eng="act")
                        DMA("sp", vwp[l], vlast, ["vlast"], (), "vwp", final=True)
                    CP(Vh[:, l, :], p[:, 0:128], [pk], ["Vh"], eng="act")
            sqb2 = [sqb, aalloc([1, 512], BF16, parts=64)[:, 0, :]]
            rb2 = [rb, aalloc([1, 512], F32, parts=64)[:, 0, :]]
            items = []
            for (t0, n) in tiles_of(si):
                for h in range(4):
                    items.append(("q", h, t0, n))
                for g in range(2):
                    items.append(("k", g, t0, n))
            state = {}

            def b_proj(i):
                kind, idx, t0, n = items[i]
                wcol = idx * 64 if kind == "q" else 256 + idx * 64
                p, pk = pb()
                for k in range(8):
                    MM(p[0:64, 0:n], wa[:, k, wcol:wcol + 64], hT[:, k, t0:t0 + n], k == 0, k == 7, [gk, ("h", t0)], [pk])
                state[i] = (p, pk)

            def b_finish(i):
                kind, idx, t0, n = items[i]
                p, pk = state.pop(i)
                s2 = i % 2
                ACT(sqb2[s2][:, 0:n], p[0:64, 0:n], AF.Square, [pk], [("sqbA", s2)])
                p2, pk2 = pb()
                MM(p2[0:64, 0:n], onesv[0:64, 1, 0:64], sqb2[s2][:, 0:n], True, True, [("sqbA", s2), "onesv"], [pk2])
                rstd_from(p2[0:64, 0:n], pk2, n, 64, rb2[s2], ("rbA", s2))
                if kind == "q":
                    STT(qn[idx][:, t0:t0 + n], p[0:64, 0:n], qg8[:, l:l + 1], rb2[s2][:, 0:n], ALU.mult, ALU.mult,
                        [pk, ("rbA", s2)], [("qn", idx, t0)])
                else:
                    g = idx
                    STT(kn[g][:, 128 + t0:128 + t0 + n], p[0:64, 0:n], kgs[:, l:l + 1], rb2[s2][:, 0:n], ALU.mult, ALU.mult,
                        [pk, ("rbA", s2)], [("kn", g, t0)])
                    if t0 == 512:
                        STT(knf[:, g, :], p[0:64, 384:512], kgs[:, l:l + 1], rb2[s2][:, 384:512], ALU.mult, ALU.mult,
                            [pk, ("rbA", s2)], [("knf", g)])
                        if last:
                            DMA("sp", kwp[l, :, g * 64:(g + 1) * 64].rearrange("t d -> d t"), knf[:, g, :], [("knf", g)],
                                (), "kwp", final=True, slow=True)
                        CP(kTh[:, l, g, :], kn[g][:, 128 + 896:128 + 1024], [("kn", g, 512)], ["kTh"])

            b_proj(0)
            for i in range(len(items)):
                if i + 1 < len(items):
                    b_proj(i + 1)
                b_finish(i)
            lnd = qsb
            it = 0
            for g in range(2):
                for ti in range(2):
                    t0 = ti * 512
                    base = 4 * (it % 2)
                    oth = 4 - base
                    it += 1
                    pO = [(ps[base + 0], ("ps", base + 0)), (ps[base + 1], ("ps", base + 1))]
                    pD = [(ps[base + 2], ("ps", base + 2)), (ps[base + 3], ("ps", base + 3))]
                    for nn in range(4):
                        n = ti * 4 + nn
                        c0 = n * 128
                        pS, pSk = ps[oth + nn], ("ps", oth + nn)
                        prevk = ("kn", g, ((c0 - 128) // 512) * 512) if c0 >= 128 else ("kn", g, -1)
                        for hh in range(2):
                            h = 2 * g + hh
                            MM(pS[:, hh * 256:hh * 256 + 128], kn[g][:, c0:c0 + 128], qn[h][:, c0:c0 + 128], True, True,
                               [prevk, ("qn", h, t0)], [pSk])
                            MM(pS[:, hh * 256 + 128:hh * 256 + 256], kn[g][:, 128 + c0:256 + c0], qn[h][:, c0:c0 + 128],
                               True, True, [("kn", g, t0), ("qn", h, t0)], [pSk])
                    for nn in range(4):
                        pS, pSk = ps[oth + nn], ("ps", oth + nn)
                        ACT(E[nn], pS[:, :], AF.Exp, [pSk], [("E", nn)])
                        TT(PT[nn], E[nn], amask[:, g, :], ALU.mult, [("E", nn), "amask"], [("PT", nn)])
                    for nn in range(4):
                        n = ti * 4 + nn
                        skip_prev = (si == 0 and n == 0)
                        s2 = nn
                        for hh in range(2):
                            cs = slice(nn * 128, (nn + 1) * 128)
                            prevP = PT[s2][:, hh * 256:hh * 256 + 128]
                            curP = PT[s2][:, hh * 256 + 128:hh * 256 + 256]
                            if not skip_prev:
                                MM(pO[hh][0][0:64, cs], Va[:, n, g * 64:(g + 1) * 64], prevP, True, False,
                                   [("Va", n), ("PT", s2)], [pO[hh][1]])
                            MM(pO[hh][0][0:64, cs], Va[:, n + 1, g * 64:(g + 1) * 64], curP, skip_prev, True,
                               [("Va", n + 1), ("PT", s2)], [pO[hh][1]])
                            if not skip_prev:
                                MM(pD[hh][0][0:64, cs], onesv[:, 4, 0:64], prevP, True, False, [("PT", s2), "onesv"],
                                   [pD[hh][1]])
                            MM(pD[hh][0][0:64, cs], onesv[:, 4, 0:64], curP, skip_prev, True, [("PT", s2), "onesv"],
                               [pD[hh][1]])
                    for hh in range(2):
                        h = 2 * g + hh
                        ACT(lnd, pD[hh][0][0:64, :], AF.Ln, [pD[hh][1], "esink"], ["qsb"],
                            bias=esink[:, l * 4 + h:l * 4 + h + 1])
                        ACT(den, lnd, AF.Exp, ["qsb"], ["den"], scale=-1.0)
                        TT(oT[0:64, 4 + h, t0:t0 + 512], pO[hh][0][0:64, :], den[:, :], ALU.mult, [pO[hh][1], "den"],
                           [("o", 4 + h, t0)])
            pctr[0] = 0
            if si == 0:
                t0 = TP
                HB = NS // 2
                kcf = aalloc([HB, 128], F32)
                vcb = aalloc([NS, 128], BF16)
                kcT = aalloc([NS, 128], BF16, parts=64)
                vnT = aalloc([2, NS], F32, parts=64)
                knS = aalloc([2, NS], F32, parts=64)
                Es = aalloc([1, 4 * NS], F32)[:, 0, :]
                Ps = aalloc([4, NS], BF16)
                prod = aalloc([1, NS], BF16, parts=64)[:, 0, :]
                pn = aalloc([4, NS], F32, parts=64)
                num = aalloc([4, NS], F32, parts=64)
                dn = aalloc([4, NS], F32, parts=64)
                DMA("pool", vcb[:], cv[l].rearrange("b k f -> k b f"), (), ["vcb"], "vcb")
                DMA("sp", kws[l, :, 0:127, :], ck[l, :, 1:128, :], (), (), "kws", final=True)
                DMA("sp", vws[l, :, 0:127, :], cv[l, :, 1:128, :], (), (), "vws", final=True)
                for g in range(2):
                    p, pk = pb()
                    for k in range(8):
                        MM(p[0:64, 0:NS], wa[:, k, 384 + g * 64:384 + (g + 1) * 64], hT[:, k, t0:t0 + NS], k == 0, k == 7,
                           [gk, ("h", t0)], [pk])
                    CP(vnT[:, g, :], p[0:64, 0:NS], [pk], [("vnT", g)])
                    DMA("sp", vws[l, :, 127, g * 64:(g + 1) * 64].rearrange("b d -> d b"), vnT[:, g, :], [("vnT", g)], (),
                        "vws2", final=True, slow=True)
                    p, pk = pb()
                    for k in range(8):
                        MM(p[0:64, 0:NS], wa[:, k, 256 + g * 64:256 + (g + 1) * 64], hT[:, k, t0:t0 + NS], k == 0, k == 7,
                           [gk, ("h", t0)], [pk])
                    CP(qsb[:, 0:NS], p[0:64, 0:NS], [pk], ["qsb"], eng="act")
                    ACT(sqb[:, 0:NS], p[0:64, 0:NS], AF.Square, [pk], ["sqbA"])
                    p2, pk2 = pb()
                    MM(p2[0:64, 0:NS], onesv[0:64, 1, 0:64], sqb[:, 0:NS], True, True, ["sqbA", "onesv"], [pk2])
                    rstd_from(p2[0:64, 0:NS], pk2, NS, 64, rb, "rbA")
                    STT(knS[:, g, :], qsb[:, 0:NS], kgs[:, l:l + 1], rb[:, 0:NS], ALU.mult, ALU.mult, ["qsb", "rbA"],
                        [("knS", g)])
                    DMA("sp", kws[l, :, 127, g * 64:(g + 1) * 64].rearrange("b d -> d b"), knS[:, g, :], [("knS", g)], (),
                        "kws2", final=True, slow=True)
                pS, pSk = ps[7], ("ps", 7)
                for g in range(2):
                    for hb in range(2):
                        DMA("sp", kcf[:], ck[l, hb * HB:(hb + 1) * HB].rearrange("b k f -> k b f"), (), ["kcf"], "kcf")
                        for q in range(HB // 4):
                            p, pk = ps[q % 4], ("ps", q % 4)
                            for bb in range(4):
                                TR(p[0:64, bb * 128:(bb + 1) * 128], kcf[:, 4 * q + bb, g * 64:(g + 1) * 64], ident[:, :],
                                   ["kcf", "ident"], [pk])
                            b0 = hb * HB + 4 * q
                            CP(kcT[:, b0:b0 + 4, :], p[0:64, :].rearrange("p (a b) -> p a b", a=4), [pk],
                               ["kcT"], eng="act" if q % 2 else "dve")
                    for hh in range(2):
                        h = 2 * g + hh
                        for b in range(NS):
                            MM(pS[:, h * NS + b:h * NS + b + 1], kcT[:, b, :], qn[h][:, t0 + b:t0 + b + 1], True, True,
                               ["kcT", ("qn", h, t0)], [pSk])
                pctr[0] = 0
                ACT(Es, pS[:, 0:4 * NS], AF.Exp, [pSk], ["Es"])
                TT(Ps[:], Es.rearrange("p (h b) -> p h b", h=4), smask[:, :].unsqueeze(2).to_broadcast([128, 4, NS]),
                   ALU.mult, ["Es", "smask"], ["Ps"])
                pO2, pO2k = pb()
                for h in range(4):
                    g = h // 2
                    for b in range(NS):
                        MM(pO2[0:64, h * NS + b:h * NS + b + 1], vcb[:, b, g * 64:(g + 1) * 64], Ps[:, h, b:b + 1], True, True,
                           ["vcb", "Ps"], [pO2k])
                pD2, pD2k = pb()
                MM(pD2[0:64, 0:4 * NS], onesv[:, 4, 0:64], Ps[:].rearrange("p h b -> p (h b)"), True, True,
                   ["Ps", "onesv"], [pD2k])
                pN, pNk = pb()
                for h in range(4):
                    g = h // 2
                    TT(prod, qn[h][:, t0:t0 + NS], kn[g][:, 128 + t0:128 + t0 + NS], ALU.mult,
                       [("qn", h, t0), ("kn", g, t0)], ["prodA"])
                    MM(pN[0:64, h * NS:(h + 1) * NS], onesv[0:64, 4, 0:64], prod, True, True, ["prodA", "onesv"], [pNk])
                ACT(pn[:].rearrange("p h b -> p (h b)"), pN[0:64, 0:4 * NS], AF.Exp, [pNk], ["pn"])
                for h in range(4):
                    g = h // 2
                    TT(num[:, h, :], pn[:, h, :], vnT[:, g, :], ALU.mult, ["pn", ("vnT", g)], ["num"])
                TT(num[:].rearrange("p h b -> p (h b)"), num[:].rearrange("p h b -> p (h b)"), pO2[0:64, 0:4 * NS], ALU.add,
                   ["num", pO2k], ["num"])
                TT(dn[:].rearrange("p h b -> p (h b)"), pn[:].rearrange("p h b -> p (h b)"), pD2[0:64, 0:4 * NS], ALU.add,
                   ["pn", pD2k], ["dn"])
                for h in range(4):
                    TS(dn[:, h, :], dn[:, h, :], esink[:, l * 4 + h:l * 4 + h + 1], ALU.add, ["dn", "esink"], ["dn"])
                RCP(dn[:].rearrange("p h b -> p (h b)"), dn[:].rearrange("p h b -> p (h b)"), ["dn"], ["dn"])
                for h in range(4):
                    TT(oT[0:64, 4 + h, t0:t0 + NS], num[:, h, :], dn[:, h, :], ALU.mult, ["num", "dn"], [("o", 4 + h, t0)])
            S.barrier()

        def conv(si, l, last):
            areset()
            wv3 = W["w_in"][l].rearrange("(k p) f -> p k f", p=128)
            Wg, gk = wg_slot()
            wc = wview(Wg, 0, 8, 512)
            wpw = wview(Wg, 4096, 2, 256)
            Dg = aalloc([2, 31, 128], BF16)
            uT = aalloc([2, 30 + TP], BF16)
            sig = aalloc([2, 512], F32)
            ulast = aalloc([2, 30], F32)
            yb = aalloc([2, 512], F32)
            ybf = aalloc([2, 512], BF16)
            ysq = aalloc([2, 512], BF16)
            msb = aalloc([1, 512], F32)[:, 0, :]
            var = aalloc([1, 512], F32)[:, 0, :]
            rb = aalloc([1, 512], F32)[:, 0, :]
            dd = aalloc([2, 512], F32)
            ysl = aalloc([2, 512], BF16)
            WDMA(wc[:], wv3[:, :, 2048:2560], gk)
            WDMA(wpw[:], W["conv_pw"][l].rearrange("(k p) o -> p k o", p=128), gk)
            if si == 0:
                wcv = wview(Wg, 4608, 1, 256, parts=30)[:, 0, :]
                WDMA(wcv, W["conv_w"][l, 0:30, :], gk)
            for c in range(2):
                for j in range(31):
                    TS(Dg[:, c, j, :], identb[:, :], convw[:, l, c, j:j + 1], ALU.mult, ["identb", "convw"], ["Dg"])
                CP(uT[:, c, 0:30], uh[:, l, c, :], ["uh"], [("uT", c, -1)])

            def ln_epilogue(n, t0):
                pm, pmk = pb()
                pq, pqk = pb()
                for c in range(2):
                    MM(pm[:, 0:n], onesv[:, 3, :], ybf[:, c, 0:n], c == 0, c == 1, [("ybf", c), "onesv"], [pmk])
                for c in range(2):
                    MM(pq[:, 0:n], onesv[:, 3, :], ysq[:, c, 0:n], c == 0, c == 1, [("ysq", c), "onesv"], [pqk])
                CP(msb[:, 0:n], pm[:, 0:n], [pmk], ["msb"], eng="act")
                STT(var[:, 0:n], msb[:, 0:n], -1.0, msb[:, 0:n], ALU.mult, ALU.mult, ["msb"], ["var"])
                TT(var[:, 0:n], var[:, 0:n], pq[:, 0:n], ALU.add, ["var", pqk], ["var"])
                ACT(rb[:, 0:n], var[:, 0:n], AF.Ln, ["var", "epsb"], ["rbC"], bias=epsb[:, 0:1])
                ACT(rb[:, 0:n], rb[:, 0:n], AF.Exp, ["rbC"], ["rbC"], scale=-0.5)
                for c in range(2):
                    TT(dd[:, c, 0:n], yb[:, c, 0:n], msb[:, 0:n], ALU.subtract, [("yb", c), "msb"], [("dd", c)])
                    TT(dd[:, c, 0:n], dd[:, c, 0:n], rb[:, 0:n], ALU.mult, [("dd", c), "rbC"], [("dd", c)])
                    ACT(ysl[:, c, 0:n], dd[:, c, 0:n], AF.Silu, [("dd", c), "lng", "lnb"], [("ysl", c)],
                        bias=lnb[:, l, c:c + 1], scale=lng[:, l, c:c + 1])
                for oc in range(2):
                    p, pk = pb()
                    for c in range(2):
                        MM(p[:, 0:n], wpw[:, c, oc * 128:(oc + 1) * 128], ysl[:, c, 0:n], c == 0, c == 1,
                           [gk, ("ysl", c)], [pk])
                    CP(oT[:, 8 + oc, t0:t0 + n], p[:, 0:n], [pk], [("o", 8 + oc, t0)], eng="act" if oc else "dve")

            uS = aalloc([2, NS], F32)
            for (t0, n) in tiles_of(si):
                for c in range(2):
                    pa, pak = pb()
                    pg, pgk = pb()
                    for k in range(8):
                        MM(pa[:, 0:n], wc[:, k, c * 128:(c + 1) * 128], hT[:, k, t0:t0 + n], k == 0, k == 7,
                           [gk, ("h", t0)], [pak])
                    for k in range(8):
                        MM(pg[:, 0:n], wc[:, k, 256 + c * 128:256 + (c + 1) * 128], hT[:, k, t0:t0 + n], k == 0, k == 7,
                           [gk, ("h", t0)], [pgk])
                    ACT(sig[:, c, 0:n], pg[:, 0:n], AF.Sigmoid, [pgk], [("sig", c)])
                    if t0 < TP:
                        TT(uT[:, c, 30 + t0:30 + t0 + n], pa[:, 0:n], sig[:, c, 0:n], ALU.mult, [pak, ("sig", c)],
                           [("uT", c, t0)])
                        if t0 == 512:
                            TT(ulast[:, c, :], pa[:, 482:512], sig[:, c, 482:512], ALU.mult, [pak, ("sig", c)],
                               [("ulast", c)])
                            if last:
                                DMA("sp", cvp[l, :, c * 128:(c + 1) * 128].rearrange("t p -> p t"), ulast[:, c, :],
                                    [("ulast", c)], (), "cvp", final=True, slow=True)
                            CP(uh[:, l, c, :], uT[:, c, TP:TP + 30], [("uT", c, 512)], ["uh"])
                    else:
                        TT(uS[:, c, :], pa[:, 0:n], sig[:, c, 0:n], ALU.mult, [pak, ("sig", c)], [("uS", c)])
            wst = wout_prepare(l)
            taps = {}
            for ti in range(2):
                t0 = ti * 512
                for c in range(2):
                    p, pk = pb()
                    rk = [("uT", c, t0), ("uT", c, t0 - 512 if t0 else -1), "Dg"]
                    for j in range(31):
                        MM(p[:, :], Dg[:, c, j, :], uT[:, c, t0 + j:t0 + j + 512], j == 0, j == 30, rk, [pk])
                    taps[(ti, c)] = (p, pk)
            def evac(ti):
                for c in range(2):
                    p, pk = taps[(ti, c)]
                    ACT(yb[:, c, :], p[:, :], AF.Identity, [pk, "convb"], [("yb", c)], bias=convb[:, l, c:c + 1])
                    ACT(ysq[:, c, :], p[:, :], AF.Square, [pk, "convb"], [("ysq", c)], bias=convb[:, l, c:c + 1])
                    CP(ybf[:, c, :], yb[:, c, :], [("yb", c)], [("ybf", c)])

            evac(0)
            ln_epilogue(512, 0)
            evac(1)
            wout_tile(wst, 0, 512)
            ln_epilogue(512, 512)
            if si == 0:
                t0 = TP
                cbb = aalloc([NS, 256], BF16, parts=30)
                msk = aalloc([4, 128], F32)
                y0 = aalloc([2, NS], F32)
                DMA("pool", cbb[:], sconv[l].rearrange("b j c -> j b c"), (), ["cbb"], "cbb")
                DMA("sp", cvs[l, :, 0:29, :], sconv[l, :, 1:30, :], (), (), "cvs", final=True)
                for c in range(2):
                    DMA("sp", cvs[l, :, 29, c * 128:(c + 1) * 128].rearrange("b p -> p b"), uS[:, c, :], [("uS", c)], (),
                        "cvs2", final=True, slow=True)
                    for q in range(NS // 4):
                        p, pk = pb()
                        MM(p[:, :], wcv[:, c * 128:(c + 1) * 128], cbb[:, 4 * q:4 * q + 4, c * 128:(c + 1) * 128], True, True,
                           [gk, "cbb"], [pk])
                        TT(msk[:], p[:, :].rearrange("p (a b) -> p a b", a=4),
                           ident[:, :].unsqueeze(1).to_broadcast([128, 4, 128]), ALU.mult, [pk, "ident"], ["msk"])
                        S.op("dve", (lambda o_, i_: (lambda e: e.reduce_sum(o_, i_, axis=AX.X)))(y0[:, c, 4 * q:4 * q + 4],
                                                                                                  msk[:]),
                             ["msk"], [("y0", c)])
                    STT(yb[:, c, 0:NS], uS[:, c, :], convw[:, l, c, 30:31], y0[:, c, :], ALU.mult, ALU.add,
                        [("uS", c), ("y0", c), "convw"], [("yb", c)])
                    TS(yb[:, c, 0:NS], yb[:, c, 0:NS], convb[:, l, c:c + 1], ALU.add, [("yb", c), "convb"], [("yb", c)])
                    CP(ybf[:, c, 0:NS], yb[:, c, 0:NS], [("yb", c)], [("ybf", c)])
                    ACT(ysq[:, c, 0:NS], yb[:, c, 0:NS], AF.Square, [("yb", c)], [("ysq", c)])
                ln_epilogue(NS, t0)
            S.barrier()
            wout_tile(wst, 512, 512)
            if si == 0:
                wout_tile(wst, TP, NS)

        def wout_prepare(l):
            Wg, gk = wg_slot()
            Wd, dk = wd_slot()
            wo8 = wview(Wg, 0, 8, D)
            wo2 = wview(Wd, 0, 2, D)
            wod = W["w_out"][l]
            WDMA(wo8[:, 0:4, :], wod[0:512, :].rearrange("(k p) d -> p k d", p=128), gk)
            WDMA(wo8[0:64, 4:8, :], wod[512:768, :].rearrange("(k p) d -> p k d", p=64), gk)
            WDMA(wo2[:, :, :], wod[768:1024, :].rearrange("(k p) d -> p k d", p=128), dk)
            return wo8, wo2, gk, dk

        def wout_tile(wst, t0, n):
            wo8, wo2, gk, dk = wst
            for c in range(8):
                p, pk = pb()
                for k in range(10):
                    kp = 64 if 4 <= k < 8 else 128
                    wsrc = wo8[0:kp, k, c * 128:(c + 1) * 128] if k < 8 else wo2[:, k - 8, c * 128:(c + 1) * 128]
                    MM(p[:, 0:n], wsrc, oT[0:kp, k, t0:t0 + n], k == 0, k == 9, [gk, dk, ("o", k, t0)], [pk])
                TT(xT[:, c, t0:t0 + n], xT[:, c, t0:t0 + n], p[:, 0:n], ALU.add, [pk, ("x", c, t0)], [("x", c, t0)])

        PH = DBG.get("phases")

        def on(name):
            return PH is None or name in PH

        for si in range(DBG.get("nseg", NSEG)):
            last = si == DBG.get("nseg", NSEG) - 1
            load_x(si)
            for l in range(DBG.get("depth", DEPTH)):
                if on("ffn1"):
                    ffn(si, l, 1)
                if on("mixnorm"):
                    areset()
                    rmsnorm(si, mng, l)
                    S.barrier()
                if on("ret"):
                    for pr in range(2):
                        retention(si, l, pr, last)
                if on("att"):
                    attention(si, l, last)
                if on("conv"):
                    conv(si, l, last)
                if on("ffn2"):
                    ffn(si, l, 2)
            store_y(si)
        S.emit()
        print("sched stats", S.stats, flush=True)
    return nc


_CACHE = {}


def kernel(**inputs):
    f32 = lambda a: np.ascontiguousarray(np.asarray(a), dtype=np.float32)
    inp = {k: f32(v) for k, v in inputs.items()}
    if "nc" not in _CACHE:
        _CACHE["nc"] = build()
    nc = _CACHE["nc"]
    ctab, _ = const_tables()
    wnames = ["ffn1_norm", "ffn1_wg", "ffn1_wu", "ffn1_wd", "mix_norm", "w_in", "ret_norm_g", "q_norm_g", "k_norm_g",
              "sinks", "conv_w", "conv_b", "conv_ln_g", "conv_ln_b", "conv_pw", "w_out", "ffn2_norm", "ffn2_wg",
              "ffn2_wu", "ffn2_wd"]
    in_maps = []
    zx = np.zeros((NSEG * TP, D), np.float32)
    for core in range(NCORES):
        m = {k: inp[k] for k in wnames}
        m.update(ctab)
        if core in ACT_CORES:
            c = ACT_CORES.index(core)
            sl = slice(c * NS, (c + 1) * NS)
            m["xp"] = inp["x_prompt"][c]
            m["xs"] = np.ascontiguousarray(inp["x_sample"][sl, 0, :])
            m["sret"] = np.ascontiguousarray(inp["state_ret"][:, sl])
            m["ck"] = np.ascontiguousarray(inp["cache_k_win"][:, sl].reshape(DEPTH, NS, 128, 128))
            m["cv"] = np.ascontiguousarray(inp["cache_v_win"][:, sl].reshape(DEPTH, NS, 128, 128))
            m["sconv"] = np.ascontiguousarray(inp["state_conv"][:, sl])
        else:
            m["xp"] = zx
            m["xs"] = np.zeros((NS, D), np.float32)
            m["sret"] = np.zeros((DEPTH, NS, 4, 64, 128), np.float32)
            m["ck"] = np.zeros((DEPTH, NS, 128, 128), np.float32)
            m["cv"] = np.zeros((DEPTH, NS, 128, 128), np.float32)
            m["sconv"] = np.zeros((DEPTH, NS, 30, 256), np.float32)
        in_maps.append(m)
    res = run_bass_kernel_spmd(nc, in_maps, core_ids=list(range(NCORES)))
    R = res.results
    A = ACT_CORES
    y_p = np.stack([R[c]["yp"] for c in A]).reshape(BATCH, SEQ, D)
    y_s = np.concatenate([R[c]["ys"] for c in A]).reshape(DECB, 1, D)
    ret_p = np.stack([R[c]["retp"] for c in A], axis=1)
    ret_s = np.concatenate([R[c]["rets"] for c in A], axis=1)
    kw_p = np.stack([R[c]["kwp"] for c in A], axis=1).reshape(DEPTH, BATCH, 128, 2, 64)
    kw_s = np.concatenate([R[c]["kws"] for c in A], axis=1).reshape(DEPTH, DECB, 128, 2, 64)
    vw_p = np.stack([R[c]["vwp"] for c in A], axis=1).reshape(DEPTH, BATCH, 128, 2, 64)
    vw_s = np.concatenate([R[c]["vws"] for c in A], axis=1).reshape(DEPTH, DECB, 128, 2, 64)
    cv_p = np.stack([R[c]["cvp"] for c in A], axis=1)
    cv_s = np.concatenate([R[c]["cvs"] for c in A], axis=1)
    outs = (y_p, y_s, ret_p, ret_s, kw_p, kw_s, vw_p, vw_s, cv_p, cv_s)
    return tuple(np.ascontiguousarray(o, dtype=np.float32) for o in outs)
```

```python
import numpy as np
from contextlib import ExitStack
import concourse.bass as bass
import concourse.mybir as mybir
from concourse.bass_utils import run_bass_kernel_spmd

F32 = mybir.dt.float32
BF16 = mybir.dt.bfloat16
ALU = mybir.AluOpType
AF = mybir.ActivationFunctionType
AX = mybir.AxisListType

D = 1024
FF = 2816
DEPTH = 2
SEQ = 4096
BATCH = 4
DECB = 128
IN_DIM = 2560
EPS = 1e-6
NCORES = 8
ACTIVE = 4
NSEG = 4
TP = 1024
NS = DECB // ACTIVE
NT = TP + NS
SEM_LIMIT = 30000
ACT_CORES = [0, 2, 4, 6]
DBG = {}


class Sched:
    ENGS = ("pe", "act", "dve", "pool", "sp")

    def __init__(self, nc, es):
        self.nc = nc
        self.es = es
        self.ops = []
        self.last_write = {}
        self.readers = {}
        self.out_dma_ops = []
        self.bar_deps = set()
        self.bar_pending = set()

    def op(self, eng, fn, reads=(), writes=(), dma=False, semkey=None, final=False, nobar=False):
        idx = len(self.ops)
        raw = set()
        other = set()
        for k in reads:
            w = self.last_write.get(k)
            if w is not None:
                raw.add(w)
        for k in writes:
            w = self.last_write.get(k)
            if w is not None:
                other.add(w)
            for r in self.readers.get(k, {}).values():
                other.add(r)
        for k in writes:
            self.last_write[k] = idx
            self.readers[k] = {}
        rk = ("dma", semkey) if dma else eng
        for k in reads:
            self.readers.setdefault(k, {})[rk] = idx
        if eng in self.bar_pending and not nobar and eng != "pe":
            if any(not self.persistent(k) for k in list(reads) + list(writes)):
                other |= self.bar_deps
                self.bar_pending.discard(eng)
        deps = set()
        for d in raw | other:
            if d == idx:
                continue
            p = self.ops[d]
            if (not p["dma"]) and p["eng"] == eng:
                if eng == "pe" or d not in raw:
                    continue
            deps.add(d)
        self.ops.append(dict(eng=eng, fn=fn, deps=deps, dma=dma, semkey=semkey, sig=False, ev=None))
        if final:
            self.out_dma_ops.append(idx)
        return idx

    def barrier(self):
        last = {}
        for i, o in enumerate(self.ops):
            if o["fn"] is None:
                continue
            if o["dma"]:
                last[("dma", o["semkey"])] = i
            else:
                last[o["eng"]] = i
        self.bar_deps = set(last.values())
        self.bar_pending = set(self.ENGS)
        self.last_write = {k: v for k, v in self.last_write.items() if self.persistent(k)}
        self.readers = {k: v for k, v in self.readers.items() if self.persistent(k)}

    PERSIST = frozenset(("x", "h", "o", "ps", "WG", "WD", "stage", "Sst", "kTh", "Vh", "uh", "ident", "identb", "onesv",
                         "epsb", "rdm", "qdec", "kdec", "g128", "g1", "amask", "smask", "eyeN", "hsel", "n1g", "mng",
                         "n2g", "retg", "qg8", "kgs", "esink", "convw", "convb", "lng", "lnb", "oT"))

    def persistent(self, k):
        return (k[0] if isinstance(k, tuple) else k) in self.PERSIST

    def emit(self):
        nc = self.nc
        ops = self.ops
        if self.out_dma_ops:
            ops.append(dict(eng="sp", fn=None, deps=set(self.out_dma_ops), dma=False, semkey=None,
                            sig=False, ev=None))
        for o in ops:
            for d in o["deps"]:
                ops[d]["sig"] = True
        eng_sems = {e: [] for e in self.ENGS}
        eng_cnt = {e: 0 for e in self.ENGS}
        dma_sems = {}
        dma_cnt = {}
        nsem = [0]

        def new_sem():
            nsem[0] += 1
            return self.es.enter_context(nc.semaphore("s%d" % nsem[0]))

        for o in ops:
            if o["fn"] is None:
                continue
            if o["dma"]:
                k = o["semkey"]
                if k not in dma_sems or dma_cnt[k] + 16 >= SEM_LIMIT:
                    dma_sems[k] = new_sem()
                    dma_cnt[k] = 0
                dma_cnt[k] += 16
                o["ev"] = (dma_sems[k], dma_cnt[k])
            elif o["sig"]:
                e = o["eng"]
                if not eng_sems[e] or eng_cnt[e] >= SEM_LIMIT:
                    eng_sems[e].append(new_sem())
                    eng_cnt[e] = 0
                eng_cnt[e] += 1
                o["ev"] = (eng_sems[e][-1], eng_cnt[e])
        nwaits = [0]

        def run_engine(e, eng):
            seen = {}
            for o in ops:
                if o["eng"] != e:
                    continue
                need = {}
                for d in o["deps"]:
                    sem, val = ops[d]["ev"]
                    key = id(sem)
                    if seen.get(key, 0) >= val:
                        continue
                    if key not in need or need[key][1] < val:
                        need[key] = (sem, val)
                for key, (sem, val) in need.items():
                    eng.wait_ge(sem, val)
                    seen[key] = val
                    nwaits[0] += 1
                if o["fn"] is None:
                    continue
                ins = o["fn"](eng)
                if o["ev"] is not None:
                    ins.then_inc(o["ev"][0], 16 if o["dma"] else 1)

        with nc.Block() as block:
            @block.tensor
            def _(eng):
                run_engine("pe", eng)

            @block.scalar
            def _(eng):
                run_engine("act", eng)

            @block.vector
            def _(eng):
                run_engine("dve", eng)

            @block.gpsimd
            def _(eng):
                run_engine("pool", eng)

            @block.sync
            def _(eng):
                run_engine("sp", eng)
        self.stats = dict(n_ops=len(ops), n_waits=nwaits[0], n_sems=nsem[0])


def const_tables():
    lg = np.log1p(-np.exp2(-5.0 - np.arange(4, dtype=np.float64)))
    slopes = np.exp2(-8.0 * (np.arange(4, dtype=np.float64) + 1.0) / 4.0)
    j = np.arange(128)[:, None].astype(np.float64)
    i = np.arange(128)[None, :].astype(np.float64)
    t = {}
    rdm = np.zeros((128, 4, 128), np.float64)
    for h in range(4):
        rdm[:, h, :] = np.where(i >= j, np.exp(lg[h] * np.maximum(i - j, 0.0)), 0.0)
    t["c_rdm"] = rdm
    qdec = np.zeros((128, 2, 128), np.float64)
    kdec = np.zeros((128, 4), np.float64)
    g128 = np.zeros((128, 2), np.float64)
    g1 = np.zeros((128, 2), np.float64)
    for pr in range(2):
        for hh in range(2):
            h = 2 * pr + hh
            qdec[64 * hh:64 * hh + 64, pr, :] = np.exp(lg[h] * (np.arange(128) + 1.0))[None, :]
            g128[64 * hh:64 * hh + 64, pr] = np.exp(lg[h] * 128.0)
            g1[64 * hh:64 * hh + 64, pr] = np.exp(lg[h])
    for h in range(4):
        kdec[:, h] = np.exp(lg[h] * (127.0 - np.arange(128))) * 0.125
    t["c_qdec"] = qdec
    t["c_kdec"] = kdec
    t["c_g128"] = g128
    t["c_g1"] = g1
    am = np.zeros((128, 2, 2, 2, 128), np.float64)
    for g in range(2):
        for hh in range(2):
            h = 2 * g + hh
            dist_prev = 128.0 + i - j
            am[:, g, hh, 0, :] = np.where(j > i, np.exp(-slopes[h] * dist_prev), 0.0)
            am[:, g, hh, 1, :] = np.where(i >= j, np.exp(-slopes[h] * (i - j)), 0.0)
    t["c_amask"] = am
    sm = np.zeros((128, 4), np.float64)
    p = np.arange(128, dtype=np.float64)
    for h in range(4):
        sm[:, h] = np.where(p >= 1, np.exp(-slopes[h] * (128.0 - p)), 0.0)
    t["c_smask"] = sm
    t["c_eye"] = np.eye(NS)
    hs = np.zeros((128, 2, 128), np.float64)
    hs[0:64, 0, :] = 1.0
    hs[64:128, 1, :] = 1.0
    t["c_hsel"] = hs
    return {k: np.ascontiguousarray(v, dtype=np.float32) for k, v in t.items()}, lg


def build():
    ctab, lg = const_tables()
    gam = [float(np.exp(x)) for x in lg]
    nc = bass.Bass("TRN2", target_bir_lowering=False)

    def din(name, shape):
        return nc.dram_tensor(name, list(shape), F32, kind="ExternalInput").ap()

    def dout(name, shape):
        return nc.dram_tensor(name, list(shape), F32, kind="ExternalOutput").ap()

    xp = din("xp", (NSEG * TP, D))
    xs = din("xs", (NS, D))
    sret = din("sret", (DEPTH, NS, 4, 64, 128))
    ck = din("ck", (DEPTH, NS, 128, 128))
    cv = din("cv", (DEPTH, NS, 128, 128))
    sconv = din("sconv", (DEPTH, NS, 30, 256))
    W = {}
    for nm, shp in (("ffn1_norm", (DEPTH, D)), ("ffn1_wg", (DEPTH, D, FF)), ("ffn1_wu", (DEPTH, D, FF)),
                    ("ffn1_wd", (DEPTH, FF, D)), ("mix_norm", (DEPTH, D)), ("w_in", (DEPTH, D, IN_DIM)),
                    ("ret_norm_g", (DEPTH, 4, 128)), ("q_norm_g", (DEPTH, 64)), ("k_norm_g", (DEPTH, 64)),
                    ("sinks", (DEPTH, 4)), ("conv_w", (DEPTH, 31, 256)), ("conv_b", (DEPTH, 256)),
                    ("conv_ln_g", (DEPTH, 256)), ("conv_ln_b", (DEPTH, 256)), ("conv_pw", (DEPTH, 256, 256)),
                    ("w_out", (DEPTH, D, D)), ("ffn2_norm", (DEPTH, D)), ("ffn2_wg", (DEPTH, D, FF)),
                    ("ffn2_wu", (DEPTH, D, FF)), ("ffn2_wd", (DEPTH, FF, D))):
        W[nm] = din(nm, shp)
    C = {k: din(k, v.shape) for k, v in ctab.items()}
    yp = dout("yp", (NSEG * TP, D))
    ys = dout("ys", (NS, D))
    retp = dout("retp", (DEPTH, 4, 64, 128))
    rets = dout("rets", (DEPTH, NS, 4, 64, 128))
    kwp = dout("kwp", (DEPTH, 128, 128))
    kws = dout("kws", (DEPTH, NS, 128, 128))
    vwp = dout("vwp", (DEPTH, 128, 128))
    vws = dout("vws", (DEPTH, NS, 128, 128))
    cvp = dout("cvp", (DEPTH, 30, 256))
    cvs = dout("cvs", (DEPTH, NS, 30, 256))

    es = ExitStack()
    with es:
        def sb(name, shape, dt):
            return es.enter_context(nc.sbuf_tensor(name, list(shape), dt))

        S = Sched(nc, es)
        ps = [es.enter_context(nc.psum_tensor("ps%d" % i, [128, 512], F32)) for i in range(8)]
        pctr = [0]

        def pb():
            i = pctr[0] % 8
            pctr[0] += 1
            return ps[i], ("ps", i)

        def MM(out, lhsT, rhs, st, sp, r, w):
            S.op("pe", lambda e: e.matmul(out, lhsT=lhsT, rhs=rhs, start=st, stop=sp), r, w)

        def TR(out, in_, idn, r, w):
            S.op("pe", lambda e: e.transpose(out, in_, idn), r, w)

        def ACT(out, in_, func, r, w, bias=None, scale=1.0):
            if bias is None:
                S.op("act", lambda e: e.activation(out=out, in_=in_, func=func, scale=scale), r, w)
            else:
                S.op("act", lambda e: e.activation(out=out, in_=in_, func=func, bias=bias, scale=scale), r, w)

        def STT(out, in0, scalar, in1, op0, op1, r, w, eng="dve"):
            S.op(eng, lambda e: e.scalar_tensor_tensor(out=out, in0=in0, scalar=scalar, in1=in1, op0=op0, op1=op1), r, w)

        def TT(out, in0, in1, op, r, w, eng="dve"):
            S.op(eng, lambda e: e.tensor_tensor(out=out, in0=in0, in1=in1, op=op), r, w)

        def TS(out, in0, s1, op0, r, w, s2=None, op1=None, eng="dve"):
            if op1 is None:
                S.op(eng, lambda e: e.tensor_scalar(out=out, in0=in0, scalar1=s1, scalar2=None, op0=op0), r, w)
            else:
                S.op(eng, lambda e: e.tensor_scalar(out=out, in0=in0, scalar1=s1, scalar2=s2, op0=op0, op1=op1), r, w)

        def CP(out, in_, r, w, eng="dve"):
            if DBG.get("noactcp") and eng == "act":
                eng = "dve"
            if eng == "act":
                S.op("act", lambda e: e.activation(out=out, in_=in_, func=AF.Identity), r, w)
            else:
                S.op(eng, lambda e: e.tensor_copy(out, in_), r, w)

        def RCP(out, in_, r, w):
            S.op("dve", lambda e: e.reciprocal(out, in_), r, w)

        def MSET(ap, val, w, eng="dve"):
            S.op(eng, lambda e: e.memset(ap, val), (), w)

        def DMA(q, out, in_, r, w, key, final=False, slow=False):
            if slow:
                S.op(q, lambda e: e.dma_start(out=out, in_=in_, allow_slow_non_contiguous=True), r, w, dma=True,
                     semkey=key, final=final)
            else:
                S.op(q, lambda e: e.dma_start(out=out, in_=in_), r, w, dma=True, semkey=key, final=final)

        xT = sb("xT", [128, 8, NT], F32)
        hT = sb("hT", [128, 8, NT], BF16)
        oT = sb("oT", [128, 10, NT], BF16)
        ident = sb("ident", [128, 128], F32)
        identb = sb("identb", [128, 128], BF16)
        onesv = sb("onesv", [128, 5, 128], BF16)
        epsb = sb("epsb", [128, 1], F32)
        stage = sb("stage", [128, 2, D], F32)
        rdm = sb("rdm", [128, 4, 128], F32)
        qdec = sb("qdec", [128, 2, 128], F32)
        kdec = sb("kdec", [128, 4], F32)
        g128 = sb("g128", [128, 2], F32)
        g1 = sb("g1", [128, 2], F32)
        amask = sb("amask", [128, 2, 512], BF16)
        smask = sb("smask", [128, 4], F32)
        eyeN = sb("eyeN", [NS, NS], BF16)
        hsel = sb("hsel", [128, 2, 128], BF16)
        n1g = sb("n1g", [128, DEPTH, 8], F32)
        mng = sb("mng", [128, DEPTH, 8], F32)
        n2g = sb("n2g", [128, DEPTH, 8], F32)
        retg = sb("retg", [128, DEPTH, 4], F32)
        qg8 = sb("qg8", [64, DEPTH], F32)
        kgs = sb("kgs", [64, DEPTH], F32)
        esink = sb("esink", [64, DEPTH * 4], F32)
        convw = sb("convw", [128, DEPTH, 2, 31], F32)
        convb = sb("convb", [128, DEPTH, 2], F32)
        lng = sb("lng", [128, DEPTH, 2], F32)
        lnb = sb("lnb", [128, DEPTH, 2], F32)
        Sst = sb("Sst", [128, DEPTH, 2, 128], F32)
        kTh = sb("kTh", [64, DEPTH, 2, 128], BF16)
        Vh = sb("Vh", [128, DEPTH, 128], BF16)
        uh = sb("uh", [128, DEPTH, 2, 30], BF16)
        NA = 16640
        arena = sb("arena", [128, NA], F32)
        aoff = [0]
        WG = [sb("WG%d" % i, [128, 8192], BF16) for i in range(2)]
        WD = [sb("WD%d" % i, [128, 4096], BF16) for i in range(2)]
        wgc = [0]
        wdc = [0]

        def wg_slot():
            i = wgc[0] % 2
            wgc[0] += 1
            return WG[i], ("WG", i)

        def wd_slot():
            i = wdc[0] % 2
            wdc[0] += 1
            return WD[i], ("WD", i)

        def wview(Wt, off, a, b, parts=128):
            return Wt[0:parts, off:off + a * b].rearrange("p (a b) -> p a b", a=a, b=b)

        def WDMA(out, in_, key):
            S.op("pool", lambda e: e.dma_start(out=out, in_=in_), (), [key], dma=True, semkey=key, nobar=True)

        def areset():
            aoff[0] = 0

        def aalloc(shape, dt, parts=128):
            n = int(np.prod(shape))
            words = (n + 1) // 2 if dt == BF16 else n
            o = aoff[0]
            aoff[0] += words
            assert aoff[0] <= NA, ("arena overflow", aoff[0])
            a = arena[0:parts, o:o + words]
            if dt == BF16:
                a = a.bitcast(BF16)
                if n % 2:
                    a = a[:, 0:n]
            if len(shape) == 2:
                return a.rearrange("p (a b) -> p a b", a=shape[0], b=shape[1])
            if len(shape) == 3:
                return a.rearrange("p (a b c) -> p a b c", a=shape[0], b=shape[1], c=shape[2])
            return a

        MSET(ident[:], 0.0, ["ident"], eng="pool")
        S.op("pool", lambda e: e.affine_select(out=ident[:], in_=ident[:], pattern=[[-1, 128]],
                                               compare_op=ALU.not_equal, fill=1.0, base=0, channel_multiplier=1),
             ["ident"], ["ident"])
        CP(identb[:], ident[:], ["ident"], ["identb"])
        for i, v in enumerate((1.0 / 1024, 1.0 / 64, 1.0 / 128, 1.0 / 256, 1.0)):
            MSET(onesv[:, i, :], v, ["onesv"])
        MSET(epsb[:], EPS, ["epsb"])
        MSET(Sst[:], 0.0, ["Sst"])
        MSET(kTh[:], 0.0, ["kTh"])
        MSET(Vh[:], 0.0, ["Vh"])
        MSET(uh[:], 0.0, ["uh"])
        if DBG.get("zero_o"):
            MSET(oT[:], 0.0, ["oT"])
        DMA("sp", rdm[:], C["c_rdm"], (), ["rdm"], "c_rdm")
        DMA("sp", qdec[:], C["c_qdec"], (), ["qdec"], "c_qdec")
        DMA("sp", kdec[:], C["c_kdec"], (), ["kdec"], "c_kdec")
        DMA("sp", g128[:], C["c_g128"], (), ["g128"], "c_g128")
        DMA("sp", g1[:], C["c_g1"], (), ["g1"], "c_g1")
        DMA("sp", smask[:], C["c_smask"], (), ["smask"], "c_smask")
        if not DBG.get("nopoolc"):
          DMA("pool", amask[:], C["c_amask"].rearrange("p a b c d -> p a (b c d)"), (), ["amask"], "c_amask")
          DMA("pool", eyeN[:], C["c_eye"], (), ["eyeN"], "c_eyeN")
          DMA("pool", hsel[:], C["c_hsel"], (), ["hsel"], "c_hsel")
        for l in range(DEPTH if not DBG.get("noparams") else 0):
            DMA("sp", n1g[:, l, :], W["ffn1_norm"][l].rearrange("(c p) -> p c", p=128), (), ["n1g"], "c_n1g", slow=True)
            DMA("sp", mng[:, l, :], W["mix_norm"][l].rearrange("(c p) -> p c", p=128), (), ["mng"], "c_mng", slow=True)
            DMA("sp", n2g[:, l, :], W["ffn2_norm"][l].rearrange("(c p) -> p c", p=128), (), ["n2g"], "c_n2g", slow=True)
            DMA("sp", retg[:, l, :], W["ret_norm_g"][l].rearrange("h p -> p h"), (), ["retg"], "c_retg", slow=True)
            DMA("sp", qg8[:, l:l + 1], W["q_norm_g"][l].rearrange("(p o) -> p o", o=1), (), ["qg8"], "c_qg8", slow=True)
            DMA("sp", kgs[:, l:l + 1], W["k_norm_g"][l].rearrange("(p o) -> p o", o=1), (), ["kgs"], "c_kgs", slow=True)
            DMA("sp", esink[:, l * 4:(l + 1) * 4], W["sinks"][l:l + 1, :].partition_broadcast(64), (), ["esink"],
                "c_esink", slow=True)
            for c in range(2):
                DMA("sp", convw[:, l, c, :], W["conv_w"][l][:, c * 128:(c + 1) * 128].rearrange("j p -> p j"), (),
                    ["convw"], "c_convw", slow=True)
            DMA("sp", convb[:, l, :], W["conv_b"][l].rearrange("(c p) -> p c", p=128), (), ["convb"], "c_convb", slow=True)
            DMA("sp", lng[:, l, :], W["conv_ln_g"][l].rearrange("(c p) -> p c", p=128), (), ["lng"], "c_lng", slow=True)
            DMA("sp", lnb[:, l, :], W["conv_ln_b"][l].rearrange("(c p) -> p c", p=128), (), ["lnb"], "c_lnb", slow=True)
        S.barrier()
        TS(qg8[:], qg8[:], 0.125, ALU.mult, ["qg8"], ["qg8"])
        ACT(esink[:], esink[:], AF.Exp, ["esink"], ["esink"])
        S.barrier()

        def tiles_of(si):
            t = [(0, 512), (512, 512)]
            if si == 0:
                t.append((TP, NS))
            return t

        def load_x(si):
            chunks = [(xp[si * TP + n * 128: si * TP + (n + 1) * 128, :], n * 128, 128) for n in range(TP // 128)]
            if si == 0 and not DBG.get("nosample"):
                chunks.append((xs[:, :], TP, NS))
            if DBG.get("noload"):
                chunks = []
            for ci, (src, t0, n) in enumerate(chunks):
                sl = ci % 2
                DMA("sp", stage[0:n, sl, :], src, (), [("stage", sl)], ("stage", sl))
                for half in range(2):
                    p, pk = pb()
                    for q in range(4):
                        c = half * 4 + q
                        TR(p[:, q * 128:q * 128 + n], stage[0:n, sl, c * 128:(c + 1) * 128], ident[0:n, 0:n],
                           [("stage", sl), "ident"], [pk])
                    for q in range(4):
                        c = half * 4 + q
                        CP(xT[:, c, t0:t0 + n], p[:, q * 128:q * 128 + n], [pk],
                           [("x", c, (t0 // 512) * 512 if t0 < TP else TP)], eng="act" if half else "dve")

        def store_y(si):
            chunks = [(yp[si * TP + n * 128: si * TP + (n + 1) * 128, :], n * 128, 128) for n in range(TP // 128)]
            if si == 0 and not DBG.get("nosample"):
                chunks.append((ys[:, :], TP, NS))
            if DBG.get("nostore"):
                chunks = chunks[:1]
            for ci, (dst, t0, n) in enumerate(chunks):
                sl = ci % 2
                tk = (t0 // 512) * 512 if t0 < TP else TP
                for half in range(2):
                    p, pk = pb()
                    for q in range(4):
                        c = half * 4 + q
                        TR(p[0:n, q * 128:(q + 1) * 128], xT[:, c, t0:t0 + n], ident[:, :],
                           [("x", c, tk), "ident"], [pk])
                    CP(stage[0:n, sl, half * 512:(half + 1) * 512], p[0:n, :], [pk], [("stage", sl)],
                       eng="act" if half else "dve")
                DMA("sp", dst, stage[0:n, sl, :], [("stage", sl)], (), ("stage", sl), final=True)

        def xkey(c, t0):
            return ("x", c, t0)

        def xkeys(c, t0, n):
            return [("x", c, t0)]

        def rstd_from(psmean, pk, n, parts, rbuf, rk):
            ACT(rbuf[0:parts, 0:n], psmean, AF.Ln, [pk, "epsb"], [rk], bias=epsb[0:parts, 0:1])
            ACT(rbuf[0:parts, 0:n], rbuf[0:parts, 0:n], AF.Exp, [rk], [rk], scale=-0.5)

        def rmsnorm(si, gain, l):
            assert aoff[0] == 0
            sq = aalloc([8, 512], BF16)
            rb = aalloc([2, 512], F32)
            for ti, (t0, n) in enumerate(tiles_of(si)):
                p, pk = pb()
                for c in range(8):
                    ACT(sq[:, c, 0:n], xT[:, c, t0:t0 + n], AF.Square, [("x", c, t0)], [("sq", c)])
                for c in range(8):
                    MM(p[:, 0:n], onesv[:, 0, :], sq[:, c, 0:n], c == 0, c == 7, [("sq", c), "onesv"], [pk])
                rk = ("rstd", ti % 2)
                rstd_from(p[:, 0:n], pk, n, 128, rb[:, ti % 2, :], rk)
                for c in range(8):
                    STT(hT[:, c, t0:t0 + n], xT[:, c, t0:t0 + n], gain[:, l, c:c + 1], rb[:, ti % 2, 0:n],
                        ALU.mult, ALU.mult, [("x", c, t0), rk], [("h", t0)])

        def fix_xkeys(si):
            pass

        def ffn(si, l, which):
            areset()
            wg_d = W["ffn%d_wg" % which][l]
            wu_d = W["ffn%d_wu" % which][l]
            wd_d = W["ffn%d_wd" % which][l]
            gain = n1g if which == 1 else n2g
            rmsnorm(si, gain, l)
            act = [aalloc([4, NT], BF16) for _ in range(2)]
            sg = aalloc([2, 512], F32)
            groups = [(g * 512, 4) for g in range(5)] + [(2560, 2)]
            tl = tiles_of(si)
            gslots = {}

            def gateup(j):
                f0, nfc = groups[j]
                sl = j % 2
                Wg, gk = wg_slot()
                Wd, dk = wd_slot()
                wgv = wview(Wg, 0, 8, 512)
                wuv = wview(Wg, 4096, 8, 512)
                wdv = wview(Wd, 0, 4, D)
                gslots[j] = (wdv, dk)
                WDMA(wgv[:, :, 0:nfc * 128], wg_d[:, f0:f0 + nfc * 128].rearrange("(k p) f -> p k f", p=128), gk)
                WDMA(wuv[:, :, 0:nfc * 128], wu_d[:, f0:f0 + nfc * 128].rearrange("(k p) f -> p k f", p=128), gk)
                WDMA(wdv[:, 0:nfc, :], wd_d[f0:f0 + nfc * 128, :].rearrange("(k p) d -> p k d", p=128), dk)
                cnt = 0
                for (t0, n) in tl:
                    for fc in range(nfc):
                        pg, pgk = pb()
                        pu, puk = pb()
                        for k in range(8):
                            MM(pg[:, 0:n], wgv[:, k, fc * 128:(fc + 1) * 128], hT[:, k, t0:t0 + n], k == 0, k == 7,
                               [gk, ("h", t0)], [pgk])
                        for k in range(8):
                            MM(pu[:, 0:n], wuv[:, k, fc * 128:(fc + 1) * 128], hT[:, k, t0:t0 + n], k == 0, k == 7,
                               [gk, ("h", t0)], [puk])
                        s2 = cnt % 2
                        cnt += 1
                        ACT(sg[:, s2, 0:n], pg[:, 0:n], AF.Silu, [pgk], [("sg", s2)])
                        TT(act[sl][:, fc, t0:t0 + n], sg[:, s2, 0:n], pu[:, 0:n], ALU.mult, [("sg", s2), puk],
                           [("act", sl, t0)])

            def down(j):
                f0, nfc = groups[j]
                sl = j % 2
                wdv, dk = gslots[j]
                for (t0, n) in tl:
                    for c in range(8):
                        p, pk = pb()
                        for fc in range(nfc):
                            MM(p[:, 0:n], wdv[:, fc, c * 128:(c + 1) * 128], act[sl][:, fc, t0:t0 + n], fc == 0,
                               fc == nfc - 1, [dk, ("act", sl, t0)], [pk])
                        STT(xT[:, c, t0:t0 + n], p[:, 0:n], 0.5, xT[:, c, t0:t0 + n], ALU.mult, ALU.add,
                            [pk, ("x", c, t0)], [("x", c, t0)])

            for j in range(len(groups) + 1):
                if j < len(groups):
                    gateup(j)
                if j >= 1:
                    down(j - 1)

        def ret_epilogue(Oin, Okey, n, l, h, hh, sgT, t0, scr, ebank=None):
            Osb, sqb, rb = scr
            CP(Osb[:, 0:n], Oin, [Okey], ["Osb"], eng="act")
            ACT(sqb[:, 0:n], Oin, AF.Square, [Okey], ["sqb"])
            p, pk = pb() if ebank is None else (ps[ebank], ("ps", ebank))
            MM(p[:, 0:n], onesv[:, 2, :], sqb[:, 0:n], True, True, ["sqb", "onesv"], [pk])
            rstd_from(p[:, 0:n], pk, n, 128, rb, "rb")
            STT(Osb[:, 0:n], Osb[:, 0:n], retg[:, l, h:h + 1], rb[:, 0:n], ALU.mult, ALU.mult, ["Osb", "rb"], ["Osb"])
            TT(oT[:, h, t0:t0 + n], Osb[:, 0:n], sgT[:, hh, t0:t0 + n], ALU.mult, ["Osb", ("sgT", t0)], [("o", h, t0)])

        def retention(si, l, pr, last):
            areset()
            win = W["w_in"][l]
            Wg, gk = wg_slot()
            wqk = wview(Wg, 0, 8, 256)
            wv = wview(Wg, 2048, 8, 256)
            wgt = wview(Wg, 4096, 8, 256)
            Kp = aalloc([8, 128], BF16)
            Vt = aalloc([8, 256], BF16)
            qT = aalloc([1, NT], BF16)[:, 0, :]
            kT = aalloc([1, NT], BF16)[:, 0, :]
            qdT = aalloc([1, TP], BF16)[:, 0, :]
            sgT = aalloc([2, NT], BF16)
            AT = [aalloc([4, 128], BF16) for _ in range(2)]
            Sbf = aalloc([1, 128], BF16)[:, 0, :]
            Osb = aalloc([1, 512], F32)[:, 0, :]
            sqb = aalloc([1, 512], BF16)[:, 0, :]
            rb = aalloc([1, 512], F32)[:, 0, :]
            scr = (Osb, sqb, rb)
            wv3 = win.rearrange("(k p) f -> p k f", p=128)
            WDMA(wqk[:, :, 0:128], wv3[:, :, pr * 128:(pr + 1) * 128], gk)
            WDMA(wqk[:, :, 128:256], wv3[:, :, 256 + pr * 128:256 + (pr + 1) * 128], gk)
            WDMA(wv[:], wv3[:, :, 512 + pr * 256:512 + (pr + 1) * 256], gk)
            WDMA(wgt[:], wv3[:, :, 1024 + pr * 256:1024 + (pr + 1) * 256], gk)
            Sf = Sst[:, l, pr, :]
            CP(Sbf, Sf, ["Sst"], ["Sbf"])
            for n in range(8):
                p, pk = pb()
                p2, pk2 = pb()
                for k in range(8):
                    MM(p[:, 0:128], hT[:, k, n * 128:(n + 1) * 128], wqk[:, k, 128:256], k == 0, k == 7,
                       [("h", (n // 4) * 512), gk], [pk])
                for k in range(8):
                    MM(p2[:, 0:256], hT[:, k, n * 128:(n + 1) * 128], wv[:, k, :], k == 0, k == 7,
                       [("h", (n // 4) * 512), gk], [pk2])
                for hh in range(2):
                    TS(Kp[:, n, hh * 64:(hh + 1) * 64], p[:, hh * 64:(hh + 1) * 64], kdec[:, 2 * pr + hh:2 * pr + hh + 1],
                       ALU.mult, [pk, "kdec"], [("Kp", n)])
                CP(Vt[:, n, :], p2[:, 0:256], [pk2], [("Vt", n)], eng="act")
            for (t0, n) in tiles_of(si):
                p, pk = pb()
                for k in range(8):
                    MM(p[:, 0:n], wqk[:, k, 0:128], hT[:, k, t0:t0 + n], k == 0, k == 7, [gk, ("h", t0)], [pk])
                CP(qT[:, t0:t0 + n], p[:, 0:n], [pk], [("qT", t0)], eng="dve")
                if t0 < TP:
                    TT(qdT[:, t0:t0 + n].rearrange("p (a b) -> p a b", a=4), p[:, 0:n].rearrange("p (a b) -> p a b", a=4),
                       qdec[:, pr:pr + 1, :].to_broadcast([128, 4, 128]), ALU.mult, [pk, "qdec"], [("qdT", t0)])
                p, pk = pb()
                for k in range(8):
                    MM(p[:, 0:n], wqk[:, k, 128:256], hT[:, k, t0:t0 + n], k == 0, k == 7, [gk, ("h", t0)], [pk])
                ACT(kT[:, t0:t0 + n], p[:, 0:n], AF.Identity, [pk], [("kT", t0)], scale=0.125)
                if t0 >= TP:
                    for hh in range(2):
                        p, pk = pb()
                        for k in range(8):
                            MM(p[:, 0:n], wgt[:, k, hh * 128:(hh + 1) * 128], hT[:, k, t0:t0 + n], k == 0, k == 7,
                               [gk, ("h", t0)], [pk])
                        ACT(sgT[:, hh, t0:t0 + n], p[:, 0:n], AF.Silu, [pk], [("sgT", t0)])
            Sall = aalloc([8, 128], BF16)
            AT2 = [[AT[0], AT[1]], [aalloc([4, 128], BF16), aalloc([4, 128], BF16)]]
            for n in range(8):
                b = n // 2
                off = (n % 2) * 256
                MM(ps[b][:, off:off + 128], Kp[:, n, :], Vt[:, n, 0:128], True, True, [("Kp", n), ("Vt", n)], [("ps", b)])
                MM(ps[b][:, off + 128:off + 256], Kp[:, n, :], Vt[:, n, 128:256], True, True, [("Kp", n), ("Vt", n)],
                   [("ps", b)])
            for ti in range(2):
                t0 = ti * 512
                for hh in range(2):
                    bank = 4 + 2 * ti + hh
                    lo, hi = 64 * hh, 64 * hh + 64
                    for nn in range(4):
                        c0 = t0 + nn * 128
                        MM(ps[bank][:, nn * 128:(nn + 1) * 128], kT[lo:hi, c0:c0 + 128], qT[lo:hi, c0:c0 + 128], True, True,
                           [("kT", t0), ("qT", t0)], [("ps", bank)])
                    TT(AT2[ti][hh][:], ps[bank][:, :].rearrange("p (a b) -> p a b", a=4),
                       rdm[:, 2 * pr + hh:2 * pr + hh + 1, :].to_broadcast([128, 4, 128]), ALU.mult, [("ps", bank), "rdm"],
                       [("AT", ti, hh)])
            for ti in range(2):
                t0 = ti * 512
                for hh in range(2):
                    bank = 4 + 2 * ti + hh
                    for k in range(8):
                        MM(ps[bank][:, :], wgt[:, k, hh * 128:(hh + 1) * 128], hT[:, k, t0:t0 + 512], k == 0, k == 7,
                           [gk, ("h", t0)], [("ps", bank)])
                    ACT(sgT[:, hh, t0:t0 + 512], ps[bank][:, :], AF.Silu, [("ps", bank)], [("sgT", t0)])
            for n in range(8):
                b = n // 2
                off = (n % 2) * 256
                CP(Sall[:, n, :], Sst[:, l, pr, :], ["Sst"], [("Sall", n)])
                for hh in range(2):
                    lo, hi = 64 * hh, 64 * hh + 64
                    STT(Sst[lo:hi, l, pr, :], Sst[lo:hi, l, pr, :], gam[2 * pr + hh] ** 128,
                        ps[b][lo:hi, off + hh * 128:off + (hh + 1) * 128], ALU.mult, ALU.add, [("ps", b), "Sst"], ["Sst"])
            for ti in range(2):
                t0 = ti * 512
                for hh in range(2):
                    bank = 2 * ti + hh
                    lo, hi = 64 * hh, 64 * hh + 64
                    for nn in range(4):
                        n = ti * 4 + nn
                        c0 = t0 + nn * 128
                        MM(ps[bank][:, nn * 128:(nn + 1) * 128], Vt[:, n, hh * 128:(hh + 1) * 128], AT2[ti][hh][:, nn, :],
                           True, False, [("Vt", n), ("AT", ti, hh)], [("ps", bank)])
                        MM(ps[bank][:, nn * 128:(nn + 1) * 128], Sall[lo:hi, n, :], qdT[lo:hi, c0:c0 + 128], False, True,
                           [("Sall", n), ("qdT", t0)], [("ps", bank)])
                for hh in range(2):
                    bank = 2 * ti + hh
                    ret_epilogue(ps[bank][:, :], ("ps", bank), 512, l, 2 * pr + hh, hh, sgT, t0, scr, ebank=4 + 2 * ti + hh)
            pctr[0] = 0
            if last:
                for hh in range(2):
                    DMA("sp", retp[l, 2 * pr + hh], Sst[64 * hh:64 * hh + 64, l, pr, :], ["Sst"], (), "retp", final=True)
            if si == 0:
                t0 = TP
                S0f = aalloc([NS, 128], F32)
                S0b = aalloc([NS, 128], BF16)
                Ks = aalloc([1, 128], BF16, parts=NS)[:, 0, :]
                Vs = aalloc([1, 256], BF16, parts=NS)[:, 0, :]
                Vbd1 = aalloc([NS, 128], BF16, parts=NS)
                Vbd = [Vbd1, Vbd1]
                vTs = aalloc([2, NS], F32)
                prod = aalloc([1, NS], BF16)[:, 0, :]
                tmp = aalloc([1, NS], F32)[:, 0, :]
                Os = aalloc([1, NS], F32)[:, 0, :]
                Sn = aalloc([2, 512], F32)
                DMA("sp", S0f[:], sret[l, :, 2 * pr:2 * pr + 2].rearrange("b h d v -> (h d) b v"), (), ["S0f"], "S0f")
                CP(S0b[:], S0f[:], ["S0f"], ["S0b"])
                p, pk = pb()
                for k in range(8):
                    MM(p[0:NS, 0:128], hT[:, k, t0:t0 + NS], wqk[:, k, 128:256], k == 0, k == 7, [("h", t0), gk], [pk])
                for k in range(8):
                    MM(p[0:NS, 128:384], hT[:, k, t0:t0 + NS], wv[:, k, :], k == 0, k == 7, [("h", t0), gk], [pk])
                ACT(Ks, p[0:NS, 0:128], AF.Identity, [pk], ["Ks"], scale=0.125)
                CP(Vs, p[0:NS, 128:384], [pk], ["Vs"], eng="act")
                for hh in range(2):
                    p, pk = pb()
                    for k in range(8):
                        MM(p[:, 0:NS], wv[:, k, hh * 128:(hh + 1) * 128], hT[:, k, t0:t0 + NS], k == 0, k == 7,
                           [gk, ("h", t0)], [pk])
                    CP(vTs[:, hh, :], p[:, 0:NS], [pk], [("vTs", hh)], eng="act")
                TT(prod, qT[:, t0:t0 + NS], kT[:, t0:t0 + NS], ALU.mult, [("qT", t0), ("kT", t0)], ["prod"])
                for hh in range(2):
                    lo, hi = 64 * hh, 64 * hh + 64
                    h = 2 * pr + hh
                    pt, ptk = pb()
                    for b in range(NS):
                        MM(pt[:, b:b + 1], S0b[lo:hi, b, :], qT[lo:hi, t0 + b:t0 + b + 1], True, True,
                           ["S0b", ("qT", t0)], [ptk])
                    pq, pqk = pb()
                    MM(pq[:, 0:NS], hsel[:, hh, :], prod, True, True, ["hsel", "prod"], [pqk])
                    TT(tmp, pq[:, 0:NS], vTs[:, hh, :], ALU.mult, [pqk, ("vTs", hh)], ["tmp"])
                    STT(Os, pt[:, 0:NS], gam[h], tmp, ALU.mult, ALU.add, [ptk, "tmp"], ["Os"])
                    ret_epilogue(Os, "Os", NS, l, h, hh, sgT, t0, scr)
                    TT(Vbd[hh][:], Vs[:, hh * 128:(hh + 1) * 128].unsqueeze(1).to_broadcast([NS, NS, 128]),
                       eyeN[:, :].unsqueeze(2).to_broadcast([NS, NS, 128]), ALU.mult, ["Vs", "eyeN"], ["Vbd"])
                    for q in range(NS // 4):
                        pn, pnk = pb()
                        MM(pn[:, :], Ks[:, :], Vbd[hh][:, 4 * q:4 * q + 4, :].rearrange("p a b -> p (a b)"), True, True, ["Ks", "Vbd"], [pnk])
                        s2 = q % 2
                        STT(Sn[lo:hi, s2, :], S0f[lo:hi, 4 * q:4 * q + 4, :].rearrange("p a b -> p (a b)"), gam[h], pn[lo:hi, :], ALU.mult, ALU.add,
                            [pnk, "S0f"], [("Sn", s2)])
                        DMA("sp", rets[l, 4 * q:4 * q + 4, h].rearrange("b d v -> d b v"),
                            Sn[lo:hi, s2, :].rearrange("p (b v) -> p b v", b=4), [("Sn", s2)], (), ("Sn", s2), final=True)
            S.barrier()

        def attention(si, l, last):
            areset()
            wv3 = W["w_in"][l].rearrange("(k p) f -> p k f", p=128)
            Wg, gk = wg_slot()
            wa = wview(Wg, 0, 8, 512)
            Va = aalloc([9, 128], BF16)
            qn = [aalloc([1, NT], BF16, parts=64)[:, 0, :] for _ in range(4)]
            kn = [aalloc([1, 128 + NT], BF16, parts=64)[:, 0, :] for _ in range(2)]
            qsb = aalloc([1, 512], F32, parts=64)[:, 0, :]
            sqb = aalloc([1, 512], BF16, parts=64)[:, 0, :]
            rb = aalloc([1, 512], F32, parts=64)[:, 0, :]
            knf = aalloc([2, 128], F32, parts=64)
            vlast = aalloc([1, 128], F32)[:, 0, :]
            E = [aalloc([1, 512], BF16)[:, 0, :] for _ in range(4)]
            PT = [aalloc([1, 512], BF16)[:, 0, :] for _ in range(4)]
            den = aalloc([1, 512], F32, parts=64)[:, 0, :]
            WDMA(wa[:], wv3[:, :, 1536:2048], gk)
            CP(Va[:, 0, :], Vh[:, l, :], ["Vh"], [("Va", 0)])
            for g in range(2):
                CP(kn[g][:, 0:128], kTh[:, l, g, :], ["kTh"], [("kn", g, -1)])
            for n in range(8):
                p, pk = pb()
                for k in range(8):
                    MM(p[:, 0:128], hT[:, k, n * 128:(n + 1) * 128], wa[:, k, 384:512], k == 0, k == 7,
                       [("h", (n // 4) * 512), gk], [pk])
                CP(Va[:, n + 1, :], p[:, 0:128], [pk], [("Va", n + 1)], eng="act")
                if n == 7:
                    if last:
                        CP(vlast, p[:, 0:128], [pk], ["vlast"], eng="act")
                        DMA("sp", vwp[l], vlast, ["vlast"], (), "vwp", final=True)
                    CP(Vh[:, l, :], p[:, 0:128], [pk], ["Vh"], eng="act")
            sqb2 = [sqb, aalloc([1, 512], BF16, parts=64)[:, 0, :]]
            rb2 = [rb, aalloc([1, 512], F32, parts=64)[:, 0, :]]
            items = []
            for (t0, n) in tiles_of(si):
                for h in range(4):
                    items.append(("q", h, t0, n))
                for g in range(2):
                    items.append(("k", g, t0, n))
            state = {}

            def b_proj(i):
                kind, idx, t0, n = items[i]
                wcol = idx * 64 if kind == "q" else 256 + idx * 64
                p, pk = pb()
                for k in range(8):
                    MM(p[0:64, 0:n], wa[:, k, wcol:wcol + 64], hT[:, k, t0:t0 + n], k == 0, k == 7, [gk, ("h", t0)], [pk])
                state[i] = (p, pk)

            def b_finish(i):
                kind, idx, t0, n = items[i]
                p, pk = state.pop(i)
                s2 = i % 2
                ACT(sqb2[s2][:, 0:n], p[0:64, 0:n], AF.Square, [pk], [("sqbA", s2)])
                p2, pk2 = pb()
                MM(p2[0:64, 0:n], onesv[0:64, 1, 0:64], sqb2[s2][:, 0:n], True, True, [("sqbA", s2), "onesv"], [pk2])
                rstd_from(p2[0:64, 0:n], pk2, n, 64, rb2[s2], ("rbA", s2))
                if kind == "q":
                    STT(qn[idx][:, t0:t0 + n], p[0:64, 0:n], qg8[:, l:l + 1], rb2[s2][:, 0:n], ALU.mult, ALU.mult,
                        [pk, ("rbA", s2)], [("qn", idx, t0)])
                else:
                    g = idx
                    STT(kn[g][:, 128 + t0:128 + t0 + n], p[0:64, 0:n], kgs[:, l:l + 1], rb2[s2][:, 0:n], ALU.mult, ALU.mult,
                        [pk, ("rbA", s2)], [("kn", g, t0)])
                    if t0 == 512:
                        STT(knf[:, g, :], p[0:64, 384:512], kgs[:, l:l + 1], rb2[s2][:, 384:512], ALU.mult, ALU.mult,
                            [pk, ("rbA", s2)], [("knf", g)])
                        if last:
                            DMA("sp", kwp[l, :, g * 64:(g + 1) * 64].rearrange("t d -> d t"), knf[:, g, :], [("knf", g)],
                                (), "kwp", final=True, slow=True)
                        CP(kTh[:, l, g, :], kn[g][:, 128 + 896:128 + 1024], [("kn", g, 512)], ["kTh"])

            b_proj(0)
            for i in range(len(items)):
                if i + 1 < len(items):
                    b_proj(i + 1)
                b_finish(i)
            lnd = qsb
            it = 0
            for g in range(2):
                for ti in range(2):
                    t0 = ti * 512
                    base = 4 * (it % 2)
                    oth = 4 - base
                    it += 1
                    pO = [(ps[base + 0], ("ps", base + 0)), (ps[base + 1], ("ps", base + 1))]
                    pD = [(ps[base + 2], ("ps", base + 2)), (ps[base + 3], ("ps", base + 3))]
                    for nn in range(4):
                        n = ti * 4 + nn
                        c0 = n * 128
                        pS, pSk = ps[oth + nn], ("ps", oth + nn)
                        prevk = ("kn", g, ((c0 - 128) // 512) * 512) if c0 >= 128 else ("kn", g, -1)
                        for hh in range(2):
                            h = 2 * g + hh
                            MM(pS[:, hh * 256:hh * 256 + 128], kn[g][:, c0:c0 + 128], qn[h][:, c0:c0 + 128], True, True,
                               [prevk, ("qn", h, t0)], [pSk])
                            MM(pS[:, hh * 256 + 128:hh * 256 + 256], kn[g][:, 128 + c0:256 + c0], qn[h][:, c0:c0 + 128],
                               True, True, [("kn", g, t0), ("qn", h, t0)], [pSk])
                    for nn in range(4):
                        pS, pSk = ps[oth + nn], ("ps", oth + nn)
                        ACT(E[nn], pS[:, :], AF.Exp, [pSk], [("E", nn)])
                        TT(PT[nn], E[nn], amask[:, g, :], ALU.mult, [("E", nn), "amask"], [("PT", nn)])
                    for nn in range(4):
                        n = ti * 4 + nn
                        skip_prev = (si == 0 and n == 0)
                        s2 = nn
                        for hh in range(2):
                            cs = slice(nn * 128, (nn + 1) * 128)
                            prevP = PT[s2][:, hh * 256:hh * 256 + 128]
                            curP = PT[s2][:, hh * 256 + 128:hh * 256 + 256]
                            if not skip_prev:
                                MM(pO[hh][0][0:64, cs], Va[:, n, g * 64:(g + 1) * 64], prevP, True, False,
                                   [("Va", n), ("PT", s2)], [pO[hh][1]])
                            MM(pO[hh][0][0:64, cs], Va[:, n + 1, g * 64:(g + 1) * 64], curP, skip_prev, True,
                               [("Va", n + 1), ("PT", s2)], [pO[hh][1]])
                            if not skip_prev:
                                MM(pD[hh][0][0:64, cs], onesv[:, 4, 0:64], prevP, True, False, [("PT", s2), "onesv"],
                                   [pD[hh][1]])
                            MM(pD[hh][0][0:64, cs], onesv[:, 4, 0:64], curP, skip_prev, True, [("PT", s2), "onesv"],
                               [pD[hh][1]])
                    for hh in range(2):
                        h = 2 * g + hh
                        ACT(lnd, pD[hh][0][0:64, :], AF.Ln, [pD[hh][1], "esink"], ["qsb"],
                            bias=esink[:, l * 4 + h:l * 4 + h + 1])
                        ACT(den, lnd, AF.Exp, ["qsb"], ["den"], scale=-1.0)
                        TT(oT[0:64, 4 + h, t0:t0 + 512], pO[hh][0][0:64, :], den[:, :], ALU.mult, [pO[hh][1], "den"],
                           [("o", 4 + h, t0)])
            pctr[0] = 0
            if si == 0:
                t0 = TP
                HB = NS // 2
                kcf = aalloc([HB, 128], F32)
                vcb = aalloc([NS, 128], BF16)
                kcT = aalloc([NS, 128], BF16, parts=64)
                vnT = aalloc([2, NS], F32, parts=64)
                knS = aalloc([2, NS], F32, parts=64)
                Es = aalloc([1, 4 * NS], F32)[:, 0, :]
                Ps = aalloc([4, NS], BF16)
                prod = aalloc([1, NS], BF16, parts=64)[:, 0, :]
                pn = aalloc([4, NS], F32, parts=64)
                num = aalloc([4, NS], F32, parts=64)
                dn = aalloc([4, NS], F32, parts=64)
                DMA("pool", vcb[:], cv[l].rearrange("b k f -> k b f"), (), ["vcb"], "vcb")
                DMA("sp", kws[l, :, 0:127, :], ck[l, :, 1:128, :], (), (), "kws", final=True)
                DMA("sp", vws[l, :, 0:127, :], cv[l, :, 1:128, :], (), (), "vws", final=True)
                for g in range(2):
                    p, pk = pb()
                    for k in range(8):
                        MM(p[0:64, 0:NS], wa[:, k, 384 + g * 64:384 + (g + 1) * 64], hT[:, k, t0:t0 + NS], k == 0, k == 7,
                           [gk, ("h", t0)], [pk])
                    CP(vnT[:, g, :], p[0:64, 0:NS], [pk], [("vnT", g)])
                    DMA("sp", vws[l, :, 127, g * 64:(g + 1) * 64].rearrange("b d -> d b"), vnT[:, g, :], [("vnT", g)], (),
                        "vws2", final=True, slow=True)
                    p, pk = pb()
                    for k in range(8):
                        MM(p[0:64, 0:NS], wa[:, k, 256 + g * 64:256 + (g + 1) * 64], hT[:, k, t0:t0 + NS], k == 0, k == 7,
                           [gk, ("h", t0)], [pk])
                    CP(qsb[:, 0:NS], p[0:64, 0:NS], [pk], ["qsb"], eng="act")
                    ACT(sqb[:, 0:NS], p[0:64, 0:NS], AF.Square, [pk], ["sqbA"])
                    p2, pk2 = pb()
                    MM(p2[0:64, 0:NS], onesv[0:64, 1, 0:64], sqb[:, 0:NS], True, True, ["sqbA", "onesv"], [pk2])
                    rstd_from(p2[0:64, 0:NS], pk2, NS, 64, rb, "rbA")
                    STT(knS[:, g, :], qsb[:, 0:NS], kgs[:, l:l + 1], rb[:, 0:NS], ALU.mult, ALU.mult, ["qsb", "rbA"],
                        [("knS", g)])
                    DMA("sp", kws[l, :, 127, g * 64:(g + 1) * 64].rearrange("b d -> d b"), knS[:, g, :], [("knS", g)], (),
                        "kws2", final=True, slow=True)
                pS, pSk = ps[7], ("ps", 7)
                for g in range(2):
                    for hb in range(2):
                        DMA("sp", kcf[:], ck[l, hb * HB:(hb + 1) * HB].rearrange("b k f -> k b f"), (), ["kcf"], "kcf")
                        for q in range(HB // 4):
                            p, pk = ps[q % 4], ("ps", q % 4)
                            for bb in range(4):
                                TR(p[0:64, bb * 128:(bb + 1) * 128], kcf[:, 4 * q + bb, g * 64:(g + 1) * 64], ident[:, :],
                                   ["kcf", "ident"], [pk])
                            b0 = hb * HB + 4 * q
                            CP(kcT[:, b0:b0 + 4, :], p[0:64, :].rearrange("p (a b) -> p a b", a=4), [pk],
                               ["kcT"], eng="act" if q % 2 else "dve")
                    for hh in range(2):
                        h = 2 * g + hh
                        for b in range(NS):
                            MM(pS[:, h * NS + b:h * NS + b + 1], kcT[:, b, :], qn[h][:, t0 + b:t0 + b + 1], True, True,
                               ["kcT", ("qn", h, t0)], [pSk])
                pctr[0] = 0
                ACT(Es, pS[:, 0:4 * NS], AF.Exp, [pSk], ["Es"])
                TT(Ps[:], Es.rearrange("p (h b) -> p h b", h=4), smask[:, :].unsqueeze(2).to_broadcast([128, 4, NS]),
                   ALU.mult, ["Es", "smask"], ["Ps"])
                pO2, pO2k = pb()
                for h in range(4):
                    g = h // 2
                    for b in range(NS):
                        MM(pO2[0:64, h * NS + b:h * NS + b + 1], vcb[:, b, g * 64:(g + 1) * 64], Ps[:, h, b:b + 1], True, True,
                           ["vcb", "Ps"], [pO2k])
                pD2, pD2k = pb()
                MM(pD2[0:64, 0:4 * NS], onesv[:, 4, 0:64], Ps[:].rearrange("p h b -> p (h b)"), True, True,
                   ["Ps", "onesv"], [pD2k])
                pN, pNk = pb()
                for h in range(4):
                    g = h // 2
                    TT(prod, qn[h][:, t0:t0 + NS], kn[g][:, 128 + t0:128 + t0 + NS], ALU.mult,
                       [("qn", h, t0), ("kn", g, t0)], ["prodA"])
                    MM(pN[0:64, h * NS:(h + 1) * NS], onesv[0:64, 4, 0:64], prod, True, True, ["prodA", "onesv"], [pNk])
                ACT(pn[:].rearrange("p h b -> p (h b)"), pN[0:64, 0:4 * NS], AF.Exp, [pNk], ["pn"])
                for h in range(4):
                    g = h // 2
                    TT(num[:, h, :], pn[:, h, :], vnT[:, g, :], ALU.mult, ["pn", ("vnT", g)], ["num"])
                TT(num[:].rearrange("p h b -> p (h b)"), num[:].rearrange("p h b -> p (h b)"), pO2[0:64, 0:4 * NS], ALU.add,
                   ["num", pO2k], ["num"])
                TT(dn[:].rearrange("p h b -> p (h b)"), pn[:].rearrange("p h b -> p (h b)"), pD2[0:64, 0:4 * NS], ALU.add,
                   ["pn", pD2k], ["dn"])
                for h in range(4):
                    TS(dn[:, h, :], dn[:, h, :], esink[:, l * 4 + h:l * 4 + h + 1], ALU.add, ["dn", "esink"], ["dn"])
                RCP(dn[:].rearrange("p h b -> p (h b)"), dn[:].rearrange("p h b -> p (h b)"), ["dn"], ["dn"])
                for h in range(4):
                    TT(oT[0:64, 4 + h, t0:t0 + NS], num[:, h, :], dn[:, h, :], ALU.mult, ["num", "dn"], [("o", 4 + h, t0)])
            S.barrier()

        def conv(si, l, last):
            areset()
            wv3 = W["w_in"][l].rearrange("(k p) f -> p k f", p=128)
            Wg, gk = wg_slot()
            wc = wview(Wg, 0, 8, 512)
            wpw = wview(Wg, 4096, 2, 256)
            Dg = aalloc([2, 31, 128], BF16)
            uT = aalloc([2, 30 + TP], BF16)
            sig = aalloc([2, 512], F32)
            ulast = aalloc([2, 30], F32)
            yb = aalloc([2, 512], F32)
            ybf = aalloc([2, 512], BF16)
            ysq = aalloc([2, 512], BF16)
            msb = aalloc([1, 512], F32)[:, 0, :]
            var = aalloc([1, 512], F32)[:, 0, :]
            rb = aalloc([1, 512], F32)[:, 0, :]
            dd = aalloc([2, 512], F32)
            ysl = aalloc([2, 512], BF16)
            WDMA(wc[:], wv3[:, :, 2048:2560], gk)
            WDMA(wpw[:], W["conv_pw"][l].rearrange("(k p) o -> p k o", p=128), gk)
            if si == 0:
                wcv = wview(Wg, 4608, 1, 256, parts=30)[:, 0, :]
                WDMA(wcv, W["conv_w"][l, 0:30, :], gk)
            for c in range(2):
                for j in range(31):
                    TS(Dg[:, c, j, :], identb[:, :], convw[:, l, c, j:j + 1], ALU.mult, ["identb", "convw"], ["Dg"])
                CP(uT[:, c, 0:30], uh[:, l, c, :], ["uh"], [("uT", c, -1)])

            def ln_epilogue(n, t0):
                pm, pmk = pb()
                pq, pqk = pb()
                for c in range(2):
                    MM(pm[:, 0:n], onesv[:, 3, :], ybf[:, c, 0:n], c == 0, c == 1, [("ybf", c), "onesv"], [pmk])
                for c in range(2):
                    MM(pq[:, 0:n], onesv[:, 3, :], ysq[:, c, 0:n], c == 0, c == 1, [("ysq", c), "onesv"], [pqk])
                CP(msb[:, 0:n], pm[:, 0:n], [pmk], ["msb"], eng="act")
                STT(var[:, 0:n], msb[:, 0:n], -1.0, msb[:, 0:n], ALU.mult, ALU.mult, ["msb"], ["var"])
                TT(var[:, 0:n], var[:, 0:n], pq[:, 0:n], ALU.add, ["var", pqk], ["var"])
                ACT(rb[:, 0:n], var[:, 0:n], AF.Ln, ["var", "epsb"], ["rbC"], bias=epsb[:, 0:1])
                ACT(rb[:, 0:n], rb[:, 0:n], AF.Exp, ["rbC"], ["rbC"], scale=-0.5)
                for c in range(2):
                    TT(dd[:, c, 0:n], yb[:, c, 0:n], msb[:, 0:n], ALU.subtract, [("yb", c), "msb"], [("dd", c)])
                    TT(dd[:, c, 0:n], dd[:, c, 0:n], rb[:, 0:n], ALU.mult, [("dd", c), "rbC"], [("dd", c)])
                    ACT(ysl[:, c, 0:n], dd[:, c, 0:n], AF.Silu, [("dd", c), "lng", "lnb"], [("ysl", c)],
                        bias=lnb[:, l, c:c + 1], scale=lng[:, l, c:c + 1])
                for oc in range(2):
                    p, pk = pb()
                    for c in range(2):
                        MM(p[:, 0:n], wpw[:, c, oc * 128:(oc + 1) * 128], ysl[:, c, 0:n], c == 0, c == 1,
                           [gk, ("ysl", c)], [pk])
                    CP(oT[:, 8 + oc, t0:t0 + n], p[:, 0:n], [pk], [("o", 8 + oc, t0)], eng="act" if oc else "dve")

            uS = aalloc([2, NS], F32)
            for (t0, n) in tiles_of(si):
                for c in range(2):
                    pa, pak = pb()
                    pg, pgk = pb()
                    for k in range(8):
                        MM(pa[:, 0:n], wc[:, k, c * 128:(c + 1) * 128], hT[:, k, t0:t0 + n], k == 0, k == 7,
                           [gk, ("h", t0)], [pak])
                    for k in range(8):
                        MM(pg[:, 0:n], wc[:, k, 256 + c * 128:256 + (c + 1) * 128], hT[:, k, t0:t0 + n], k == 0, k == 7,
                           [gk, ("h", t0)], [pgk])
                    ACT(sig[:, c, 0:n], pg[:, 0:n], AF.Sigmoid, [pgk], [("sig", c)])
                    if t0 < TP:
                        TT(uT[:, c, 30 + t0:30 + t0 + n], pa[:, 0:n], sig[:, c, 0:n], ALU.mult, [pak, ("sig", c)],
                           [("uT", c, t0)])
                        if t0 == 512:
                            TT(ulast[:, c, :], pa[:, 482:512], sig[:, c, 482:512], ALU.mult, [pak, ("sig", c)],
                               [("ulast", c)])
                            if last:
                                DMA("sp", cvp[l, :, c * 128:(c + 1) * 128].rearrange("t p -> p t"), ulast[:, c, :],
                                    [("ulast", c)], (), "cvp", final=True, slow=True)
                            CP(uh[:, l, c, :], uT[:, c, TP:TP + 30], [("uT", c, 512)], ["uh"])
                    else:
                        TT(uS[:, c, :], pa[:, 0:n], sig[:, c, 0:n], ALU.mult, [pak, ("sig", c)], [("uS", c)])
            wst = wout_prepare(l)
            taps = {}
            for ti in range(2):
                t0 = ti * 512
                for c in range(2):
                    p, pk = pb()
                    rk = [("uT", c, t0), ("uT", c, t0 - 512 if t0 else -1), "Dg"]
                    for j in range(31):
                        MM(p[:, :], Dg[:, c, j, :], uT[:, c, t0 + j:t0 + j + 512], j == 0, j == 30, rk, [pk])
                    taps[(ti, c)] = (p, pk)
            def evac(ti):
                for c in range(2):
                    p, pk = taps[(ti, c)]
                    ACT(yb[:, c, :], p[:, :], AF.Identity, [pk, "convb"], [("yb", c)], bias=convb[:, l, c:c + 1])
                    ACT(ysq[:, c, :], p[:, :], AF.Square, [pk, "convb"], [("ysq", c)], bias=convb[:, l, c:c + 1])
                    CP(ybf[:, c, :], yb[:, c, :], [("yb", c)], [("ybf", c)])

            evac(0)
            ln_epilogue(512, 0)
            evac(1)
            wout_tile(wst, 0, 512)
            ln_epilogue(512, 512)
            if si == 0:
                t0 = TP
                cbb = aalloc([NS, 256], BF16, parts=30)
                msk = aalloc([4, 128], F32)
                y0 = aalloc([2, NS], F32)
                DMA("pool", cbb[:], sconv[l].rearrange("b j c -> j b c"), (), ["cbb"], "cbb")
                DMA("sp", cvs[l, :, 0:29, :], sconv[l, :, 1:30, :], (), (), "cvs", final=True)
                for c in range(2):
                    DMA("sp", cvs[l, :, 29, c * 128:(c + 1) * 128].rearrange("b p -> p b"), uS[:, c, :], [("uS", c)], (),
                        "cvs2", final=True, slow=True)
                    for q in range(NS // 4):
                        p, pk = pb()
                        MM(p[:, :], wcv[:, c * 128:(c + 1) * 128], cbb[:, 4 * q:4 * q + 4, c * 128:(c + 1) * 128], True, True,
                           [gk, "cbb"], [pk])
                        TT(msk[:], p[:, :].rearrange("p (a b) -> p a b", a=4),
                           ident[:, :].unsqueeze(1).to_broadcast([128, 4, 128]), ALU.mult, [pk, "ident"], ["msk"])
                        S.op("dve", (lambda o_, i_: (lambda e: e.reduce_sum(o_, i_, axis=AX.X)))(y0[:, c, 4 * q:4 * q + 4],
                                                                                                  msk[:]),
                             ["msk"], [("y0", c)])
                    STT(yb[:, c, 0:NS], uS[:, c, :], convw[:, l, c, 30:31], y0[:, c, :], ALU.mult, ALU.add,
                        [("uS", c), ("y0", c), "convw"], [("yb", c)])
                    TS(yb[:, c, 0:NS], yb[:, c, 0:NS], convb[:, l, c:c + 1], ALU.add, [("yb", c), "convb"], [("yb", c)])
                    CP(ybf[:, c, 0:NS], yb[:, c, 0:NS], [("yb", c)], [("ybf", c)])
                    ACT(ysq[:, c, 0:NS], yb[:, c, 0:NS], AF.Square, [("yb", c)], [("ysq", c)])
                ln_epilogue(NS, t0)
            S.barrier()
            wout_tile(wst, 512, 512)
            if si == 0:
                wout_tile(wst, TP, NS)

        def wout_prepare(l):
            Wg, gk = wg_slot()
            Wd, dk = wd_slot()
            wo8 = wview(Wg, 0, 8, D)
            wo2 = wview(Wd, 0, 2, D)
            wod = W["w_out"][l]
            WDMA(wo8[:, 0:4, :], wod[0:512, :].rearrange("(k p) d -> p k d", p=128), gk)
            WDMA(wo8[0:64, 4:8, :], wod[512:768, :].rearrange("(k p) d -> p k d", p=64), gk)
            WDMA(wo2[:, :, :], wod[768:1024, :].rearrange("(k p) d -> p k d", p=128), dk)
            return wo8, wo2, gk, dk

        def wout_tile(wst, t0, n):
            wo8, wo2, gk, dk = wst
            for c in range(8):
                p, pk = pb()
                for k in range(10):
                    kp = 64 if 4 <= k < 8 else 128
                    wsrc = wo8[0:kp, k, c * 128:(c + 1) * 128] if k < 8 else wo2[:, k - 8, c * 128:(c + 1) * 128]
                    MM(p[:, 0:n], wsrc, oT[0:kp, k, t0:t0 + n], k == 0, k == 9, [gk, dk, ("o", k, t0)], [pk])
                TT(xT[:, c, t0:t0 + n], xT[:, c, t0:t0 + n], p[:, 0:n], ALU.add, [pk, ("x", c, t0)], [("x", c, t0)])

        PH = DBG.get("phases")

        def on(name):
            return PH is None or name in PH

        for si in range(DBG.get("nseg", NSEG)):
            last = si == DBG.get("nseg", NSEG) - 1
            load_x(si)
            for l in range(DBG.get("depth", DEPTH)):
                if on("ffn1"):
                    ffn(si, l, 1)
                if on("mixnorm"):
                    areset()
                    rmsnorm(si, mng, l)
                    S.barrier()
                if on("ret"):
                    for pr in range(2):
                        retention(si, l, pr, last)
                if on("att"):
                    attention(si, l, last)
                if on("conv"):
                    conv(si, l, last)
                if on("ffn2"):
                    ffn(si, l, 2)
            store_y(si)
        S.emit()
        print("sched stats", S.stats, flush=True)
    return nc


_CACHE = {}


def kernel(**inputs):
    f32 = lambda a: np.ascontiguousarray(np.asarray(a), dtype=np.float32)
    inp = {k: f32(v) for k, v in inputs.items()}
    if "nc" not in _CACHE:
        _CACHE["nc"] = build()
    nc = _CACHE["nc"]
    ctab, _ = const_tables()
    wnames = ["ffn1_norm", "ffn1_wg", "ffn1_wu", "ffn1_wd", "mix_norm", "w_in", "ret_norm_g", "q_norm_g", "k_norm_g",
              "sinks", "conv_w", "conv_b", "conv_ln_g", "conv_ln_b", "conv_pw", "w_out", "ffn2_norm", "ffn2_wg",
              "ffn2_wu", "ffn2_wd"]
    in_maps = []
    zx = np.zeros((NSEG * TP, D), np.float32)
    for core in range(NCORES):
        m = {k: inp[k] for k in wnames}
        m.update(ctab)
        if core in ACT_CORES:
            c = ACT_CORES.index(core)
            sl = slice(c * NS, (c + 1) * NS)
            m["xp"] = inp["x_prompt"][c]
            m["xs"] = np.ascontiguousarray(inp["x_sample"][sl, 0, :])
            m["sret"] = np.ascontiguousarray(inp["state_ret"][:, sl])
            m["ck"] = np.ascontiguousarray(inp["cache_k_win"][:, sl].reshape(DEPTH, NS, 128, 128))
            m["cv"] = np.ascontiguousarray(inp["cache_v_win"][:, sl].reshape(DEPTH, NS, 128, 128))
            m["sconv"] = np.ascontiguousarray(inp["state_conv"][:, sl])
        else:
            m["xp"] = zx
            m["xs"] = np.zeros((NS, D), np.float32)
            m["sret"] = np.zeros((DEPTH, NS, 4, 64, 128), np.float32)
            m["ck"] = np.zeros((DEPTH, NS, 128, 128), np.float32)
            m["cv"] = np.zeros((DEPTH, NS, 128, 128), np.float32)
            m["sconv"] = np.zeros((DEPTH, NS, 30, 256), np.float32)
        in_maps.append(m)
    res = run_bass_kernel_spmd(nc, in_maps, core_ids=list(range(NCORES)))
    R = res.results
    A = ACT_CORES
    y_p = np.stack([R[c]["yp"] for c in A]).reshape(BATCH, SEQ, D)
    y_s = np.concatenate([R[c]["ys"] for c in A]).reshape(DECB, 1, D)
    ret_p = np.stack([R[c]["retp"] for c in A], axis=1)
    ret_s = np.concatenate([R[c]["rets"] for c in A], axis=1)
    kw_p = np.stack([R[c]["kwp"] for c in A], axis=1).reshape(DEPTH, BATCH, 128, 2, 64)
    kw_s = np.concatenate([R[c]["kws"] for c in A], axis=1).reshape(DEPTH, DECB, 128, 2, 64)
    vw_p = np.stack([R[c]["vwp"] for c in A], axis=1).reshape(DEPTH, BATCH, 128, 2, 64)
    vw_s = np.concatenate([R[c]["vws"] for c in A], axis=1).reshape(DEPTH, DECB, 128, 2, 64)
    cv_p = np.stack([R[c]["cvp"] for c in A], axis=1)
    cv_s = np.concatenate([R[c]["cvs"] for c in A], axis=1)
    outs = (y_p, y_s, ret_p, ret_s, kw_p, kw_s, vw_p, vw_s, cv_p, cv_s)
    return tuple(np.ascontiguousarray(o, dtype=np.float32) for o in outs)
```

```python
import numpy as np
from contextlib import ExitStack
import concourse.bass as bass
import concourse.mybir as mybir
from concourse.bass_utils import run_bass_kernel_spmd

F32 = mybir.dt.float32
BF16 = mybir.dt.bfloat16
ALU = mybir.AluOpType
AF = mybir.ActivationFunctionType
AX = mybir.AxisListType

D = 1024
FF = 2816
DEPTH = 2
SEQ = 4096
BATCH = 4
DECB = 128
IN_DIM = 2560
EPS = 1e-6
NCORES = 8
ACTIVE = 4
NSEG = 4
TP = 1024
NS = DECB // ACTIVE
NT = TP + NS
SEM_LIMIT = 30000
ACT_CORES = [0, 2, 4, 6]
DBG = {}


class Sched:
    ENGS = ("pe", "act", "dve", "pool", "sp")

    def __init__(self, nc, es):
        self.nc = nc
        self.es = es
        self.ops = []
        self.last_write = {}
        self.readers = {}
        self.out_dma_ops = []
        self.bar_deps = set()
        self.bar_pending = set()

    def op(self, eng, fn, reads=(), writes=(), dma=False, semkey=None, final=False, nobar=False):
        idx = len(self.ops)
        raw = set()
        other = set()
        for k in reads:
            w = self.last_write.get(k)
            if w is not None:
                raw.add(w)
        for k in writes:
            w = self.last_write.get(k)
            if w is not None:
                other.add(w)
            for r in self.readers.get(k, {}).values():
                other.add(r)
        for k in writes:
            self.last_write[k] = idx
            self.readers[k] = {}
        rk = ("dma", semkey) if dma else eng
        for k in reads:
            self.readers.setdefault(k, {})[rk] = idx
        if eng in self.bar_pending and not nobar and eng != "pe":
            if any(not self.persistent(k) for k in list(reads) + list(writes)):
                other |= self.bar_deps
                self.bar_pending.discard(eng)
        deps = set()
        for d in raw | other:
            if d == idx:
                continue
            p = self.ops[d]
            if (not p["dma"]) and p["eng"] == eng:
                if eng == "pe" or d not in raw:
                    continue
            if nobar and dma and p["dma"] and p["semkey"] == semkey:
                continue
            deps.add(d)
        self.ops.append(dict(eng=eng, fn=fn, deps=deps, dma=dma, semkey=semkey, sig=False, ev=None))
        if final:
            self.out_dma_ops.append(idx)
        return idx

    def barrier(self):
        last = {}
        for i, o in enumerate(self.ops):
            if o["fn"] is None:
                continue
            if o["dma"]:
                last[("dma", o["semkey"])] = i
            else:
                last[o["eng"]] = i
        self.bar_deps = set(last.values())
        self.bar_pending = set(self.ENGS)
        self.last_write = {k: v for k, v in self.last_write.items() if self.persistent(k)}
        self.readers = {k: v for k, v in self.readers.items() if self.persistent(k)}

    PERSIST = frozenset(("x", "h", "o", "ps", "WG", "WD", "stage", "Sst", "kTh", "Vh", "uh", "ident", "identb", "onesv",
                         "epsb", "rdm", "qdec", "kdec", "g128", "g1", "amask", "smask", "eyeN", "hsel", "n1g", "mng",
                         "n2g", "retg", "qg8", "kgs", "esink", "convw", "convb", "lng", "lnb", "oT"))

    def persistent(self, k):
        return (k[0] if isinstance(k, tuple) else k) in self.PERSIST

    def emit(self):
        nc = self.nc
        ops = self.ops
        if self.out_dma_ops:
            ops.append(dict(eng="sp", fn=None, deps=set(self.out_dma_ops), dma=False, semkey=None,
                            sig=False, ev=None))
        for o in ops:
            for d in o["deps"]:
                ops[d]["sig"] = True
        eng_sems = {e: [] for e in self.ENGS}
        eng_cnt = {e: 0 for e in self.ENGS}
        dma_sems = {}
        dma_cnt = {}
        nsem = [0]

        def new_sem():
            nsem[0] += 1
            return self.es.enter_context(nc.semaphore("s%d" % nsem[0]))

        for o in ops:
            if o["fn"] is None:
                continue
            if o["dma"]:
                k = o["semkey"]
                if k not in dma_sems or dma_cnt[k] + 16 >= SEM_LIMIT:
                    dma_sems[k] = new_sem()
                    dma_cnt[k] = 0
                dma_cnt[k] += 16
                o["ev"] = (dma_sems[k], dma_cnt[k])
            elif o["sig"]:
                e = o["eng"]
                if not eng_sems[e] or eng_cnt[e] >= SEM_LIMIT:
                    eng_sems[e].append(new_sem())
                    eng_cnt[e] = 0
                eng_cnt[e] += 1
                o["ev"] = (eng_sems[e][-1], eng_cnt[e])
        nwaits = [0]

        def run_engine(e, eng):
            seen = {}
            for o in ops:
                if o["eng"] != e:
                    continue
                need = {}
                for d in o["deps"]:
                    sem, val = ops[d]["ev"]
                    key = id(sem)
                    if seen.get(key, 0) >= val:
                        continue
                    if key not in need or need[key][1] < val:
                        need[key] = (sem, val)
                for key, (sem, val) in need.items():
                    eng.wait_ge(sem, val)
                    seen[key] = val
                    nwaits[0] += 1
                if o["fn"] is None:
                    continue
                ins = o["fn"](eng)
                if o["ev"] is not None:
                    ins.then_inc(o["ev"][0], 16 if o["dma"] else 1)

        with nc.Block() as block:
            @block.tensor
            def _(eng):
                run_engine("pe", eng)

            @block.scalar
            def _(eng):
                run_engine("act", eng)

            @block.vector
            def _(eng):
                run_engine("dve", eng)

            @block.gpsimd
            def _(eng):
                run_engine("pool", eng)

            @block.sync
            def _(eng):
                run_engine("sp", eng)
        self.stats = dict(n_ops=len(ops), n_waits=nwaits[0], n_sems=nsem[0])


def const_tables():
    lg = np.log1p(-np.exp2(-5.0 - np.arange(4, dtype=np.float64)))
    slopes = np.exp2(-8.0 * (np.arange(4, dtype=np.float64) + 1.0) / 4.0)
    j = np.arange(128)[:, None].astype(np.float64)
    i = np.arange(128)[None, :].astype(np.float64)
    t = {}
    rdm = np.zeros((128, 4, 128), np.float64)
    for h in range(4):
        rdm[:, h, :] = np.where(i >= j, np.exp(lg[h] * np.maximum(i - j, 0.0)), 0.0)
    t["c_rdm"] = rdm
    qdec = np.zeros((128, 2, 128), np.float64)
    kdec = np.zeros((128, 4), np.float64)
    g128 = np.zeros((128, 2), np.float64)
    g1 = np.zeros((128, 2), np.float64)
    for pr in range(2):
        for hh in range(2):
            h = 2 * pr + hh
            qdec[64 * hh:64 * hh + 64, pr, :] = np.exp(lg[h] * (np.arange(128) + 1.0))[None, :]
            g128[64 * hh:64 * hh + 64, pr] = np.exp(lg[h] * 128.0)
            g1[64 * hh:64 * hh + 64, pr] = np.exp(lg[h])
    for h in range(4):
        kdec[:, h] = np.exp(lg[h] * (127.0 - np.arange(128))) * 0.125
    t["c_qdec"] = qdec
    t["c_kdec"] = kdec
    t["c_g128"] = g128
    t["c_g1"] = g1
    am = np.zeros((128, 2, 2, 2, 128), np.float64)
    for g in range(2):
        for hh in range(2):
            h = 2 * g + hh
            dist_prev = 128.0 + i - j
            am[:, g, hh, 0, :] = np.where(j > i, np.exp(-slopes[h] * dist_prev), 0.0)
            am[:, g, hh, 1, :] = np.where(i >= j, np.exp(-slopes[h] * (i - j)), 0.0)
    t["c_amask"] = am
    sm = np.zeros((128, 4), np.float64)
    p = np.arange(128, dtype=np.float64)
    for h in range(4):
        sm[:, h] = np.where(p >= 1, np.exp(-slopes[h] * (128.0 - p)), 0.0)
    t["c_smask"] = sm
    t["c_eye"] = np.eye(NS)
    hs = np.zeros((128, 2, 128), np.float64)
    hs[0:64, 0, :] = 1.0
    hs[64:128, 1, :] = 1.0
    t["c_hsel"] = hs
    return {k: np.ascontiguousarray(v, dtype=np.float32) for k, v in t.items()}, lg


def build():
    ctab, lg = const_tables()
    gam = [float(np.exp(x)) for x in lg]
    nc = bass.Bass("TRN2", target_bir_lowering=False)

    def din(name, shape):
        return nc.dram_tensor(name, list(shape), F32, kind="ExternalInput").ap()

    def dout(name, shape):
        return nc.dram_tensor(name, list(shape), F32, kind="ExternalOutput").ap()

    xp = din("xp", (NSEG * TP, D))
    xs = din("xs", (NS, D))
    sret = din("sret", (DEPTH, NS, 4, 64, 128))
    ck = din("ck", (DEPTH, NS, 128, 128))
    cv = din("cv", (DEPTH, NS, 128, 128))
    sconv = din("sconv", (DEPTH, NS, 30, 256))
    W = {}
    for nm, shp in (("ffn1_norm", (DEPTH, D)), ("ffn1_wg", (DEPTH, D, FF)), ("ffn1_wu", (DEPTH, D, FF)),
                    ("ffn1_wd", (DEPTH, FF, D)), ("mix_norm", (DEPTH, D)), ("w_in", (DEPTH, D, IN_DIM)),
                    ("ret_norm_g", (DEPTH, 4, 128)), ("q_norm_g", (DEPTH, 64)), ("k_norm_g", (DEPTH, 64)),
                    ("sinks", (DEPTH, 4)), ("conv_w", (DEPTH, 31, 256)), ("conv_b", (DEPTH, 256)),
                    ("conv_ln_g", (DEPTH, 256)), ("conv_ln_b", (DEPTH, 256)), ("conv_pw", (DEPTH, 256, 256)),
                    ("w_out", (DEPTH, D, D)), ("ffn2_norm", (DEPTH, D)), ("ffn2_wg", (DEPTH, D, FF)),
                    ("ffn2_wu", (DEPTH, D, FF)), ("ffn2_wd", (DEPTH, FF, D))):
        W[nm] = din(nm, shp)
    C = {k: din(k, v.shape) for k, v in ctab.items()}
    yp = dout("yp", (NSEG * TP, D))
    ys = dout("ys", (NS, D))
    retp = dout("retp", (DEPTH, 4, 64, 128))
    rets = dout("rets", (DEPTH, NS, 4, 64, 128))
    kwp = dout("kwp", (DEPTH, 128, 128))
    kws = dout("kws", (DEPTH, NS, 128, 128))
    vwp = dout("vwp", (DEPTH, 128, 128))
    vws = dout("vws", (DEPTH, NS, 128, 128))
    cvp = dout("cvp", (DEPTH, 30, 256))
    cvs = dout("cvs", (DEPTH, NS, 30, 256))

    es = ExitStack()
    with es:
        def sb(name, shape, dt):
            return es.enter_context(nc.sbuf_tensor(name, list(shape), dt))

        S = Sched(nc, es)
        ps = [es.enter_context(nc.psum_tensor("ps%d" % i, [128, 512], F32)) for i in range(8)]
        pctr = [0]

        def pb():
            i = pctr[0] % 8
            pctr[0] += 1
            return ps[i], ("ps", i)

        def MM(out, lhsT, rhs, st, sp, r, w):
            S.op("pe", lambda e: e.matmul(out, lhsT=lhsT, rhs=rhs, start=st, stop=sp), r, w)

        def TR(out, in_, idn, r, w):
            S.op("pe", lambda e: e.transpose(out, in_, idn), r, w)

        def ACT(out, in_, func, r, w, bias=None, scale=1.0):
            if bias is None:
                S.op("act", lambda e: e.activation(out=out, in_=in_, func=func, scale=scale), r, w)
            else:
                S.op("act", lambda e: e.activation(out=out, in_=in_, func=func, bias=bias, scale=scale), r, w)

        def STT(out, in0, scalar, in1, op0, op1, r, w, eng="dve"):
            S.op(eng, lambda e: e.scalar_tensor_tensor(out=out, in0=in0, scalar=scalar, in1=in1, op0=op0, op1=op1), r, w)

        def TT(out, in0, in1, op, r, w, eng="dve"):
            S.op(eng, lambda e: e.tensor_tensor(out=out, in0=in0, in1=in1, op=op), r, w)

        def TS(out, in0, s1, op0, r, w, s2=None, op1=None, eng="dve"):
            if op1 is None:
                S.op(eng, lambda e: e.tensor_scalar(out=out, in0=in0, scalar1=s1, scalar2=None, op0=op0), r, w)
            else:
                S.op(eng, lambda e: e.tensor_scalar(out=out, in0=in0, scalar1=s1, scalar2=s2, op0=op0, op1=op1), r, w)

        def CP(out, in_, r, w, eng="dve"):
            if DBG.get("noactcp") and eng == "act":
                eng = "dve"
            if eng == "act":
                S.op("act", lambda e: e.activation(out=out, in_=in_, func=AF.Identity), r, w)
            else:
                S.op(eng, lambda e: e.tensor_copy(out, in_), r, w)

        def RCP(out, in_, r, w):
            S.op("dve", lambda e: e.reciprocal(out, in_), r, w)

        def MSET(ap, val, w, eng="dve"):
            S.op(eng, lambda e: e.memset(ap, val), (), w)

        def DMA(q, out, in_, r, w, key, final=False, slow=False):
            if slow:
                S.op(q, lambda e: e.dma_start(out=out, in_=in_, allow_slow_non_contiguous=True), r, w, dma=True,
                     semkey=key, final=final)
            else:
                S.op(q, lambda e: e.dma_start(out=out, in_=in_), r, w, dma=True, semkey=key, final=final)

        xT = sb("xT", [128, 8, NT], F32)
        hT = sb("hT", [128, 8, NT], BF16)
        oT = sb("oT", [128, 10, NT], BF16)
        ident = sb("ident", [128, 128], F32)
        identb = sb("identb", [128, 128], BF16)
        onesv = sb("onesv", [128, 5, 128], BF16)
        epsb = sb("epsb", [128, 1], F32)
        stage = sb("stage", [128, 2, D], F32)
        rdm = sb("rdm", [128, 4, 128], F32)
        qdec = sb("qdec", [128, 2, 128], F32)
        kdec = sb("kdec", [128, 4], F32)
        g128 = sb("g128", [128, 2], F32)
        g1 = sb("g1", [128, 2], F32)
        amask = sb("amask", [128, 2, 512], BF16)
        smask = sb("smask", [128, 4], F32)
        eyeN = sb("eyeN", [NS, NS], BF16)
        hsel = sb("hsel", [128, 2, 128], BF16)
        n1g = sb("n1g", [128, DEPTH, 8], F32)
        mng = sb("mng", [128, DEPTH, 8], F32)
        n2g = sb("n2g", [128, DEPTH, 8], F32)
        retg = sb("retg", [128, DEPTH, 4], F32)
        qg8 = sb("qg8", [64, DEPTH], F32)
        kgs = sb("kgs", [64, DEPTH], F32)
        esink = sb("esink", [64, DEPTH * 4], F32)
        convw = sb("convw", [128, DEPTH, 2, 31], F32)
        convb = sb("convb", [128, DEPTH, 2], F32)
        lng = sb("lng", [128, DEPTH, 2], F32)
        lnb = sb("lnb", [128, DEPTH, 2], F32)
        Sst = sb("Sst", [128, DEPTH, 2, 128], F32)
        kTh = sb("kTh", [64, DEPTH, 2, 128], BF16)
        Vh = sb("Vh", [128, DEPTH, 128], BF16)
        uh = sb("uh", [128, DEPTH, 2, 30], BF16)
        NA = 16640
        arena = sb("arena", [128, NA], F32)
        aoff = [0]
        WG = [sb("WG%d" % i, [128, 8192], BF16) for i in range(2)]
        WD = [sb("WD%d" % i, [128, 4096], BF16) for i in range(2)]
        wgc = [0]
        wdc = [0]

        def wg_slot():
            i = wgc[0] % 2
            wgc[0] += 1
            return WG[i], ("WG", i)

        def wd_slot():
            i = wdc[0] % 2
            wdc[0] += 1
            return WD[i], ("WD", i)

        def wview(Wt, off, a, b, parts=128):
            return Wt[0:parts, off:off + a * b].rearrange("p (a b) -> p a b", a=a, b=b)

        def WDMA(out, in_, key):
            S.op("pool", lambda e: e.dma_start(out=out, in_=in_), (), [key], dma=True, semkey=key, nobar=True)

        def areset():
            aoff[0] = 0

        def aalloc(shape, dt, parts=128):
            n = int(np.prod(shape))
            words = (n + 1) // 2 if dt == BF16 else n
            o = aoff[0]
            aoff[0] += words
            assert aoff[0] <= NA, ("arena overflow", aoff[0])
            a = arena[0:parts, o:o + words]
            if dt == BF16:
                a = a.bitcast(BF16)
                if n % 2:
                    a = a[:, 0:n]
            if len(shape) == 2:
                return a.rearrange("p (a b) -> p a b", a=shape[0], b=shape[1])
            if len(shape) == 3:
                return a.rearrange("p (a b c) -> p a b c", a=shape[0], b=shape[1], c=shape[2])
            return a

        MSET(ident[:], 0.0, ["ident"], eng="pool")
        S.op("pool", lambda e: e.affine_select(out=ident[:], in_=ident[:], pattern=[[-1, 128]],
                                               compare_op=ALU.not_equal, fill=1.0, base=0, channel_multiplier=1),
             ["ident"], ["ident"])
        CP(identb[:], ident[:], ["ident"], ["identb"])
        for i, v in enumerate((1.0 / 1024, 1.0 / 64, 1.0 / 128, 1.0 / 256, 1.0)):
            MSET(onesv[:, i, :], v, ["onesv"])
        MSET(epsb[:], EPS, ["epsb"])
        MSET(Sst[:], 0.0, ["Sst"])
        MSET(kTh[:], 0.0, ["kTh"])
        MSET(Vh[:], 0.0, ["Vh"])
        MSET(uh[:], 0.0, ["uh"])
        if DBG.get("zero_o"):
            MSET(oT[:], 0.0, ["oT"])
        DMA("sp", rdm[:], C["c_rdm"], (), ["rdm"], "c_rdm")
        DMA("sp", qdec[:], C["c_qdec"], (), ["qdec"], "c_qdec")
        DMA("sp", kdec[:], C["c_kdec"], (), ["kdec"], "c_kdec")
        DMA("sp", g128[:], C["c_g128"], (), ["g128"], "c_g128")
        DMA("sp", g1[:], C["c_g1"], (), ["g1"], "c_g1")
        DMA("sp", smask[:], C["c_smask"], (), ["smask"], "c_smask")
        if not DBG.get("nopoolc"):
          DMA("pool", amask[:], C["c_amask"].rearrange("p a b c d -> p a (b c d)"), (), ["amask"], "c_amask")
          DMA("pool", eyeN[:], C["c_eye"], (), ["eyeN"], "c_eyeN")
          DMA("pool", hsel[:], C["c_hsel"], (), ["hsel"], "c_hsel")
        for l in range(DEPTH if not DBG.get("noparams") else 0):
            DMA("sp", n1g[:, l, :], W["ffn1_norm"][l].rearrange("(c p) -> p c", p=128), (), ["n1g"], "c_n1g", slow=True)
            DMA("sp", mng[:, l, :], W["mix_norm"][l].rearrange("(c p) -> p c", p=128), (), ["mng"], "c_mng", slow=True)
            DMA("sp", n2g[:, l, :], W["ffn2_norm"][l].rearrange("(c p) -> p c", p=128), (), ["n2g"], "c_n2g", slow=True)
            DMA("sp", retg[:, l, :], W["ret_norm_g"][l].rearrange("h p -> p h"), (), ["retg"], "c_retg", slow=True)
            DMA("sp", qg8[:, l:l + 1], W["q_norm_g"][l].rearrange("(p o) -> p o", o=1), (), ["qg8"], "c_qg8", slow=True)
            DMA("sp", kgs[:, l:l + 1], W["k_norm_g"][l].rearrange("(p o) -> p o", o=1), (), ["kgs"], "c_kgs", slow=True)
            DMA("sp", esink[:, l * 4:(l + 1) * 4], W["sinks"][l:l + 1, :].partition_broadcast(64), (), ["esink"],
                "c_esink", slow=True)
            for c in range(2):
                DMA("sp", convw[:, l, c, :], W["conv_w"][l][:, c * 128:(c + 1) * 128].rearrange("j p -> p j"), (),
                    ["convw"], "c_convw", slow=True)
            DMA("sp", convb[:, l, :], W["conv_b"][l].rearrange("(c p) -> p c", p=128), (), ["convb"], "c_convb", slow=True)
            DMA("sp", lng[:, l, :], W["conv_ln_g"][l].rearrange("(c p) -> p c", p=128), (), ["lng"], "c_lng", slow=True)
            DMA("sp", lnb[:, l, :], W["conv_ln_b"][l].rearrange("(c p) -> p c", p=128), (), ["lnb"], "c_lnb", slow=True)
        S.barrier()
        TS(qg8[:], qg8[:], 0.125, ALU.mult, ["qg8"], ["qg8"])
        ACT(esink[:], esink[:], AF.Exp, ["esink"], ["esink"])
        S.barrier()

        def tiles_of(si):
            t = [(0, 512), (512, 512)]
            if si == 0:
                t.append((TP, NS))
            return t

        def load_x(si):
            chunks = [(xp[si * TP + n * 128: si * TP + (n + 1) * 128, :], n * 128, 128) for n in range(TP // 128)]
            if si == 0 and not DBG.get("nosample"):
                chunks.append((xs[:, :], TP, NS))
            if DBG.get("noload"):
                chunks = []
            for ci, (src, t0, n) in enumerate(chunks):
                sl = ci % 2
                DMA("sp", stage[0:n, sl, :], src, (), [("stage", sl)], ("stage", sl))
                for half in range(2):
                    p, pk = pb()
                    for q in range(4):
                        c = half * 4 + q
                        TR(p[:, q * 128:q * 128 + n], stage[0:n, sl, c * 128:(c + 1) * 128], ident[0:n, 0:n],
                           [("stage", sl), "ident"], [pk])
                    for q in range(4):
                        c = half * 4 + q
                        CP(xT[:, c, t0:t0 + n], p[:, q * 128:q * 128 + n], [pk],
                           [("x", c, (t0 // 512) * 512 if t0 < TP else TP)], eng="act" if half else "dve")

        def store_y(si):
            chunks = [(yp[si * TP + n * 128: si * TP + (n + 1) * 128, :], n * 128, 128) for n in range(TP // 128)]
            if si == 0 and not DBG.get("nosample"):
                chunks.append((ys[:, :], TP, NS))
            if DBG.get("nostore"):
                chunks = chunks[:1]
            for ci, (dst, t0, n) in enumerate(chunks):
                sl = ci % 2
                tk = (t0 // 512) * 512 if t0 < TP else TP
                for half in range(2):
                    p, pk = pb()
                    for q in range(4):
                        c = half * 4 + q
                        TR(p[0:n, q * 128:(q + 1) * 128], xT[:, c, t0:t0 + n], ident[:, :],
                           [("x", c, tk), "ident"], [pk])
                    CP(stage[0:n, sl, half * 512:(half + 1) * 512], p[0:n, :], [pk], [("stage", sl)],
                       eng="act" if half else "dve")
                DMA("sp", dst, stage[0:n, sl, :], [("stage", sl)], (), ("stage", sl), final=True)

        def xkey(c, t0):
            return ("x", c, t0)

        def xkeys(c, t0, n):
            return [("x", c, t0)]

        def rstd_from(psmean, pk, n, parts, rbuf, rk):
            ACT(rbuf[0:parts, 0:n], psmean, AF.Ln, [pk, "epsb"], [rk], bias=epsb[0:parts, 0:1])
            ACT(rbuf[0:parts, 0:n], rbuf[0:parts, 0:n], AF.Exp, [rk], [rk], scale=-0.5)

        def rmsnorm(si, gain, l):
            sq = aalloc([8, 512], BF16)
            rb = aalloc([2, 512], F32)
            for ti, (t0, n) in enumerate(tiles_of(si)):
                p, pk = pb()
                for c in range(8):
                    ACT(sq[:, c, 0:n], xT[:, c, t0:t0 + n], AF.Square, [("x", c, t0)], [("sq", c)])
                for c in range(8):
                    MM(p[:, 0:n], onesv[:, 0, :], sq[:, c, 0:n], c == 0, c == 7, [("sq", c), "onesv"], [pk])
                rk = ("rstd", ti % 2)
                rstd_from(p[:, 0:n], pk, n, 128, rb[:, ti % 2, :], rk)
                for c in range(8):
                    STT(hT[:, c, t0:t0 + n], xT[:, c, t0:t0 + n], gain[:, l, c:c + 1], rb[:, ti % 2, 0:n],
                        ALU.mult, ALU.mult, [("x", c, t0), rk], [("h", t0)])

        def fix_xkeys(si):
            pass

        def ffn(si, l, which):
            areset()
            wg_d = W["ffn%d_wg" % which][l]
            wu_d = W["ffn%d_wu" % which][l]
            wd_d = W["ffn%d_wd" % which][l]
            gain = n1g if which == 1 else n2g
            act = [aalloc([4, NT], BF16) for _ in range(2)]
            sg = aalloc([2, 512], F32)
            rmsnorm(si, gain, l)
            groups = [(g * 512, 4) for g in range(5)] + [(2560, 2)]
            tl = tiles_of(si)
            gslots = {}

            def gateup(j):
                f0, nfc = groups[j]
                sl = j % 2
                Wg, gk = wg_slot()
                Wd, dk = wd_slot()
                wgv = wview(Wg, 0, 8, 512)
                wuv = wview(Wg, 4096, 8, 512)
                wdv = wview(Wd, 0, 4, D)
                gslots[j] = (wdv, dk)
                WDMA(wgv[:, :, 0:nfc * 128], wg_d[:, f0:f0 + nfc * 128].rearrange("(k p) f -> p k f", p=128), gk)
                WDMA(wuv[:, :, 0:nfc * 128], wu_d[:, f0:f0 + nfc * 128].rearrange("(k p) f -> p k f", p=128), gk)
                WDMA(wdv[:, 0:nfc, :], wd_d[f0:f0 + nfc * 128, :].rearrange("(k p) d -> p k d", p=128), dk)
                cnt = 0
                for (t0, n) in tl:
                    for fc in range(nfc):
                        pg, pgk = pb()
                        pu, puk = pb()
                        for k in range(8):
                            MM(pg[:, 0:n], wgv[:, k, fc * 128:(fc + 1) * 128], hT[:, k, t0:t0 + n], k == 0, k == 7,
                               [gk, ("h", t0)], [pgk])
                        for k in range(8):
                            MM(pu[:, 0:n], wuv[:, k, fc * 128:(fc + 1) * 128], hT[:, k, t0:t0 + n], k == 0, k == 7,
                               [gk, ("h", t0)], [puk])
                        s2 = cnt % 2
                        cnt += 1
                        ACT(sg[:, s2, 0:n], pg[:, 0:n], AF.Silu, [pgk], [("sg", s2)])
                        TT(act[sl][:, fc, t0:t0 + n], sg[:, s2, 0:n], pu[:, 0:n], ALU.mult, [("sg", s2), puk],
                           [("act", sl, t0)])

            def down(j):
                f0, nfc = groups[j]
                sl = j % 2
                wdv, dk = gslots[j]
                for (t0, n) in tl:
                    for c in range(8):
                        p, pk = pb()
                        for fc in range(nfc):
                            MM(p[:, 0:n], wdv[:, fc, c * 128:(c + 1) * 128], act[sl][:, fc, t0:t0 + n], fc == 0,
                               fc == nfc - 1, [dk, ("act", sl, t0)], [pk])
                        STT(xT[:, c, t0:t0 + n], p[:, 0:n], 0.5, xT[:, c, t0:t0 + n], ALU.mult, ALU.add,
                            [pk, ("x", c, t0)], [("x", c, t0)])

            for j in range(len(groups) + 1):
                if j < len(groups):
                    gateup(j)
                if j >= 1:
                    down(j - 1)
            if which == 1:
                S.barrier()

        def ret_epilogue(Oin, Okey, n, l, h, hh, sgT, t0, scr, ebank=None):
            Osb, sqb, rb = scr
            CP(Osb[:, 0:n], Oin, [Okey], ["Osb"], eng="act")
            ACT(sqb[:, 0:n], Oin, AF.Square, [Okey], ["sqb"])
            p, pk = pb() if ebank is None else (ps[ebank], ("ps", ebank))
            MM(p[:, 0:n], onesv[:, 2, :], sqb[:, 0:n], True, True, ["sqb", "onesv"], [pk])
            rstd_from(p[:, 0:n], pk, n, 128, rb, "rb")
            STT(Osb[:, 0:n], Osb[:, 0:n], retg[:, l, h:h + 1], rb[:, 0:n], ALU.mult, ALU.mult, ["Osb", "rb"], ["Osb"])
            TT(oT[:, h, t0:t0 + n], Osb[:, 0:n], sgT[:, hh, t0:t0 + n], ALU.mult, ["Osb", ("sgT", t0)], [("o", h, t0)])

        def retention(si, l, pr, last):
            areset()
            win = W["w_in"][l]
            Wg, gk = wg_slot()
            wqk = wview(Wg, 0, 8, 256)
            wv = wview(Wg, 2048, 8, 256)
            wgt = wview(Wg, 4096, 8, 256)
            Kp = aalloc([8, 128], BF16)
            Vt = aalloc([8, 256], BF16)
            qT = aalloc([1, NT], BF16)[:, 0, :]
            kT = aalloc([1, NT], BF16)[:, 0, :]
            qdT = aalloc([1, TP], BF16)[:, 0, :]
            sgT = aalloc([2, NT], BF16)
            AT = [aalloc([4, 128], BF16) for _ in range(2)]
            Sbf = aalloc([1, 128], BF16)[:, 0, :]
            Osb = aalloc([1, 512], F32)[:, 0, :]
            sqb = aalloc([1, 512], BF16)[:, 0, :]
            rb = aalloc([1, 512], F32)[:, 0, :]
            scr = (Osb, sqb, rb)
            wv3 = win.rearrange("(k p) f -> p k f", p=128)
            WDMA(wqk[:, :, 0:128], wv3[:, :, pr * 128:(pr + 1) * 128], gk)
            WDMA(wqk[:, :, 128:256], wv3[:, :, 256 + pr * 128:256 + (pr + 1) * 128], gk)
            WDMA(wv[:], wv3[:, :, 512 + pr * 256:512 + (pr + 1) * 256], gk)
            WDMA(wgt[:], wv3[:, :, 1024 + pr * 256:1024 + (pr + 1) * 256], gk)
            Sf = Sst[:, l, pr, :]
            CP(Sbf, Sf, ["Sst"], ["Sbf"])
            for n in range(8):
                p, pk = pb()
                p2, pk2 = pb()
                for k in range(8):
                    MM(p[:, 0:128], hT[:, k, n * 128:(n + 1) * 128], wqk[:, k, 128:256], k == 0, k == 7,
                       [("h", (n // 4) * 512), gk], [pk])
                for k in range(8):
                    MM(p2[:, 0:256], hT[:, k, n * 128:(n + 1) * 128], wv[:, k, :], k == 0, k == 7,
                       [("h", (n // 4) * 512), gk], [pk2])
                for hh in range(2):
                    TS(Kp[:, n, hh * 64:(hh + 1) * 64], p[:, hh * 64:(hh + 1) * 64], kdec[:, 2 * pr + hh:2 * pr + hh + 1],
                       ALU.mult, [pk, "kdec"], [("Kp", n)])
                CP(Vt[:, n, :], p2[:, 0:256], [pk2], [("Vt", n)], eng="act")
            for (t0, n) in tiles_of(si):
                p, pk = pb()
                for k in range(8):
                    MM(p[:, 0:n], wqk[:, k, 0:128], hT[:, k, t0:t0 + n], k == 0, k == 7, [gk, ("h", t0)], [pk])
                CP(qT[:, t0:t0 + n], p[:, 0:n], [pk], [("qT", t0)], eng="dve")
                if t0 < TP:
                    TT(qdT[:, t0:t0 + n].rearrange("p (a b) -> p a b", a=4), p[:, 0:n].rearrange("p (a b) -> p a b", a=4),
                       qdec[:, pr:pr + 1, :].to_broadcast([128, 4, 128]), ALU.mult, [pk, "qdec"], [("qdT", t0)])
                p, pk = pb()
                for k in range(8):
                    MM(p[:, 0:n], wqk[:, k, 128:256], hT[:, k, t0:t0 + n], k == 0, k == 7, [gk, ("h", t0)], [pk])
                ACT(kT[:, t0:t0 + n], p[:, 0:n], AF.Identity, [pk], [("kT", t0)], scale=0.125)
                if t0 >= TP:
                    for hh in range(2):
                        p, pk = pb()
                        for k in range(8):
                            MM(p[:, 0:n], wgt[:, k, hh * 128:(hh + 1) * 128], hT[:, k, t0:t0 + n], k == 0, k == 7,
                               [gk, ("h", t0)], [pk])
                        ACT(sgT[:, hh, t0:t0 + n], p[:, 0:n], AF.Silu, [pk], [("sgT", t0)])
            Sall = aalloc([8, 128], BF16)
            AT2 = [[AT[0], AT[1]], [aalloc([4, 128], BF16), aalloc([4, 128], BF16)]]
            for n in range(8):
                b = n // 2
                off = (n % 2) * 256
                MM(ps[b][:, off:off + 128], Kp[:, n, :], Vt[:, n, 0:128], True, True, [("Kp", n), ("Vt", n)], [("ps", b)])
                MM(ps[b][:, off + 128:off + 256], Kp[:, n, :], Vt[:, n, 128:256], True, True, [("Kp", n), ("Vt", n)],
                   [("ps", b)])
            for ti in range(2):
                t0 = ti * 512
                for hh in range(2):
                    bank = 4 + 2 * ti + hh
                    lo, hi = 64 * hh, 64 * hh + 64
                    for nn in range(4):
                        c0 = t0 + nn * 128
                        MM(ps[bank][:, nn * 128:(nn + 1) * 128], kT[lo:hi, c0:c0 + 128], qT[lo:hi, c0:c0 + 128], True, True,
                           [("kT", t0), ("qT", t0)], [("ps", bank)])
                    TT(AT2[ti][hh][:], ps[bank][:, :].rearrange("p (a b) -> p a b", a=4),
                       rdm[:, 2 * pr + hh:2 * pr + hh + 1, :].to_broadcast([128, 4, 128]), ALU.mult, [("ps", bank), "rdm"],
                       [("AT", ti, hh)])
            for ti in range(2):
                t0 = ti * 512
                for hh in range(2):
                    bank = 4 + 2 * ti + hh
                    for k in range(8):
                        MM(ps[bank][:, :], wgt[:, k, hh * 128:(hh + 1) * 128], hT[:, k, t0:t0 + 512], k == 0, k == 7,
                           [gk, ("h", t0)], [("ps", bank)])
                    ACT(sgT[:, hh, t0:t0 + 512], ps[bank][:, :], AF.Silu, [("ps", bank)], [("sgT", t0)])
            for n in range(8):
                b = n // 2
                off = (n % 2) * 256
                CP(Sall[:, n, :], Sst[:, l, pr, :], ["Sst"], [("Sall", n)])
                for hh in range(2):
                    lo, hi = 64 * hh, 64 * hh + 64
                    STT(Sst[lo:hi, l, pr, :], Sst[lo:hi, l, pr, :], gam[2 * pr + hh] ** 128,
                        ps[b][lo:hi, off + hh * 128:off + (hh + 1) * 128], ALU.mult, ALU.add, [("ps", b), "Sst"], ["Sst"])
            for ti in range(2):
                t0 = ti * 512
                for hh in range(2):
                    bank = 2 * ti + hh
                    lo, hi = 64 * hh, 64 * hh + 64
                    for nn in range(4):
                        n = ti * 4 + nn
                        c0 = t0 + nn * 128
                        MM(ps[bank][:, nn * 128:(nn + 1) * 128], Vt[:, n, hh * 128:(hh + 1) * 128], AT2[ti][hh][:, nn, :],
                           True, False, [("Vt", n), ("AT", ti, hh)], [("ps", bank)])
                        MM(ps[bank][:, nn * 128:(nn + 1) * 128], Sall[lo:hi, n, :], qdT[lo:hi, c0:c0 + 128], False, True,
                           [("Sall", n), ("qdT", t0)], [("ps", bank)])
                for hh in range(2):
                    bank = 2 * ti + hh
                    ret_epilogue(ps[bank][:, :], ("ps", bank), 512, l, 2 * pr + hh, hh, sgT, t0, scr, ebank=4 + 2 * ti + hh)
            pctr[0] = 0
            if last:
                for hh in range(2):
                    DMA("sp", retp[l, 2 * pr + hh], Sst[64 * hh:64 * hh + 64, l, pr, :], ["Sst"], (), "retp", final=True)
            if si == 0:
                t0 = TP
                S0f = aalloc([NS, 128], F32)
                S0b = aalloc([NS, 128], BF16)
                Ks = aalloc([1, 128], BF16, parts=NS)[:, 0, :]
                Vs = aalloc([1, 256], BF16, parts=NS)[:, 0, :]
                Vbd1 = aalloc([NS, 128], BF16, parts=NS)
                Vbd = [Vbd1, Vbd1]
                vTs = aalloc([2, NS], F32)
                prod = aalloc([1, NS], BF16)[:, 0, :]
                tmp = aalloc([1, NS], F32)[:, 0, :]
                Os = aalloc([1, NS], F32)[:, 0, :]
                Sn = aalloc([2, 512], F32)
                DMA("sp", S0f[:], sret[l, :, 2 * pr:2 * pr + 2].rearrange("b h d v -> (h d) b v"), (), ["S0f"], "S0f")
                CP(S0b[:], S0f[:], ["S0f"], ["S0b"])
                p, pk = pb()
                for k in range(8):
                    MM(p[0:NS, 0:128], hT[:, k, t0:t0 + NS], wqk[:, k, 128:256], k == 0, k == 7, [("h", t0), gk], [pk])
                for k in range(8):
                    MM(p[0:NS, 128:384], hT[:, k, t0:t0 + NS], wv[:, k, :], k == 0, k == 7, [("h", t0), gk], [pk])
                ACT(Ks, p[0:NS, 0:128], AF.Identity, [pk], ["Ks"], scale=0.125)
                CP(Vs, p[0:NS, 128:384], [pk], ["Vs"], eng="act")
                for hh in range(2):
                    p, pk = pb()
                    for k in range(8):
                        MM(p[:, 0:NS], wv[:, k, hh * 128:(hh + 1) * 128], hT[:, k, t0:t0 + NS], k == 0, k == 7,
                           [gk, ("h", t0)], [pk])
                    CP(vTs[:, hh, :], p[:, 0:NS], [pk], [("vTs", hh)], eng="act")
                TT(prod, qT[:, t0:t0 + NS], kT[:, t0:t0 + NS], ALU.mult, [("qT", t0), ("kT", t0)], ["prod"])
                for hh in range(2):
                    lo, hi = 64 * hh, 64 * hh + 64
                    h = 2 * pr + hh
                    pt, ptk = pb()
                    for b in range(NS):
                        MM(pt[:, b:b + 1], S0b[lo:hi, b, :], qT[lo:hi, t0 + b:t0 + b + 1], True, True,
                           ["S0b", ("qT", t0)], [ptk])
                    pq, pqk = pb()
                    MM(pq[:, 0:NS], hsel[:, hh, :], prod, True, True, ["hsel", "prod"], [pqk])
                    TT(tmp, pq[:, 0:NS], vTs[:, hh, :], ALU.mult, [pqk, ("vTs", hh)], ["tmp"])
                    STT(Os, pt[:, 0:NS], gam[h], tmp, ALU.mult, ALU.add, [ptk, "tmp"], ["Os"])
                    ret_epilogue(Os, "Os", NS, l, h, hh, sgT, t0, scr)
                    TT(Vbd[hh][:], Vs[:, hh * 128:(hh + 1) * 128].unsqueeze(1).to_broadcast([NS, NS, 128]),
                       eyeN[:, :].unsqueeze(2).to_broadcast([NS, NS, 128]), ALU.mult, ["Vs", "eyeN"], ["Vbd"])
                    for q in range(NS // 4):
                        pn, pnk = pb()
                        MM(pn[:, :], Ks[:, :], Vbd[hh][:, 4 * q:4 * q + 4, :].rearrange("p a b -> p (a b)"), True, True, ["Ks", "Vbd"], [pnk])
                        s2 = q % 2
                        STT(Sn[lo:hi, s2, :], S0f[lo:hi, 4 * q:4 * q + 4, :].rearrange("p a b -> p (a b)"), gam[h], pn[lo:hi, :], ALU.mult, ALU.add,
                            [pnk, "S0f"], [("Sn", s2)])
                        DMA("sp", rets[l, 4 * q:4 * q + 4, h].rearrange("b d v -> d b v"),
                            Sn[lo:hi, s2, :].rearrange("p (b v) -> p b v", b=4), [("Sn", s2)], (), ("Sn", s2), final=True)
            S.barrier()

        def attention(si, l, last):
            areset()
            wv3 = W["w_in"][l].rearrange("(k p) f -> p k f", p=128)
            Wg, gk = wg_slot()
            wa = wview(Wg, 0, 8, 512)
            Va = aalloc([9, 128], BF16)
            qn = [aalloc([1, NT], BF16, parts=64)[:, 0, :] for _ in range(4)]
            kn = [aalloc([1, 128 + NT], BF16, parts=64)[:, 0, :] for _ in range(2)]
            qsb = aalloc([1, 512], F32, parts=64)[:, 0, :]
            sqb = aalloc([1, 512], BF16, parts=64)[:, 0, :]
            rb = aalloc([1, 512], F32, parts=64)[:, 0, :]
            knf = aalloc([2, 128], F32, parts=64)
            vlast = aalloc([1, 128], F32)[:, 0, :]
            E = [aalloc([1, 512], BF16)[:, 0, :] for _ in range(4)]
            PT = [aalloc([1, 512], BF16)[:, 0, :] for _ in range(4)]
            den = aalloc([1, 512], F32, parts=64)[:, 0, :]
            WDMA(wa[:], wv3[:, :, 1536:2048], gk)
            CP(Va[:, 0, :], Vh[:, l, :], ["Vh"], [("Va", 0)])
            for g in range(2):
                CP(kn[g][:, 0:128], kTh[:, l, g, :], ["kTh"], [("kn", g, -1)])
            for n in range(8):
                p, pk = pb()
                for k in range(8):
                    MM(p[:, 0:128], hT[:, k, n * 128:(n + 1) * 128], wa[:, k, 384:512], k == 0, k == 7,
                       [("h", (n // 4) * 512), gk], [pk])
                CP(Va[:, n + 1, :], p[:, 0:128], [pk], [("Va", n + 1)], eng="act")
                if n == 7:
                    if last:
                        CP(vlast, p[:, 0:128], [pk], ["vlast"], eng="act")
                        DMA("sp", vwp[l], vlast, ["vlast"], (), "vwp", final=True)
                    CP(Vh[:, l, :], p[:, 0:128], [pk], ["Vh"], eng="act")
            sqb2 = [sqb, aalloc([1, 512], BF16, parts=64)[:, 0, :]]
            rb2 = [rb, aalloc([1, 512], F32, parts=64)[:, 0, :]]
            items = []
            for (t0, n) in tiles_of(si):
                for h in range(4):
                    items.append(("q", h, t0, n))
                for g in range(2):
                    items.append(("k", g, t0, n))
            state = {}

            def b_proj(i):
                kind, idx, t0, n = items[i]
                wcol = idx * 64 if kind == "q" else 256 + idx * 64
                p, pk = pb()
                for k in range(8):
                    MM(p[0:64, 0:n], wa[:, k, wcol:wcol + 64], hT[:, k, t0:t0 + n], k == 0, k == 7, [gk, ("h", t0)], [pk])
                state[i] = (p, pk)

            def b_finish(i):
                kind, idx, t0, n = items[i]
                p, pk = state.pop(i)
                s2 = i % 2
                ACT(sqb2[s2][:, 0:n], p[0:64, 0:n], AF.Square, [pk], [("sqbA", s2)])
                p2, pk2 = pb()
                MM(p2[0:64, 0:n], onesv[0:64, 1, 0:64], sqb2[s2][:, 0:n], True, True, [("sqbA", s2), "onesv"], [pk2])
                rstd_from(p2[0:64, 0:n], pk2, n, 64, rb2[s2], ("rbA", s2))
                if kind == "q":
                    STT(qn[idx][:, t0:t0 + n], p[0:64, 0:n], qg8[:, l:l + 1], rb2[s2][:, 0:n], ALU.mult, ALU.mult,
                        [pk, ("rbA", s2)], [("qn", idx, t0)])
                else:
                    g = idx
                    STT(kn[g][:, 128 + t0:128 + t0 + n], p[0:64, 0:n], kgs[:, l:l + 1], rb2[s2][:, 0:n], ALU.mult, ALU.mult,
                        [pk, ("rbA", s2)], [("kn", g, t0)])
                    if t0 == 512:
                        STT(knf[:, g, :], p[0:64, 384:512], kgs[:, l:l + 1], rb2[s2][:, 384:512], ALU.mult, ALU.mult,
                            [pk, ("rbA", s2)], [("knf", g)])
                        if last:
                            DMA("sp", kwp[l, :, g * 64:(g + 1) * 64].rearrange("t d -> d t"), knf[:, g, :], [("knf", g)],
                                (), "kwp", final=True, slow=True)
                        CP(kTh[:, l, g, :], kn[g][:, 128 + 896:128 + 1024], [("kn", g, 512)], ["kTh"])

            b_proj(0)
            for i in range(len(items)):
                if i + 1 < len(items):
                    b_proj(i + 1)
                b_finish(i)
            lnd = qsb
            it = 0
            for g in range(2):
                for ti in range(2):
                    t0 = ti * 512
                    base = 4 * (it % 2)
                    oth = 4 - base
                    it += 1
                    pO = [(ps[base + 0], ("ps", base + 0)), (ps[base + 1], ("ps", base + 1))]
                    pD = [(ps[base + 2], ("ps", base + 2)), (ps[base + 3], ("ps", base + 3))]
                    for nn in range(4):
                        n = ti * 4 + nn
                        c0 = n * 128
                        pS, pSk = ps[oth + nn], ("ps", oth + nn)
                        prevk = ("kn", g, ((c0 - 128) // 512) * 512) if c0 >= 128 else ("kn", g, -1)
                        for hh in range(2):
                            h = 2 * g + hh
                            MM(pS[:, hh * 256:hh * 256 + 128], kn[g][:, c0:c0 + 128], qn[h][:, c0:c0 + 128], True, True,
                               [prevk, ("qn", h, t0)], [pSk])
                            MM(pS[:, hh * 256 + 128:hh * 256 + 256], kn[g][:, 128 + c0:256 + c0], qn[h][:, c0:c0 + 128],
                               True, True, [("kn", g, t0), ("qn", h, t0)], [pSk])
                    for nn in range(4):
                        pS, pSk = ps[oth + nn], ("ps", oth + nn)
                        ACT(E[nn], pS[:, :], AF.Exp, [pSk], [("E", nn)])
                        TT(PT[nn], E[nn], amask[:, g, :], ALU.mult, [("E", nn), "amask"], [("PT", nn)])
                    for nn in range(4):
                        n = ti * 4 + nn
                        skip_prev = (si == 0 and n == 0)
                        s2 = nn
                        for hh in range(2):
                            cs = slice(nn * 128, (nn + 1) * 128)
                            prevP = PT[s2][:, hh * 256:hh * 256 + 128]
                            curP = PT[s2][:, hh * 256 + 128:hh * 256 + 256]
                            if not skip_prev:
                                MM(pO[hh][0][0:64, cs], Va[:, n, g * 64:(g + 1) * 64], prevP, True, False,
                                   [("Va", n), ("PT", s2)], [pO[hh][1]])
                            MM(pO[hh][0][0:64, cs], Va[:, n + 1, g * 64:(g + 1) * 64], curP, skip_prev, True,
                               [("Va", n + 1), ("PT", s2)], [pO[hh][1]])
                            if not skip_prev:
                                MM(pD[hh][0][0:64, cs], onesv[:, 4, 0:64], prevP, True, False, [("PT", s2), "onesv"],
                                   [pD[hh][1]])
                            MM(pD[hh][0][0:64, cs], onesv[:, 4, 0:64], curP, skip_prev, True, [("PT", s2), "onesv"],
                               [pD[hh][1]])
                    for hh in range(2):
                        h = 2 * g + hh
                        ACT(lnd, pD[hh][0][0:64, :], AF.Ln, [pD[hh][1], "esink"], ["qsb"],
                            bias=esink[:, l * 4 + h:l * 4 + h + 1])
                        ACT(den, lnd, AF.Exp, ["qsb"], ["den"], scale=-1.0)
                        TT(oT[0:64, 4 + h, t0:t0 + 512], pO[hh][0][0:64, :], den[:, :], ALU.mult, [pO[hh][1], "den"],
                           [("o", 4 + h, t0)])
            pctr[0] = 0
            if si == 0:
                t0 = TP
                HB = NS // 2
                kcf = aalloc([HB, 128], F32)
                vcb = aalloc([NS, 128], BF16)
                kcT = aalloc([NS, 128], BF16, parts=64)
                vnT = aalloc([2, NS], F32, parts=64)
                knS = aalloc([2, NS], F32, parts=64)
                Es = aalloc([1, 4 * NS], F32)[:, 0, :]
                Ps = aalloc([4, NS], BF16)
                prod = aalloc([1, NS], BF16, parts=64)[:, 0, :]
                pn = aalloc([4, NS], F32, parts=64)
                num = aalloc([4, NS], F32, parts=64)
                dn = aalloc([4, NS], F32, parts=64)
                DMA("pool", vcb[:], cv[l].rearrange("b k f -> k b f"), (), ["vcb"], "vcb")
                DMA("sp", kws[l, :, 0:127, :], ck[l, :, 1:128, :], (), (), "kws", final=True)
                DMA("sp", vws[l, :, 0:127, :], cv[l, :, 1:128, :], (), (), "vws", final=True)
                for g in range(2):
                    p, pk = pb()
                    for k in range(8):
                        MM(p[0:64, 0:NS], wa[:, k, 384 + g * 64:384 + (g + 1) * 64], hT[:, k, t0:t0 + NS], k == 0, k == 7,
                           [gk, ("h", t0)], [pk])
                    CP(vnT[:, g, :], p[0:64, 0:NS], [pk], [("vnT", g)])
                    DMA("sp", vws[l, :, 127, g * 64:(g + 1) * 64].rearrange("b d -> d b"), vnT[:, g, :], [("vnT", g)], (),
                        "vws2", final=True, slow=True)
                    p, pk = pb()
                    for k in range(8):
                        MM(p[0:64, 0:NS], wa[:, k, 256 + g * 64:256 + (g + 1) * 64], hT[:, k, t0:t0 + NS], k == 0, k == 7,
                           [gk, ("h", t0)], [pk])
                    CP(qsb[:, 0:NS], p[0:64, 0:NS], [pk], ["qsb"], eng="act")
                    ACT(sqb[:, 0:NS], p[0:64, 0:NS], AF.Square, [pk], ["sqbA"])
                    p2, pk2 = pb()
                    MM(p2[0:64, 0:NS], onesv[0:64, 1, 0:64], sqb[:, 0:NS], True, True, ["sqbA", "onesv"], [pk2])
                    rstd_from(p2[0:64, 0:NS], pk2, NS, 64, rb, "rbA")
                    STT(knS[:, g, :], qsb[:, 0:NS], kgs[:, l:l + 1], rb[:, 0:NS], ALU.mult, ALU.mult, ["qsb", "rbA"],
                        [("knS", g)])
                    DMA("sp", kws[l, :, 127, g * 64:(g + 1) * 64].rearrange("b d -> d b"), knS[:, g, :], [("knS", g)], (),
                        "kws2", final=True, slow=True)
                pS, pSk = ps[7], ("ps", 7)
                for g in range(2):
                    for hb in range(2):
                        DMA("sp", kcf[:], ck[l, hb * HB:(hb + 1) * HB].rearrange("b k f -> k b f"), (), ["kcf"], "kcf")
                        for q in range(HB // 4):
                            p, pk = ps[q % 4], ("ps", q % 4)
                            for bb in range(4):
                                TR(p[0:64, bb * 128:(bb + 1) * 128], kcf[:, 4 * q + bb, g * 64:(g + 1) * 64], ident[:, :],
                                   ["kcf", "ident"], [pk])
                            b0 = hb * HB + 4 * q
                            CP(kcT[:, b0:b0 + 4, :], p[0:64, :].rearrange("p (a b) -> p a b", a=4), [pk],
                               ["kcT"], eng="act" if q % 2 else "dve")
                    for hh in range(2):
                        h = 2 * g + hh
                        for b in range(NS):
                            MM(pS[:, h * NS + b:h * NS + b + 1], kcT[:, b, :], qn[h][:, t0 + b:t0 + b + 1], True, True,
                               ["kcT", ("qn", h, t0)], [pSk])
                pctr[0] = 0
                ACT(Es, pS[:, 0:4 * NS], AF.Exp, [pSk], ["Es"])
                TT(Ps[:], Es.rearrange("p (h b) -> p h b", h=4), smask[:, :].unsqueeze(2).to_broadcast([128, 4, NS]),
                   ALU.mult, ["Es", "smask"], ["Ps"])
                pO2, pO2k = pb()
                for h in range(4):
                    g = h // 2
                    for b in range(NS):
                        MM(pO2[0:64, h * NS + b:h * NS + b + 1], vcb[:, b, g * 64:(g + 1) * 64], Ps[:, h, b:b + 1], True, True,
                           ["vcb", "Ps"], [pO2k])
                pD2, pD2k = pb()
                MM(pD2[0:64, 0:4 * NS], onesv[:, 4, 0:64], Ps[:].rearrange("p h b -> p (h b)"), True, True,
                   ["Ps", "onesv"], [pD2k])
                pN, pNk = pb()
                for h in range(4):
                    g = h // 2
                    TT(prod, qn[h][:, t0:t0 + NS], kn[g][:, 128 + t0:128 + t0 + NS], ALU.mult,
                       [("qn", h, t0), ("kn", g, t0)], ["prodA"])
                    MM(pN[0:64, h * NS:(h + 1) * NS], onesv[0:64, 4, 0:64], prod, True, True, ["prodA", "onesv"], [pNk])
                ACT(pn[:].rearrange("p h b -> p (h b)"), pN[0:64, 0:4 * NS], AF.Exp, [pNk], ["pn"])
                for h in range(4):
                    g = h // 2
                    TT(num[:, h, :], pn[:, h, :], vnT[:, g, :], ALU.mult, ["pn", ("vnT", g)], ["num"])
                TT(num[:].rearrange("p h b -> p (h b)"), num[:].rearrange("p h b -> p (h b)"), pO2[0:64, 0:4 * NS], ALU.add,
                   ["num", pO2k], ["num"])
                TT(dn[:].rearrange("p h b -> p (h b)"), pn[:].rearrange("p h b -> p (h b)"), pD2[0:64, 0:4 * NS], ALU.add,
                   ["pn", pD2k], ["dn"])
                for h in range(4):
                    TS(dn[:, h, :], dn[:, h, :], esink[:, l * 4 + h:l * 4 + h + 1], ALU.add, ["dn", "esink"], ["dn"])
                RCP(dn[:].rearrange("p h b -> p (h b)"), dn[:].rearrange("p h b -> p (h b)"), ["dn"], ["dn"])
                for h in range(4):
                    TT(oT[0:64, 4 + h, t0:t0 + NS], num[:, h, :], dn[:, h, :], ALU.mult, ["num", "dn"], [("o", 4 + h, t0)])
            S.barrier()

        def conv(si, l, last):
            areset()
            wv3 = W["w_in"][l].rearrange("(k p) f -> p k f", p=128)
            Wg, gk = wg_slot()
            wc = wview(Wg, 0, 8, 512)
            wpw = wview(Wg, 4096, 2, 256)
            Dg = aalloc([2, 31, 128], BF16)
            uT = aalloc([2, 30 + TP], BF16)
            sig = aalloc([2, 512], F32)
            ulast = aalloc([2, 30], F32)
            yb = aalloc([2, 512], F32)
            ybf = aalloc([2, 512], BF16)
            ysq = aalloc([2, 512], BF16)
            msb = aalloc([1, 512], F32)[:, 0, :]
            var = aalloc([1, 512], F32)[:, 0, :]
            rb = aalloc([1, 512], F32)[:, 0, :]
            dd = aalloc([2, 512], F32)
            ysl = aalloc([2, 512], BF16)
            WDMA(wc[:], wv3[:, :, 2048:2560], gk)
            WDMA(wpw[:], W["conv_pw"][l].rearrange("(k p) o -> p k o", p=128), gk)
            if si == 0:
                wcv = wview(Wg, 4608, 1, 256, parts=30)[:, 0, :]
                WDMA(wcv, W["conv_w"][l, 0:30, :], gk)
            for c in range(2):
                for j in range(31):
                    TS(Dg[:, c, j, :], identb[:, :], convw[:, l, c, j:j + 1], ALU.mult, ["identb", "convw"], ["Dg"])
                CP(uT[:, c, 0:30], uh[:, l, c, :], ["uh"], [("uT", c, -1)])

            def ln_epilogue(n, t0):
                pm, pmk = pb()
                pq, pqk = pb()
                for c in range(2):
                    MM(pm[:, 0:n], onesv[:, 3, :], ybf[:, c, 0:n], c == 0, c == 1, [("ybf", c), "onesv"], [pmk])
                for c in range(2):
                    MM(pq[:, 0:n], onesv[:, 3, :], ysq[:, c, 0:n], c == 0, c == 1, [("ysq", c), "onesv"], [pqk])
                CP(msb[:, 0:n], pm[:, 0:n], [pmk], ["msb"], eng="act")
                STT(var[:, 0:n], msb[:, 0:n], -1.0, msb[:, 0:n], ALU.mult, ALU.mult, ["msb"], ["var"])
                TT(var[:, 0:n], var[:, 0:n], pq[:, 0:n], ALU.add, ["var", pqk], ["var"])
                ACT(rb[:, 0:n], var[:, 0:n], AF.Ln, ["var", "epsb"], ["rbC"], bias=epsb[:, 0:1])
                ACT(rb[:, 0:n], rb[:, 0:n], AF.Exp, ["rbC"], ["rbC"], scale=-0.5)
                for c in range(2):
                    TT(dd[:, c, 0:n], yb[:, c, 0:n], msb[:, 0:n], ALU.subtract, [("yb", c), "msb"], [("dd", c)])
                    TT(dd[:, c, 0:n], dd[:, c, 0:n], rb[:, 0:n], ALU.mult, [("dd", c), "rbC"], [("dd", c)])
                    ACT(ysl[:, c, 0:n], dd[:, c, 0:n], AF.Silu, [("dd", c), "lng", "lnb"], [("ysl", c)],
                        bias=lnb[:, l, c:c + 1], scale=lng[:, l, c:c + 1])
                for oc in range(2):
                    p, pk = pb()
                    for c in range(2):
                        MM(p[:, 0:n], wpw[:, c, oc * 128:(oc + 1) * 128], ysl[:, c, 0:n], c == 0, c == 1,
                           [gk, ("ysl", c)], [pk])
                    CP(oT[:, 8 + oc, t0:t0 + n], p[:, 0:n], [pk], [("o", 8 + oc, t0)], eng="act" if oc else "dve")

            uS = aalloc([2, NS], F32)
            for (t0, n) in tiles_of(si):
                for c in range(2):
                    pa, pak = pb()
                    pg, pgk = pb()
                    for k in range(8):
                        MM(pa[:, 0:n], wc[:, k, c * 128:(c + 1) * 128], hT[:, k, t0:t0 + n], k == 0, k == 7,
                           [gk, ("h", t0)], [pak])
                    for k in range(8):
                        MM(pg[:, 0:n], wc[:, k, 256 + c * 128:256 + (c + 1) * 128], hT[:, k, t0:t0 + n], k == 0, k == 7,
                           [gk, ("h", t0)], [pgk])
                    ACT(sig[:, c, 0:n], pg[:, 0:n], AF.Sigmoid, [pgk], [("sig", c)])
                    if t0 < TP:
                        TT(uT[:, c, 30 + t0:30 + t0 + n], pa[:, 0:n], sig[:, c, 0:n], ALU.mult, [pak, ("sig", c)],
                           [("uT", c, t0)])
                        if t0 == 512:
                            TT(ulast[:, c, :], pa[:, 482:512], sig[:, c, 482:512], ALU.mult, [pak, ("sig", c)],
                               [("ulast", c)])
                            if last:
                                DMA("sp", cvp[l, :, c * 128:(c + 1) * 128].rearrange("t p -> p t"), ulast[:, c, :],
                                    [("ulast", c)], (), "cvp", final=True, slow=True)
                            CP(uh[:, l, c, :], uT[:, c, TP:TP + 30], [("uT", c, 512)], ["uh"])
                    else:
                        TT(uS[:, c, :], pa[:, 0:n], sig[:, c, 0:n], ALU.mult, [pak, ("sig", c)], [("uS", c)])
            wst = wout_prepare(l)
            taps = {}
            for ti in range(2):
                t0 = ti * 512
                for c in range(2):
                    p, pk = pb()
                    rk = [("uT", c, t0), ("uT", c, t0 - 512 if t0 else -1), "Dg"]
                    for j in range(31):
                        MM(p[:, :], Dg[:, c, j, :], uT[:, c, t0 + j:t0 + j + 512], j == 0, j == 30, rk, [pk])
                    taps[(ti, c)] = (p, pk)
            def evac(ti):
                for c in range(2):
                    p, pk = taps[(ti, c)]
                    ACT(yb[:, c, :], p[:, :], AF.Identity, [pk, "convb"], [("yb", c)], bias=convb[:, l, c:c + 1])
                    ACT(ysq[:, c, :], p[:, :], AF.Square, [pk, "convb"], [("ysq", c)], bias=convb[:, l, c:c + 1])
                    CP(ybf[:, c, :], yb[:, c, :], [("yb", c)], [("ybf", c)])

            evac(0)
            ln_epilogue(512, 0)
            evac(1)
            wout_tile(wst, 0, 512)
            ln_epilogue(512, 512)
            if si == 0:
                t0 = TP
                cbb = aalloc([NS, 256], BF16, parts=30)
                msk = aalloc([4, 128], F32)
                y0 = aalloc([2, NS], F32)
                DMA("pool", cbb[:], sconv[l].rearrange("b j c -> j b c"), (), ["cbb"], "cbb")
                DMA("sp", cvs[l, :, 0:29, :], sconv[l, :, 1:30, :], (), (), "cvs", final=True)
                for c in range(2):
                    DMA("sp", cvs[l, :, 29, c * 128:(c + 1) * 128].rearrange("b p -> p b"), uS[:, c, :], [("uS", c)], (),
                        "cvs2", final=True, slow=True)
                    for q in range(NS // 4):
                        p, pk = pb()
                        MM(p[:, :], wcv[:, c * 128:(c + 1) * 128], cbb[:, 4 * q:4 * q + 4, c * 128:(c + 1) * 128], True, True,
                           [gk, "cbb"], [pk])
                        TT(msk[:], p[:, :].rearrange("p (a b) -> p a b", a=4),
                           ident[:, :].unsqueeze(1).to_broadcast([128, 4, 128]), ALU.mult, [pk, "ident"], ["msk"])
                        S.op("dve", (lambda o_, i_: (lambda e: e.reduce_sum(o_, i_, axis=AX.X)))(y0[:, c, 4 * q:4 * q + 4],
                                                                                                  msk[:]),
                             ["msk"], [("y0", c)])
                    STT(yb[:, c, 0:NS], uS[:, c, :], convw[:, l, c, 30:31], y0[:, c, :], ALU.mult, ALU.add,
                        [("uS", c), ("y0", c), "convw"], [("yb", c)])
                    TS(yb[:, c, 0:NS], yb[:, c, 0:NS], convb[:, l, c:c + 1], ALU.add, [("yb", c), "convb"], [("yb", c)])
                    CP(ybf[:, c, 0:NS], yb[:, c, 0:NS], [("yb", c)], [("ybf", c)])
                    ACT(ysq[:, c, 0:NS], yb[:, c, 0:NS], AF.Square, [("yb", c)], [("ysq", c)])
                ln_epilogue(NS, t0)
            S.barrier()
            wout_tile(wst, 512, 512)
            if si == 0:
                wout_tile(wst, TP, NS)

        def wout_prepare(l):
            Wg, gk = wg_slot()
            Wd, dk = wd_slot()
            wo8 = wview(Wg, 0, 8, D)
            wo2 = wview(Wd, 0, 2, D)
            wod = W["w_out"][l]
            WDMA(wo8[:, 0:4, :], wod[0:512, :].rearrange("(k p) d -> p k d", p=128), gk)
            WDMA(wo8[0:64, 4:8, :], wod[512:768, :].rearrange("(k p) d -> p k d", p=64), gk)
            WDMA(wo2[:, :, :], wod[768:1024, :].rearrange("(k p) d -> p k d", p=128), dk)
            return wo8, wo2, gk, dk

        def wout_tile(wst, t0, n):
            wo8, wo2, gk, dk = wst
            for c in range(8):
                p, pk = pb()
                for k in range(10):
                    kp = 64 if 4 <= k < 8 else 128
                    wsrc = wo8[0:kp, k, c * 128:(c + 1) * 128] if k < 8 else wo2[:, k - 8, c * 128:(c + 1) * 128]
                    MM(p[:, 0:n], wsrc, oT[0:kp, k, t0:t0 + n], k == 0, k == 9, [gk, dk, ("o", k, t0)], [pk])
                TT(xT[:, c, t0:t0 + n], xT[:, c, t0:t0 + n], p[:, 0:n], ALU.add, [pk, ("x", c, t0)], [("x", c, t0)])

        PH = DBG.get("phases")

        def on(name):
            return PH is None or name in PH

        for si in range(DBG.get("nseg", NSEG)):
            last = si == DBG.get("nseg", NSEG) - 1
            load_x(si)
            for l in range(DBG.get("depth", DEPTH)):
                if on("ffn1"):
                    ffn(si, l, 1)
                if on("mixnorm"):
                    areset()
                    rmsnorm(si, mng, l)
                    S.barrier()
                if on("ret"):
                    for pr in range(2):
                        retention(si, l, pr, last)
                if on("att"):
                    attention(si, l, last)
                if on("conv"):
                    conv(si, l, last)
                if on("ffn2"):
                    ffn(si, l, 2)
            store_y(si)
        S.emit()
        print("sched stats", S.stats, flush=True)
    return nc


_CACHE = {}


def kernel(**inputs):
    f32 = lambda a: np.ascontiguousarray(np.asarray(a), dtype=np.float32)
    inp = {k: f32(v) for k, v in inputs.items()}
    if "nc" not in _CACHE:
        _CACHE["nc"] = build()
    nc = _CACHE["nc"]
    ctab, _ = const_tables()
    wnames = ["ffn1_norm", "ffn1_wg", "ffn1_wu", "ffn1_wd", "mix_norm", "w_in", "ret_norm_g", "q_norm_g", "k_norm_g",
              "sinks", "conv_w", "conv_b", "conv_ln_g", "conv_ln_b", "conv_pw", "w_out", "ffn2_norm", "ffn2_wg",
              "ffn2_wu", "ffn2_wd"]
    in_maps = []
    zx = np.zeros((NSEG * TP, D), np.float32)
    for core in range(NCORES):
        m = {k: inp[k] for k in wnames}
        m.update(ctab)
        if core in ACT_CORES:
            c = ACT_CORES.index(core)
            sl = slice(c * NS, (c + 1) * NS)
            m["xp"] = inp["x_prompt"][c]
            m["xs"] = np.ascontiguousarray(inp["x_sample"][sl, 0, :])
            m["sret"] = np.ascontiguousarray(inp["state_ret"][:, sl])
            m["ck"] = np.ascontiguousarray(inp["cache_k_win"][:, sl].reshape(DEPTH, NS, 128, 128))
            m["cv"] = np.ascontiguousarray(inp["cache_v_win"][:, sl].reshape(DEPTH, NS, 128, 128))
            m["sconv"] = np.ascontiguousarray(inp["state_conv"][:, sl])
        else:
            m["xp"] = zx
            m["xs"] = np.zeros((NS, D), np.float32)
            m["sret"] = np.zeros((DEPTH, NS, 4, 64, 128), np.float32)
            m["ck"] = np.zeros((DEPTH, NS, 128, 128), np.float32)
            m["cv"] = np.zeros((DEPTH, NS, 128, 128), np.float32)
            m["sconv"] = np.zeros((DEPTH, NS, 30, 256), np.float32)
        in_maps.append(m)
    res = run_bass_kernel_spmd(nc, in_maps, core_ids=list(range(NCORES)))
    R = res.results
    A = ACT_CORES
    y_p = np.stack([R[c]["yp"] for c in A]).reshape(BATCH, SEQ, D)
    y_s = np.concatenate([R[c]["ys"] for c in A]).reshape(DECB, 1, D)
    ret_p = np.stack([R[c]["retp"] for c in A], axis=1)
    ret_s = np.concatenate([R[c]["rets"] for c in A], axis=1)
    kw_p = np.stack([R[c]["kwp"] for c in A], axis=1).reshape(DEPTH, BATCH, 128, 2, 64)
    kw_s = np.concatenate([R[c]["kws"] for c in A], axis=1).reshape(DEPTH, DECB, 128, 2, 64)
    vw_p = np.stack([R[c]["vwp"] for c in A], axis=1).reshape(DEPTH, BATCH, 128, 2, 64)
    vw_s = np.concatenate([R[c]["vws"] for c in A], axis=1).reshape(DEPTH, DECB, 128, 2, 64)
    cv_p = np.stack([R[c]["cvp"] for c in A], axis=1)
    cv_s = np.concatenate([R[c]["cvs"] for c in A], axis=1)
    outs = (y_p, y_s, ret_p, ret_s, kw_p, kw_s, vw_p, vw_s, cv_p, cv_s)
    return tuple(np.ascontiguousarray(o, dtype=np.float32) for o in outs)
```

```python
import numpy as np
from contextlib import ExitStack
import concourse.bass as bass
import concourse.mybir as mybir
from concourse.bass_utils import run_bass_kernel_spmd

F32 = mybir.dt.float32
BF16 = mybir.dt.bfloat16
ALU = mybir.AluOpType
AF = mybir.ActivationFunctionType
AX = mybir.AxisListType

D = 1024
FF = 2816
DEPTH = 2
SEQ = 4096
BATCH = 4
DECB = 128
IN_DIM = 2560
EPS = 1e-6
NCORES = 8
ACTIVE = 4
NSEG = 4
TP = 1024
NS = DECB // ACTIVE
NT = TP + NS
SEM_LIMIT = 30000
ACT_CORES = [0, 2, 4, 6]
DBG = {}


class Sched:
    ENGS = ("pe", "act", "dve", "pool", "sp")

    def __init__(self, nc, es):
        self.nc = nc
        self.es = es
        self.ops = []
        self.last_write = {}
        self.readers = {}
        self.out_dma_ops = []
        self.bar_deps = set()
        self.bar_pending = set()

    def op(self, eng, fn, reads=(), writes=(), dma=False, semkey=None, final=False, nobar=False):
        idx = len(self.ops)
        raw = set()
        other = set()
        for k in reads:
            w = self.last_write.get(k)
            if w is not None:
                raw.add(w)
        for k in writes:
            w = self.last_write.get(k)
            if w is not None:
                other.add(w)
            for r in self.readers.get(k, {}).values():
                other.add(r)
        for k in writes:
            self.last_write[k] = idx
            self.readers[k] = {}
        rk = ("dma", semkey) if dma else eng
        for k in reads:
            self.readers.setdefault(k, {})[rk] = idx
        if eng in self.bar_pending and not nobar and eng != "pe":
            if any(not self.persistent(k) for k in list(reads) + list(writes)):
                other |= self.bar_deps
                self.bar_pending.discard(eng)
        deps = set()
        for d in raw | other:
            if d == idx:
                continue
            p = self.ops[d]
            if (not p["dma"]) and p["eng"] == eng:
                if eng == "pe" or d not in raw:
                    continue
            if nobar and dma and p["dma"] and p["semkey"] == semkey:
                continue
            deps.add(d)
        self.ops.append(dict(eng=eng, fn=fn, deps=deps, dma=dma, semkey=semkey, sig=False, ev=None))
        if final:
            self.out_dma_ops.append(idx)
        return idx

    def barrier(self):
        last = {}
        for i, o in enumerate(self.ops):
            if o["fn"] is None:
                continue
            if o["dma"]:
                last[("dma", o["semkey"])] = i
            else:
                last[o["eng"]] = i
        self.bar_deps = set(last.values())
        self.bar_pending = set(self.ENGS)
        self.last_write = {k: v for k, v in self.last_write.items() if self.persistent(k)}
        self.readers = {k: v for k, v in self.readers.items() if self.persistent(k)}

    PERSIST = frozenset(("x", "h", "o", "ps", "WG", "WD", "stage", "Sst", "kTh", "Vh", "uh", "ident", "identb", "onesv",
                         "epsb", "rdm", "qdec", "kdec", "g128", "g1", "amask", "smask", "eyeN", "hsel", "n1g", "mng",
                         "n2g", "retg", "qg8", "kgs", "esink", "convw", "convb", "lng", "lnb", "oT"))

    def persistent(self, k):
        return (k[0] if isinstance(k, tuple) else k) in self.PERSIST

    def emit(self):
        nc = self.nc
        ops = self.ops
        if self.out_dma_ops:
            ops.append(dict(eng="sp", fn=None, deps=set(self.out_dma_ops), dma=False, semkey=None,
                            sig=False, ev=None))
        for o in ops:
            for d in o["deps"]:
                ops[d]["sig"] = True
        eng_sems = {e: [] for e in self.ENGS}
        eng_cnt = {e: 0 for e in self.ENGS}
        dma_sems = {}
        dma_cnt = {}
        nsem = [0]

        def new_sem():
            nsem[0] += 1
            return self.es.enter_context(nc.semaphore("s%d" % nsem[0]))

        for o in ops:
            if o["fn"] is None:
                continue
            if o["dma"]:
                k = o["semkey"]
                if k not in dma_sems or dma_cnt[k] + 16 >= SEM_LIMIT:
                    dma_sems[k] = new_sem()
                    dma_cnt[k] = 0
                dma_cnt[k] += 16
                o["ev"] = (dma_sems[k], dma_cnt[k])
            elif o["sig"]:
                e = o["eng"]
                if not eng_sems[e] or eng_cnt[e] >= SEM_LIMIT:
                    eng_sems[e].append(new_sem())
                    eng_cnt[e] = 0
                eng_cnt[e] += 1
                o["ev"] = (eng_sems[e][-1], eng_cnt[e])
        nwaits = [0]

        def run_engine(e, eng):
            seen = {}
            for o in ops:
                if o["eng"] != e:
                    continue
                need = {}
                for d in o["deps"]:
                    sem, val = ops[d]["ev"]
                    key = id(sem)
                    if seen.get(key, 0) >= val:
                        continue
                    if key not in need or need[key][1] < val:
                        need[key] = (sem, val)
                for key, (sem, val) in need.items():
                    eng.wait_ge(sem, val)
                    seen[key] = val
                    nwaits[0] += 1
                if o["fn"] is None:
                    continue
                ins = o["fn"](eng)
                if o["ev"] is not None:
                    ins.then_inc(o["ev"][0], 16 if o["dma"] else 1)

        with nc.Block() as block:
            @block.tensor
            def _(eng):
                run_engine("pe", eng)

            @block.scalar
            def _(eng):
                run_engine("act", eng)

            @block.vector
            def _(eng):
                run_engine("dve", eng)

            @block.gpsimd
            def _(eng):
                run_engine("pool", eng)

            @block.sync
            def _(eng):
                run_engine("sp", eng)
        self.stats = dict(n_ops=len(ops), n_waits=nwaits[0], n_sems=nsem[0])


def const_tables():
    lg = np.log1p(-np.exp2(-5.0 - np.arange(4, dtype=np.float64)))
    slopes = np.exp2(-8.0 * (np.arange(4, dtype=np.float64) + 1.0) / 4.0)
    j = np.arange(128)[:, None].astype(np.float64)
    i = np.arange(128)[None, :].astype(np.float64)
    t = {}
    rdm = np.zeros((128, 4, 128), np.float64)
    for h in range(4):
        rdm[:, h, :] = np.where(i >= j, np.exp(lg[h] * np.maximum(i - j, 0.0)), 0.0)
    t["c_rdm"] = rdm
    qdec = np.zeros((128, 2, 128), np.float64)
    kdec = np.zeros((128, 4), np.float64)
    g128 = np.zeros((128, 2), np.float64)
    g1 = np.zeros((128, 2), np.float64)
    for pr in range(2):
        for hh in range(2):
            h = 2 * pr + hh
            qdec[64 * hh:64 * hh + 64, pr, :] = np.exp(lg[h] * (np.arange(128) + 1.0))[None, :]
            g128[64 * hh:64 * hh + 64, pr] = np.exp(lg[h] * 128.0)
            g1[64 * hh:64 * hh + 64, pr] = np.exp(lg[h])
    for h in range(4):
        kdec[:, h] = np.exp(lg[h] * (127.0 - np.arange(128))) * 0.125
    t["c_qdec"] = qdec
    t["c_kdec"] = kdec
    t["c_g128"] = g128
    t["c_g1"] = g1
    am = np.zeros((128, 2, 2, 2, 128), np.float64)
    for g in range(2):
        for hh in range(2):
            h = 2 * g + hh
            dist_prev = 128.0 + i - j
            am[:, g, hh, 0, :] = np.where(j > i, np.exp(-slopes[h] * dist_prev), 0.0)
            am[:, g, hh, 1, :] = np.where(i >= j, np.exp(-slopes[h] * (i - j)), 0.0)
    t["c_amask"] = am
    sm = np.zeros((128, 4), np.float64)
    p = np.arange(128, dtype=np.float64)
    for h in range(4):
        sm[:, h] = np.where(p >= 1, np.exp(-slopes[h] * (128.0 - p)), 0.0)
    t["c_smask"] = sm
    t["c_eye"] = np.eye(NS)
    hs = np.zeros((128, 2, 128), np.float64)
    hs[0:64, 0, :] = 1.0
    hs[64:128, 1, :] = 1.0
    t["c_hsel"] = hs
    return {k: np.ascontiguousarray(v, dtype=np.float32) for k, v in t.items()}, lg


def build():
    ctab, lg = const_tables()
    gam = [float(np.exp(x)) for x in lg]
    nc = bass.Bass("TRN2", target_bir_lowering=False)

    def din(name, shape):
        return nc.dram_tensor(name, list(shape), F32, kind="ExternalInput").ap()

    def dout(name, shape):
        return nc.dram_tensor(name, list(shape), F32, kind="ExternalOutput").ap()

    xp = din("xp", (NSEG * TP, D))
    xs = din("xs", (NS, D))
    sret = din("sret", (DEPTH, NS, 4, 64, 128))
    ck = din("ck", (DEPTH, NS, 128, 128))
    cv = din("cv", (DEPTH, NS, 128, 128))
    sconv = din("sconv", (DEPTH, NS, 30, 256))
    W = {}
    for nm, shp in (("ffn1_norm", (DEPTH, D)), ("ffn1_wg", (DEPTH, D, FF)), ("ffn1_wu", (DEPTH, D, FF)),
                    ("ffn1_wd", (DEPTH, FF, D)), ("mix_norm", (DEPTH, D)), ("w_in", (DEPTH, D, IN_DIM)),
                    ("ret_norm_g", (DEPTH, 4, 128)), ("q_norm_g", (DEPTH, 64)), ("k_norm_g", (DEPTH, 64)),
                    ("sinks", (DEPTH, 4)), ("conv_w", (DEPTH, 31, 256)), ("conv_b", (DEPTH, 256)),
                    ("conv_ln_g", (DEPTH, 256)), ("conv_ln_b", (DEPTH, 256)), ("conv_pw", (DEPTH, 256, 256)),
                    ("w_out", (DEPTH, D, D)), ("ffn2_norm", (DEPTH, D)), ("ffn2_wg", (DEPTH, D, FF)),
                    ("ffn2_wu", (DEPTH, D, FF)), ("ffn2_wd", (DEPTH, FF, D))):
        W[nm] = din(nm, shp)
    C = {k: din(k, v.shape) for k, v in ctab.items()}
    yp = dout("yp", (NSEG * TP, D))
    ys = dout("ys", (NS, D))
    retp = dout("retp", (DEPTH, 4, 64, 128))
    rets = dout("rets", (DEPTH, NS, 4, 64, 128))
    kwp = dout("kwp", (DEPTH, 128, 128))
    kws = dout("kws", (DEPTH, NS, 128, 128))
    vwp = dout("vwp", (DEPTH, 128, 128))
    vws = dout("vws", (DEPTH, NS, 128, 128))
    cvp = dout("cvp", (DEPTH, 30, 256))
    cvs = dout("cvs", (DEPTH, NS, 30, 256))

    es = ExitStack()
    with es:
        def sb(name, shape, dt):
            return es.enter_context(nc.sbuf_tensor(name, list(shape), dt))

        S = Sched(nc, es)
        ps = [es.enter_context(nc.psum_tensor("ps%d" % i, [128, 512], F32)) for i in range(8)]
        pctr = [0]

        def pb():
            i = pctr[0] % 8
            pctr[0] += 1
            return ps[i], ("ps", i)

        def MM(out, lhsT, rhs, st, sp, r, w):
            S.op("pe", lambda e: e.matmul(out, lhsT=lhsT, rhs=rhs, start=st, stop=sp), r, w)

        def TR(out, in_, idn, r, w):
            S.op("pe", lambda e: e.transpose(out, in_, idn), r, w)

        def ACT(out, in_, func, r, w, bias=None, scale=1.0):
            if bias is None:
                S.op("act", lambda e: e.activation(out=out, in_=in_, func=func, scale=scale), r, w)
            else:
                S.op("act", lambda e: e.activation(out=out, in_=in_, func=func, bias=bias, scale=scale), r, w)

        def STT(out, in0, scalar, in1, op0, op1, r, w, eng="dve"):
            S.op(eng, lambda e: e.scalar_tensor_tensor(out=out, in0=in0, scalar=scalar, in1=in1, op0=op0, op1=op1), r, w)

        def TT(out, in0, in1, op, r, w, eng="dve"):
            S.op(eng, lambda e: e.tensor_tensor(out=out, in0=in0, in1=in1, op=op), r, w)

        def TS(out, in0, s1, op0, r, w, s2=None, op1=None, eng="dve"):
            if op1 is None:
                S.op(eng, lambda e: e.tensor_scalar(out=out, in0=in0, scalar1=s1, scalar2=None, op0=op0), r, w)
            else:
                S.op(eng, lambda e: e.tensor_scalar(out=out, in0=in0, scalar1=s1, scalar2=s2, op0=op0, op1=op1), r, w)

        def CP(out, in_, r, w, eng="dve"):
            if DBG.get("noactcp") and eng == "act":
                eng = "dve"
            if eng == "act":
                S.op("act", lambda e: e.activation(out=out, in_=in_, func=AF.Identity), r, w)
            else:
                S.op(eng, lambda e: e.tensor_copy(out, in_), r, w)

        def RCP(out, in_, r, w):
            S.op("dve", lambda e: e.reciprocal(out, in_), r, w)

        def MSET(ap, val, w, eng="dve"):
            S.op(eng, lambda e: e.memset(ap, val), (), w)

        def DMA(q, out, in_, r, w, key, final=False, slow=False):
            if slow:
                S.op(q, lambda e: e.dma_start(out=out, in_=in_, allow_slow_non_contiguous=True), r, w, dma=True,
                     semkey=key, final=final)
            else:
                S.op(q, lambda e: e.dma_start(out=out, in_=in_), r, w, dma=True, semkey=key, final=final)

        xT = sb("xT", [128, 8, NT], F32)
        hT = sb("hT", [128, 8, NT], BF16)
        oT = sb("oT", [128, 10, NT], BF16)
        ident = sb("ident", [128, 128], F32)
        identb = sb("identb", [128, 128], BF16)
        onesv = sb("onesv", [128, 5, 128], BF16)
        epsb = sb("epsb", [128, 1], F32)
        stage = sb("stage", [128, 2, D], F32)
        rdm = sb("rdm", [128, 4, 128], F32)
        qdec = sb("qdec", [128, 2, 128], F32)
        kdec = sb("kdec", [128, 4], F32)
        g128 = sb("g128", [128, 2], F32)
        g1 = sb("g1", [128, 2], F32)
        amask = sb("amask", [128, 2, 512], BF16)
        smask = sb("smask", [128, 4], F32)
        eyeN = sb("eyeN", [NS, NS], BF16)
        hsel = sb("hsel", [128, 2, 128], BF16)
        n1g = sb("n1g", [128, DEPTH, 8], F32)
        mng = sb("mng", [128, DEPTH, 8], F32)
        n2g = sb("n2g", [128, DEPTH, 8], F32)
        retg = sb("retg", [128, DEPTH, 4], F32)
        qg8 = sb("qg8", [64, DEPTH], F32)
        kgs = sb("kgs", [64, DEPTH], F32)
        esink = sb("esink", [64, DEPTH * 4], F32)
        convw = sb("convw", [128, DEPTH, 2, 31], F32)
        convb = sb("convb", [128, DEPTH, 2], F32)
        lng = sb("lng", [128, DEPTH, 2], F32)
        lnb = sb("lnb", [128, DEPTH, 2], F32)
        Sst = sb("Sst", [128, DEPTH, 2, 128], F32)
        kTh = sb("kTh", [64, DEPTH, 2, 128], BF16)
        Vh = sb("Vh", [128, DEPTH, 128], BF16)
        uh = sb("uh", [128, DEPTH, 2, 30], BF16)
        NA = 16640
        arena = sb("arena", [128, NA], F32)
        aoff = [0]
        WG = [sb("WG%d" % i, [128, 8192], BF16) for i in range(2)]
        WD = [sb("WD%d" % i, [128, 4096], BF16) for i in range(2)]
        wgc = [0]
        wdc = [0]

        def wg_slot():
            i = wgc[0] % 2
            wgc[0] += 1
            return WG[i], ("WG", i)

        def wd_slot():
            i = wdc[0] % 2
            wdc[0] += 1
            return WD[i], ("WD", i)

        def wview(Wt, off, a, b, parts=128):
            return Wt[0:parts, off:off + a * b].rearrange("p (a b) -> p a b", a=a, b=b)

        def WDMA(out, in_, key):
            S.op("pool", lambda e: e.dma_start(out=out, in_=in_), (), [key], dma=True, semkey=key, nobar=True)

        def areset():
            aoff[0] = 0

        def aalloc(shape, dt, parts=128):
            n = int(np.prod(shape))
            words = (n + 1) // 2 if dt == BF16 else n
            o = aoff[0]
            aoff[0] += words
            assert aoff[0] <= NA, ("arena overflow", aoff[0])
            a = arena[0:parts, o:o + words]
            if dt == BF16:
                a = a.bitcast(BF16)
                if n % 2:
                    a = a[:, 0:n]
            if len(shape) == 2:
                return a.rearrange("p (a b) -> p a b", a=shape[0], b=shape[1])
            if len(shape) == 3:
                return a.rearrange("p (a b c) -> p a b c", a=shape[0], b=shape[1], c=shape[2])
            return a

        MSET(ident[:], 0.0, ["ident"], eng="pool")
        S.op("pool", lambda e: e.affine_select(out=ident[:], in_=ident[:], pattern=[[-1, 128]],
                                               compare_op=ALU.not_equal, fill=1.0, base=0, channel_multiplier=1),
             ["ident"], ["ident"])
        CP(identb[:], ident[:], ["ident"], ["identb"])
        for i, v in enumerate((1.0 / 1024, 1.0 / 64, 1.0 / 128, 1.0 / 256, 1.0)):
            MSET(onesv[:, i, :], v, ["onesv"])
        MSET(epsb[:], EPS, ["epsb"])
        MSET(Sst[:], 0.0, ["Sst"])
        MSET(kTh[:], 0.0, ["kTh"])
        MSET(Vh[:], 0.0, ["Vh"])
        MSET(uh[:], 0.0, ["uh"])
        if DBG.get("zero_o"):
            MSET(oT[:], 0.0, ["oT"])
        DMA("sp", rdm[:], C["c_rdm"], (), ["rdm"], "c_rdm")
        DMA("sp", qdec[:], C["c_qdec"], (), ["qdec"], "c_qdec")
        DMA("sp", kdec[:], C["c_kdec"], (), ["kdec"], "c_kdec")
        DMA("sp", g128[:], C["c_g128"], (), ["g128"], "c_g128")
        DMA("sp", g1[:], C["c_g1"], (), ["g1"], "c_g1")
        DMA("sp", smask[:], C["c_smask"], (), ["smask"], "c_smask")
        if not DBG.get("nopoolc"):
          DMA("pool", amask[:], C["c_amask"].rearrange("p a b c d -> p a (b c d)"), (), ["amask"], "c_amask")
          DMA("pool", eyeN[:], C["c_eye"], (), ["eyeN"], "c_eyeN")
          DMA("pool", hsel[:], C["c_hsel"], (), ["hsel"], "c_hsel")
        for l in range(DEPTH if not DBG.get("noparams") else 0):
            DMA("sp", n1g[:, l, :], W["ffn1_norm"][l].rearrange("(c p) -> p c", p=128), (), ["n1g"], "c_n1g", slow=True)
            DMA("sp", mng[:, l, :], W["mix_norm"][l].rearrange("(c p) -> p c", p=128), (), ["mng"], "c_mng", slow=True)
            DMA("sp", n2g[:, l, :], W["ffn2_norm"][l].rearrange("(c p) -> p c", p=128), (), ["n2g"], "c_n2g", slow=True)
            DMA("sp", retg[:, l, :], W["ret_norm_g"][l].rearrange("h p -> p h"), (), ["retg"], "c_retg", slow=True)
            DMA("sp", qg8[:, l:l + 1], W["q_norm_g"][l].rearrange("(p o) -> p o", o=1), (), ["qg8"], "c_qg8", slow=True)
            DMA("sp", kgs[:, l:l + 1], W["k_norm_g"][l].rearrange("(p o) -> p o", o=1), (), ["kgs"], "c_kgs", slow=True)
            DMA("sp", esink[:, l * 4:(l + 1) * 4], W["sinks"][l:l + 1, :].partition_broadcast(64), (), ["esink"],
                "c_esink", slow=True)
            for c in range(2):
                DMA("sp", convw[:, l, c, :], W["conv_w"][l][:, c * 128:(c + 1) * 128].rearrange("j p -> p j"), (),
                    ["convw"], "c_convw", slow=True)
            DMA("sp", convb[:, l, :], W["conv_b"][l].rearrange("(c p) -> p c", p=128), (), ["convb"], "c_convb", slow=True)
            DMA("sp", lng[:, l, :], W["conv_ln_g"][l].rearrange("(c p) -> p c", p=128), (), ["lng"], "c_lng", slow=True)
            DMA("sp", lnb[:, l, :], W["conv_ln_b"][l].rearrange("(c p) -> p c", p=128), (), ["lnb"], "c_lnb", slow=True)
        S.barrier()
        TS(qg8[:], qg8[:], 0.125, ALU.mult, ["qg8"], ["qg8"])
        ACT(esink[:], esink[:], AF.Exp, ["esink"], ["esink"])
        S.barrier()

        def tiles_of(si):
            t = [(0, 512), (512, 512)]
            if si == 0:
                t.append((TP, NS))
            return t

        def load_x(si):
            chunks = [(xp[si * TP + n * 128: si * TP + (n + 1) * 128, :], n * 128, 128) for n in range(TP // 128)]
            if si == 0 and not DBG.get("nosample"):
                chunks.append((xs[:, :], TP, NS))
            if DBG.get("noload"):
                chunks = []
            for ci, (src, t0, n) in enumerate(chunks):
                sl = ci % 2
                DMA("sp", stage[0:n, sl, :], src, (), [("stage", sl)], ("stage", sl))
                for half in range(2):
                    p, pk = pb()
                    for q in range(4):
                        c = half * 4 + q
                        TR(p[:, q * 128:q * 128 + n], stage[0:n, sl, c * 128:(c + 1) * 128], ident[0:n, 0:n],
                           [("stage", sl), "ident"], [pk])
                    for q in range(4):
                        c = half * 4 + q
                        CP(xT[:, c, t0:t0 + n], p[:, q * 128:q * 128 + n], [pk],
                           [("x", c, (t0 // 512) * 512 if t0 < TP else TP)], eng="act" if half else "dve")

        def store_y(si):
            chunks = [(yp[si * TP + n * 128: si * TP + (n + 1) * 128, :], n * 128, 128) for n in range(TP // 128)]
            if si == 0 and not DBG.get("nosample"):
                chunks.append((ys[:, :], TP, NS))
            if DBG.get("nostore"):
                chunks = chunks[:1]
            for ci, (dst, t0, n) in enumerate(chunks):
                sl = ci % 2
                tk = (t0 // 512) * 512 if t0 < TP else TP
                for half in range(2):
                    p, pk = pb()
                    for q in range(4):
                        c = half * 4 + q
                        TR(p[0:n, q * 128:(q + 1) * 128], xT[:, c, t0:t0 + n], ident[:, :],
                           [("x", c, tk), "ident"], [pk])
                    CP(stage[0:n, sl, half * 512:(half + 1) * 512], p[0:n, :], [pk], [("stage", sl)],
                       eng="act" if half else "dve")
                DMA("sp", dst, stage[0:n, sl, :], [("stage", sl)], (), ("stage", sl), final=True)

        def xkey(c, t0):
            return ("x", c, t0)

        def xkeys(c, t0, n):
            return [("x", c, t0)]

        def rstd_from(psmean, pk, n, parts, rbuf, rk):
            ACT(rbuf[0:parts, 0:n], psmean, AF.Ln, [pk, "epsb"], [rk], bias=epsb[0:parts, 0:1])
            ACT(rbuf[0:parts, 0:n], rbuf[0:parts, 0:n], AF.Exp, [rk], [rk], scale=-0.5)

        def rmsnorm(si, gain, l):
            sq = aalloc([8, 512], BF16)
            rb = aalloc([2, 512], F32)
            for ti, (t0, n) in enumerate(tiles_of(si)):
                p, pk = pb()
                for c in range(8):
                    ACT(sq[:, c, 0:n], xT[:, c, t0:t0 + n], AF.Square, [("x", c, t0)], [("sq", c)])
                for c in range(8):
                    MM(p[:, 0:n], onesv[:, 0, :], sq[:, c, 0:n], c == 0, c == 7, [("sq", c), "onesv"], [pk])
                rk = ("rstd", ti % 2)
                rstd_from(p[:, 0:n], pk, n, 128, rb[:, ti % 2, :], rk)
                for c in range(8):
                    STT(hT[:, c, t0:t0 + n], xT[:, c, t0:t0 + n], gain[:, l, c:c + 1], rb[:, ti % 2, 0:n],
                        ALU.mult, ALU.mult, [("x", c, t0), rk], [("h", t0)])

        def fix_xkeys(si):
            pass

        def ffn(si, l, which):
            areset()
            wg_d = W["ffn%d_wg" % which][l]
            wu_d = W["ffn%d_wu" % which][l]
            wd_d = W["ffn%d_wd" % which][l]
            gain = n1g if which == 1 else n2g
            act = [aalloc([4, NT], BF16) for _ in range(2)]
            sg = aalloc([2, 512], F32)
            rmsnorm(si, gain, l)
            groups = [(g * 512, 4) for g in range(5)] + [(2560, 2)]
            tl = tiles_of(si)
            gslots = {}

            def gateup(j):
                f0, nfc = groups[j]
                sl = j % 2
                Wg, gk = wg_slot()
                Wd, dk = wd_slot()
                wgv = wview(Wg, 0, 8, 512)
                wuv = wview(Wg, 4096, 8, 512)
                wdv = wview(Wd, 0, 4, D)
                gslots[j] = (wdv, dk)
                WDMA(wgv[:, :, 0:nfc * 128], wg_d[:, f0:f0 + nfc * 128].rearrange("(k p) f -> p k f", p=128), gk)
                WDMA(wuv[:, :, 0:nfc * 128], wu_d[:, f0:f0 + nfc * 128].rearrange("(k p) f -> p k f", p=128), gk)
                WDMA(wdv[:, 0:nfc, :], wd_d[f0:f0 + nfc * 128, :].rearrange("(k p) d -> p k d", p=128), dk)
                cnt = 0
                for (t0, n) in tl:
                    for fc in range(nfc):
                        pg, pgk = pb()
                        pu, puk = pb()
                        for k in range(8):
                            MM(pg[:, 0:n], wgv[:, k, fc * 128:(fc + 1) * 128], hT[:, k, t0:t0 + n], k == 0, k == 7,
                               [gk, ("h", t0)], [pgk])
                        for k in range(8):
                            MM(pu[:, 0:n], wuv[:, k, fc * 128:(fc + 1) * 128], hT[:, k, t0:t0 + n], k == 0, k == 7,
                               [gk, ("h", t0)], [puk])
                        s2 = cnt % 2
                        cnt += 1
                        ACT(sg[:, s2, 0:n], pg[:, 0:n], AF.Silu, [pgk], [("sg", s2)])
                        TT(act[sl][:, fc, t0:t0 + n], sg[:, s2, 0:n], pu[:, 0:n], ALU.mult, [("sg", s2), puk],
                           [("act", sl, t0)])

            def down(j):
                f0, nfc = groups[j]
                sl = j % 2
                wdv, dk = gslots[j]
                for (t0, n) in tl:
                    for c in range(8):
                        p, pk = pb()
                        for fc in range(nfc):
                            MM(p[:, 0:n], wdv[:, fc, c * 128:(c + 1) * 128], act[sl][:, fc, t0:t0 + n], fc == 0,
                               fc == nfc - 1, [dk, ("act", sl, t0)], [pk])
                        STT(xT[:, c, t0:t0 + n], p[:, 0:n], 0.5, xT[:, c, t0:t0 + n], ALU.mult, ALU.add,
                            [pk, ("x", c, t0)], [("x", c, t0)])

            for j in range(len(groups) + 1):
                if j < len(groups):
                    gateup(j)
                if j >= 1:
                    down(j - 1)
            if which == 1:
                S.barrier()

        def ret_epilogue(Oin, Okey, n, l, h, hh, sgT, t0, scr, ebank=None):
            Osb, sqb, rb = scr
            CP(Osb[:, 0:n], Oin, [Okey], ["Osb"], eng="act")
            ACT(sqb[:, 0:n], Oin, AF.Square, [Okey], ["sqb"])
            p, pk = pb() if ebank is None else (ps[ebank], ("ps", ebank))
            MM(p[:, 0:n], onesv[:, 2, :], sqb[:, 0:n], True, True, ["sqb", "onesv"], [pk])
            rstd_from(p[:, 0:n], pk, n, 128, rb, "rb")
            STT(Osb[:, 0:n], Osb[:, 0:n], retg[:, l, h:h + 1], rb[:, 0:n], ALU.mult, ALU.mult, ["Osb", "rb"], ["Osb"])
            TT(oT[:, h, t0:t0 + n], Osb[:, 0:n], sgT[:, hh, t0:t0 + n], ALU.mult, ["Osb", ("sgT", t0)], [("o", h, t0)])

        def retention(si, l, pr, last):
            areset()
            win = W["w_in"][l]
            Wg, gk = wg_slot()
            wqk = wview(Wg, 0, 8, 256)
            wv = wview(Wg, 2048, 8, 256)
            wgt = wview(Wg, 4096, 8, 256)
            Kp = aalloc([8, 128], BF16)
            Vt = aalloc([8, 256], BF16)
            qT = aalloc([1, NT], BF16)[:, 0, :]
            kT = aalloc([1, NT], BF16)[:, 0, :]
            qdT = aalloc([1, TP], BF16)[:, 0, :]
            sgT = aalloc([2, NT], BF16)
            AT = [aalloc([4, 128], BF16) for _ in range(2)]
            Sbf = aalloc([1, 128], BF16)[:, 0, :]
            Osb = aalloc([1, 512], F32)[:, 0, :]
            sqb = aalloc([1, 512], BF16)[:, 0, :]
            rb = aalloc([1, 512], F32)[:, 0, :]
            scr = (Osb, sqb, rb)
            wv3 = win.rearrange("(k p) f -> p k f", p=128)
            WDMA(wqk[:, :, 0:128], wv3[:, :, pr * 128:(pr + 1) * 128], gk)
            WDMA(wqk[:, :, 128:256], wv3[:, :, 256 + pr * 128:256 + (pr + 1) * 128], gk)
            WDMA(wv[:], wv3[:, :, 512 + pr * 256:512 + (pr + 1) * 256], gk)
            WDMA(wgt[:], wv3[:, :, 1024 + pr * 256:1024 + (pr + 1) * 256], gk)
            Sf = Sst[:, l, pr, :]
            CP(Sbf, Sf, ["Sst"], ["Sbf"])
            for n in range(8):
                p, pk = pb()
                p2, pk2 = pb()
                for k in range(8):
                    MM(p[:, 0:128], hT[:, k, n * 128:(n + 1) * 128], wqk[:, k, 128:256], k == 0, k == 7,
                       [("h", (n // 4) * 512), gk], [pk])
                for k in range(8):
                    MM(p2[:, 0:256], hT[:, k, n * 128:(n + 1) * 128], wv[:, k, :], k == 0, k == 7,
                       [("h", (n // 4) * 512), gk], [pk2])
                for hh in range(2):
                    TS(Kp[:, n, hh * 64:(hh + 1) * 64], p[:, hh * 64:(hh + 1) * 64], kdec[:, 2 * pr + hh:2 * pr + hh + 1],
                       ALU.mult, [pk, "kdec"], [("Kp", n)])
                CP(Vt[:, n, :], p2[:, 0:256], [pk2], [("Vt", n)], eng="act")
            for (t0, n) in tiles_of(si):
                p, pk = pb()
                for k in range(8):
                    MM(p[:, 0:n], wqk[:, k, 0:128], hT[:, k, t0:t0 + n], k == 0, k == 7, [gk, ("h", t0)], [pk])
                CP(qT[:, t0:t0 + n], p[:, 0:n], [pk], [("qT", t0)], eng="dve")
                if t0 < TP:
                    TT(qdT[:, t0:t0 + n].rearrange("p (a b) -> p a b", a=4), p[:, 0:n].rearrange("p (a b) -> p a b", a=4),
                       qdec[:, pr:pr + 1, :].to_broadcast([128, 4, 128]), ALU.mult, [pk, "qdec"], [("qdT", t0)])
                p, pk = pb()
                for k in range(8):
                    MM(p[:, 0:n], wqk[:, k, 128:256], hT[:, k, t0:t0 + n], k == 0, k == 7, [gk, ("h", t0)], [pk])
                ACT(kT[:, t0:t0 + n], p[:, 0:n], AF.Identity, [pk], [("kT", t0)], scale=0.125)
                if t0 >= TP:
                    for hh in range(2):
                        p, pk = pb()
                        for k in range(8):
                            MM(p[:, 0:n], wgt[:, k, hh * 128:(hh + 1) * 128], hT[:, k, t0:t0 + n], k == 0, k == 7,
                               [gk, ("h", t0)], [pk])
                        ACT(sgT[:, hh, t0:t0 + n], p[:, 0:n], AF.Silu, [pk], [("sgT", t0)])
            Sall = aalloc([8, 128], BF16)
            AT2 = [[AT[0], AT[1]], [aalloc([4, 128], BF16), aalloc([4, 128], BF16)]]
            for n in range(8):
                b = n // 2
                off = (n % 2) * 256
                MM(ps[b][:, off:off + 128], Kp[:, n, :], Vt[:, n, 0:128], True, True, [("Kp", n), ("Vt", n)], [("ps", b)])
                MM(ps[b][:, off + 128:off + 256], Kp[:, n, :], Vt[:, n, 128:256], True, True, [("Kp", n), ("Vt", n)],
                   [("ps", b)])
            for ti in range(2):
                t0 = ti * 512
                for hh in range(2):
                    bank = 4 + 2 * ti + hh
                    lo, hi = 64 * hh, 64 * hh + 64
                    for nn in range(4):
                        c0 = t0 + nn * 128
                        MM(ps[bank][:, nn * 128:(nn + 1) * 128], kT[lo:hi, c0:c0 + 128], qT[lo:hi, c0:c0 + 128], True, True,
                           [("kT", t0), ("qT", t0)], [("ps", bank)])
                    TT(AT2[ti][hh][:], ps[bank][:, :].rearrange("p (a b) -> p a b", a=4),
                       rdm[:, 2 * pr + hh:2 * pr + hh + 1, :].to_broadcast([128, 4, 128]), ALU.mult, [("ps", bank), "rdm"],
                       [("AT", ti, hh)])
            for ti in range(2):
                t0 = ti * 512
                for hh in range(2):
                    bank = 4 + 2 * ti + hh
                    for k in range(8):
                        MM(ps[bank][:, :], wgt[:, k, hh * 128:(hh + 1) * 128], hT[:, k, t0:t0 + 512], k == 0, k == 7,
                           [gk, ("h", t0)], [("ps", bank)])
                    ACT(sgT[:, hh, t0:t0 + 512], ps[bank][:, :], AF.Silu, [("ps", bank)], [("sgT", t0)])
            for n in range(8):
                b = n // 2
                off = (n % 2) * 256
                CP(Sall[:, n, :], Sst[:, l, pr, :], ["Sst"], [("Sall", n)])
                for hh in range(2):
                    lo, hi = 64 * hh, 64 * hh + 64
                    STT(Sst[lo:hi, l, pr, :], Sst[lo:hi, l, pr, :], gam[2 * pr + hh] ** 128,
                        ps[b][lo:hi, off + hh * 128:off + (hh + 1) * 128], ALU.mult, ALU.add, [("ps", b), "Sst"], ["Sst"])
            for ti in range(2):
                t0 = ti * 512
                for hh in range(2):
                    bank = 2 * ti + hh
                    lo, hi = 64 * hh, 64 * hh + 64
                    for nn in range(4):
                        n = ti * 4 + nn
                        c0 = t0 + nn * 128
                        MM(ps[bank][:, nn * 128:(nn + 1) * 128], Vt[:, n, hh * 128:(hh + 1) * 128], AT2[ti][hh][:, nn, :],
                           True, False, [("Vt", n), ("AT", ti, hh)], [("ps", bank)])
                        MM(ps[bank][:, nn * 128:(nn + 1) * 128], Sall[lo:hi, n, :], qdT[lo:hi, c0:c0 + 128], False, True,
                           [("Sall", n), ("qdT", t0)], [("ps", bank)])
                for hh in range(2):
                    bank = 2 * ti + hh
                    ret_epilogue(ps[bank][:, :], ("ps", bank), 512, l, 2 * pr + hh, hh, sgT, t0, scr, ebank=4 + 2 * ti + hh)
            pctr[0] = 0
            if last:
                for hh in range(2):
                    DMA("sp", retp[l, 2 * pr + hh], Sst[64 * hh:64 * hh + 64, l, pr, :], ["Sst"], (), "retp", final=True)
            if si == 0:
                t0 = TP
                S0f = aalloc([NS, 128], F32)
                S0b = aalloc([NS, 128], BF16)
                Ks = aalloc([1, 128], BF16, parts=NS)[:, 0, :]
                Vs = aalloc([1, 256], BF16, parts=NS)[:, 0, :]
                Vbd1 = aalloc([NS, 128], BF16, parts=NS)
                Vbd = [Vbd1, Vbd1]
                vTs = aalloc([2, NS], F32)
                prod = aalloc([1, NS], BF16)[:, 0, :]
                tmp = aalloc([1, NS], F32)[:, 0, :]
                Os = aalloc([1, NS], F32)[:, 0, :]
                Sn = aalloc([2, 512], F32)
                DMA("sp", S0f[:], sret[l, :, 2 * pr:2 * pr + 2].rearrange("b h d v -> (h d) b v"), (), ["S0f"], "S0f")
                CP(S0b[:], S0f[:], ["S0f"], ["S0b"])
                p, pk = pb()
                for k in range(8):
                    MM(p[0:NS, 0:128], hT[:, k, t0:t0 + NS], wqk[:, k, 128:256], k == 0, k == 7, [("h", t0), gk], [pk])
                for k in range(8):
                    MM(p[0:NS, 128:384], hT[:, k, t0:t0 + NS], wv[:, k, :], k == 0, k == 7, [("h", t0), gk], [pk])
                ACT(Ks, p[0:NS, 0:128], AF.Identity, [pk], ["Ks"], scale=0.125)
                CP(Vs, p[0:NS, 128:384], [pk], ["Vs"], eng="act")
                for hh in range(2):
                    p, pk = pb()
                    for k in range(8):
                        MM(p[:, 0:NS], wv[:, k, hh * 128:(hh + 1) * 128], hT[:, k, t0:t0 + NS], k == 0, k == 7,
                           [gk, ("h", t0)], [pk])
                    CP(vTs[:, hh, :], p[:, 0:NS], [pk], [("vTs", hh)], eng="act")
                TT(prod, qT[:, t0:t0 + NS], kT[:, t0:t0 + NS], ALU.mult, [("qT", t0), ("kT", t0)], ["prod"])
                for hh in range(2):
                    lo, hi = 64 * hh, 64 * hh + 64
                    h = 2 * pr + hh
                    pt, ptk = pb()
                    for b in range(NS):
                        MM(pt[:, b:b + 1], S0b[lo:hi, b, :], qT[lo:hi, t0 + b:t0 + b + 1], True, True,
                           ["S0b", ("qT", t0)], [ptk])
                    pq, pqk = pb()
                    MM(pq[:, 0:NS], hsel[:, hh, :], prod, True, True, ["hsel", "prod"], [pqk])
                    TT(tmp, pq[:, 0:NS], vTs[:, hh, :], ALU.mult, [pqk, ("vTs", hh)], ["tmp"])
                    STT(Os, pt[:, 0:NS], gam[h], tmp, ALU.mult, ALU.add, [ptk, "tmp"], ["Os"])
                    ret_epilogue(Os, "Os", NS, l, h, hh, sgT, t0, scr)
                    TT(Vbd[hh][:], Vs[:, hh * 128:(hh + 1) * 128].unsqueeze(1).to_broadcast([NS, NS, 128]),
                       eyeN[:, :].unsqueeze(2).to_broadcast([NS, NS, 128]), ALU.mult, ["Vs", "eyeN"], ["Vbd"])
                    for q in range(NS // 4):
                        pn, pnk = pb()
                        MM(pn[:, :], Ks[:, :], Vbd[hh][:, 4 * q:4 * q + 4, :].rearrange("p a b -> p (a b)"), True, True, ["Ks", "Vbd"], [pnk])
                        s2 = q % 2
                        STT(Sn[lo:hi, s2, :], S0f[lo:hi, 4 * q:4 * q + 4, :].rearrange("p a b -> p (a b)"), gam[h], pn[lo:hi, :], ALU.mult, ALU.add,
                            [pnk, "S0f"], [("Sn", s2)])
                        DMA("sp", rets[l, 4 * q:4 * q + 4, h].rearrange("b d v -> d b v"),
                            Sn[lo:hi, s2, :].rearrange("p (b v) -> p b v", b=4), [("Sn", s2)], (), ("Sn", s2), final=True)
            S.barrier()

        def attention(si, l, last):
            areset()
            wv3 = W["w_in"][l].rearrange("(k p) f -> p k f", p=128)
            Wg, gk = wg_slot()
            wa = wview(Wg, 0, 8, 512)
            Va = aalloc([9, 128], BF16)
            qn = [aalloc([1, NT], BF16, parts=64)[:, 0, :] for _ in range(4)]
            kn = [aalloc([1, 128 + NT], BF16, parts=64)[:, 0, :] for _ in range(2)]
            qsb = aalloc([1, 512], F32, parts=64)[:, 0, :]
            sqb = aalloc([1, 512], BF16, parts=64)[:, 0, :]
            rb = aalloc([1, 512], F32, parts=64)[:, 0, :]
            knf = aalloc([2, 128], F32, parts=64)
            vlast = aalloc([1, 128], F32)[:, 0, :]
            E = [aalloc([1, 512], BF16)[:, 0, :] for _ in range(4)]
            PT = [aalloc([1, 512], BF16)[:, 0, :] for _ in range(4)]
            den = aalloc([1, 512], F32, parts=64)[:, 0, :]
            WDMA(wa[:], wv3[:, :, 1536:2048], gk)
            CP(Va[:, 0, :], Vh[:, l, :], ["Vh"], [("Va", 0)])
            for g in range(2):
                CP(kn[g][:, 0:128], kTh[:, l, g, :], ["kTh"], [("kn", g, -1)])
            for n in range(8):
                p, pk = pb()
                for k in range(8):
                    MM(p[:, 0:128], hT[:, k, n * 128:(n + 1) * 128], wa[:, k, 384:512], k == 0, k == 7,
                       [("h", (n // 4) * 512), gk], [pk])
                CP(Va[:, n + 1, :], p[:, 0:128], [pk], [("Va", n + 1)], eng="act")
                if n == 7:
                    if last:
                        CP(vlast, p[:, 0:128], [pk], ["vlast"], eng="act")
                        DMA("sp", vwp[l], vlast, ["vlast"], (), "vwp", final=True)
                    CP(Vh[:, l, :], p[:, 0:128], [pk], ["Vh"], eng="act")
            sqb2 = [sqb, aalloc([1, 512], BF16, parts=64)[:, 0, :]]
            rb2 = [rb, aalloc([1, 512], F32, parts=64)[:, 0, :]]
            items = []
            for (t0, n) in tiles_of(si):
                for h in range(4):
                    items.append(("q", h, t0, n))
                for g in range(2):
                    items.append(("k", g, t0, n))
            state = {}

            def b_proj(i):
                kind, idx, t0, n = items[i]
                wcol = idx * 64 if kind == "q" else 256 + idx * 64
                p, pk = pb()
                for k in range(8):
                    MM(p[0:64, 0:n], wa[:, k, wcol:wcol + 64], hT[:, k, t0:t0 + n], k == 0, k == 7, [gk, ("h", t0)], [pk])
                state[i] = (p, pk)

            def b_finish(i):
                kind, idx, t0, n = items[i]
                p, pk = state.pop(i)
                s2 = i % 2
                ACT(sqb2[s2][:, 0:n], p[0:64, 0:n], AF.Square, [pk], [("sqbA", s2)])
                p2, pk2 = pb()
                MM(p2[0:64, 0:n], onesv[0:64, 1, 0:64], sqb2[s2][:, 0:n], True, True, [("sqbA", s2), "onesv"], [pk2])
                rstd_from(p2[0:64, 0:n], pk2, n, 64, rb2[s2], ("rbA", s2))
                if kind == "q":
                    STT(qn[idx][:, t0:t0 + n], p[0:64, 0:n], qg8[:, l:l + 1], rb2[s2][:, 0:n], ALU.mult, ALU.mult,
                        [pk, ("rbA", s2)], [("qn", idx, t0)])
                else:
                    g = idx
                    STT(kn[g][:, 128 + t0:128 + t0 + n], p[0:64, 0:n], kgs[:, l:l + 1], rb2[s2][:, 0:n], ALU.mult, ALU.mult,
                        [pk, ("rbA", s2)], [("kn", g, t0)])
                    if t0 == 512:
                        STT(knf[:, g, :], p[0:64, 384:512], kgs[:, l:l + 1], rb2[s2][:, 384:512], ALU.mult, ALU.mult,
                            [pk, ("rbA", s2)], [("knf", g)])
                        if last:
                            DMA("sp", kwp[l, :, g * 64:(g + 1) * 64].rearrange("t d -> d t"), knf[:, g, :], [("knf", g)],
                                (), "kwp", final=True, slow=True)
                        CP(kTh[:, l, g, :], kn[g][:, 128 + 896:128 + 1024], [("kn", g, 512)], ["kTh"])

            b_proj(0)
            for i in range(len(items)):
                if i + 1 < len(items):
                    b_proj(i + 1)
                b_finish(i)
            lnd = qsb
            it = 0
            for g in range(2):
                for ti in range(2):
                    t0 = ti * 512
                    base = 4 * (it % 2)
                    oth = 4 - base
                    it += 1
                    pO = [(ps[base + 0], ("ps", base + 0)), (ps[base + 1], ("ps", base + 1))]
                    pD = [(ps[base + 2], ("ps", base + 2)), (ps[base + 3], ("ps", base + 3))]
                    for nn in range(4):
                        n = ti * 4 + nn
                        c0 = n * 128
                        pS, pSk = ps[oth + nn], ("ps", oth + nn)
                        prevk = ("kn", g, ((c0 - 128) // 512) * 512) if c0 >= 128 else ("kn", g, -1)
                        for hh in range(2):
                            h = 2 * g + hh
                            MM(pS[:, hh * 256:hh * 256 + 128], kn[g][:, c0:c0 + 128], qn[h][:, c0:c0 + 128], True, True,
                               [prevk, ("qn", h, t0)], [pSk])
                            MM(pS[:, hh * 256 + 128:hh * 256 + 256], kn[g][:, 128 + c0:256 + c0], qn[h][:, c0:c0 + 128],
                               True, True, [("kn", g, t0), ("qn", h, t0)], [pSk])
                    for nn in range(4):
                        pS, pSk = ps[oth + nn], ("ps", oth + nn)
                        ACT(E[nn], pS[:, :], AF.Exp, [pSk], [("E", nn)])
                        TT(PT[nn], E[nn], amask[:, g, :], ALU.mult, [("E", nn), "amask"], [("PT", nn)])
                    for nn in range(4):
                        n = ti * 4 + nn
                        skip_prev = (si == 0 and n == 0)
                        s2 = nn
                        for hh in range(2):
                            cs = slice(nn * 128, (nn + 1) * 128)
                            prevP = PT[s2][:, hh * 256:hh * 256 + 128]
                            curP = PT[s2][:, hh * 256 + 128:hh * 256 + 256]
                            if not skip_prev:
                                MM(pO[hh][0][0:64, cs], Va[:, n, g * 64:(g + 1) * 64], prevP, True, False,
                                   [("Va", n), ("PT", s2)], [pO[hh][1]])
                            MM(pO[hh][0][0:64, cs], Va[:, n + 1, g * 64:(g + 1) * 64], curP, skip_prev, True,
                               [("Va", n + 1), ("PT", s2)], [pO[hh][1]])
                            if not skip_prev:
                                MM(pD[hh][0][0:64, cs], onesv[:, 4, 0:64], prevP, True, False, [("PT", s2), "onesv"],
                                   [pD[hh][1]])
                            MM(pD[hh][0][0:64, cs], onesv[:, 4, 0:64], curP, skip_prev, True, [("PT", s2), "onesv"],
                               [pD[hh][1]])
                    for hh in range(2):
                        h = 2 * g + hh
                        ACT(lnd, pD[hh][0][0:64, :], AF.Ln, [pD[hh][1], "esink"], ["qsb"],
                            bias=esink[:, l * 4 + h:l * 4 + h + 1])
                        ACT(den, lnd, AF.Exp, ["qsb"], ["den"], scale=-1.0)
                        TT(oT[0:64, 4 + h, t0:t0 + 512], pO[hh][0][0:64, :], den[:, :], ALU.mult, [pO[hh][1], "den"],
                           [("o", 4 + h, t0)])
            pctr[0] = 0
            if si == 0:
                t0 = TP
                HB = NS // 2
                kcf = aalloc([2 * HB, 64], F32)
                vcb = aalloc([NS, 128], BF16)
                kcT = aalloc([NS, 128], BF16, parts=64)
                vnT = aalloc([2, NS], F32, parts=64)
                knS = aalloc([2, NS], F32, parts=64)
                Es = aalloc([1, 4 * NS], F32)[:, 0, :]
                Ps = aalloc([4, NS], BF16)
                prod = aalloc([1, NS], BF16, parts=64)[:, 0, :]
                pn = aalloc([4, NS], F32, parts=64)
                num = aalloc([4, NS], F32, parts=64)
                dn = aalloc([4, NS], F32, parts=64)
                DMA("pool", vcb[:], cv[l].rearrange("b k f -> k b f"), (), ["vcb"], "vcb")
                DMA("sp", kws[l, :, 0:127, :], ck[l, :, 1:128, :], (), (), "kws", final=True)
                DMA("sp", vws[l, :, 0:127, :], cv[l, :, 1:128, :], (), (), "vws", final=True)
                for g in range(2):
                    p, pk = pb()
                    for k in range(8):
                        MM(p[0:64, 0:NS], wa[:, k, 384 + g * 64:384 + (g + 1) * 64], hT[:, k, t0:t0 + NS], k == 0, k == 7,
                           [gk, ("h", t0)], [pk])
                    CP(vnT[:, g, :], p[0:64, 0:NS], [pk], [("vnT", g)])
                    DMA("sp", vws[l, :, 127, g * 64:(g + 1) * 64].rearrange("b d -> d b"), vnT[:, g, :], [("vnT", g)], (),
                        "vws2", final=True, slow=True)
                    p, pk = pb()
                    for k in range(8):
                        MM(p[0:64, 0:NS], wa[:, k, 256 + g * 64:256 + (g + 1) * 64], hT[:, k, t0:t0 + NS], k == 0, k == 7,
                           [gk, ("h", t0)], [pk])
                    CP(qsb[:, 0:NS], p[0:64, 0:NS], [pk], ["qsb"], eng="act")
                    ACT(sqb[:, 0:NS], p[0:64, 0:NS], AF.Square, [pk], ["sqbA"])
                    p2, pk2 = pb()
                    MM(p2[0:64, 0:NS], onesv[0:64, 1, 0:64], sqb[:, 0:NS], True, True, ["sqbA", "onesv"], [pk2])
                    rstd_from(p2[0:64, 0:NS], pk2, NS, 64, rb, "rbA")
                    STT(knS[:, g, :], qsb[:, 0:NS], kgs[:, l:l + 1], rb[:, 0:NS], ALU.mult, ALU.mult, ["qsb", "rbA"],
                        [("knS", g)])
                    DMA("sp", kws[l, :, 127, g * 64:(g + 1) * 64].rearrange("b d -> d b"), knS[:, g, :], [("knS", g)], (),
                        "kws2", final=True, slow=True)
                pS, pSk = ps[7], ("ps", 7)
                for g in range(2):
                    for hb in range(2):
                        ks = (2 * g + hb) % 2
                        kslot = kcf[:, ks * HB:(ks + 1) * HB, :]
                        DMA("sp", kslot, ck[l, hb * HB:(hb + 1) * HB, :, g * 64:(g + 1) * 64].rearrange("b k f -> k b f"),
                            (), [("kcf", ks)], ("kcf", ks))
                        for q in range(HB // 4):
                            p, pk = ps[q % 4], ("ps", q % 4)
                            for bb in range(4):
                                TR(p[0:64, bb * 128:(bb + 1) * 128], kslot[:, 4 * q + bb, :], ident[:, :],
                                   [("kcf", ks), "ident"], [pk])
                            b0 = hb * HB + 4 * q
                            CP(kcT[:, b0:b0 + 4, :], p[0:64, :].rearrange("p (a b) -> p a b", a=4), [pk],
                               ["kcT"], eng="act" if q % 2 else "dve")
                    for hh in range(2):
                        h = 2 * g + hh
                        for b in range(NS):
                            MM(pS[:, h * NS + b:h * NS + b + 1], kcT[:, b, :], qn[h][:, t0 + b:t0 + b + 1], True, True,
                               ["kcT", ("qn", h, t0)], [pSk])
                pctr[0] = 0
                ACT(Es, pS[:, 0:4 * NS], AF.Exp, [pSk], ["Es"])
                TT(Ps[:], Es.rearrange("p (h b) -> p h b", h=4), smask[:, :].unsqueeze(2).to_broadcast([128, 4, NS]),
                   ALU.mult, ["Es", "smask"], ["Ps"])
                pO2, pO2k = pb()
                for h in range(4):
                    g = h // 2
                    for b in range(NS):
                        MM(pO2[0:64, h * NS + b:h * NS + b + 1], vcb[:, b, g * 64:(g + 1) * 64], Ps[:, h, b:b + 1], True, True,
                           ["vcb", "Ps"], [pO2k])
                pD2, pD2k = pb()
                MM(pD2[0:64, 0:4 * NS], onesv[:, 4, 0:64], Ps[:].rearrange("p h b -> p (h b)"), True, True,
                   ["Ps", "onesv"], [pD2k])
                pN, pNk = pb()
                for h in range(4):
                    g = h // 2
                    TT(prod, qn[h][:, t0:t0 + NS], kn[g][:, 128 + t0:128 + t0 + NS], ALU.mult,
                       [("qn", h, t0), ("kn", g, t0)], ["prodA"])
                    MM(pN[0:64, h * NS:(h + 1) * NS], onesv[0:64, 4, 0:64], prod, True, True, ["prodA", "onesv"], [pNk])
                ACT(pn[:].rearrange("p h b -> p (h b)"), pN[0:64, 0:4 * NS], AF.Exp, [pNk], ["pn"])
                for h in range(4):
                    g = h // 2
                    TT(num[:, h, :], pn[:, h, :], vnT[:, g, :], ALU.mult, ["pn", ("vnT", g)], ["num"])
                TT(num[:].rearrange("p h b -> p (h b)"), num[:].rearrange("p h b -> p (h b)"), pO2[0:64, 0:4 * NS], ALU.add,
                   ["num", pO2k], ["num"])
                TT(dn[:].rearrange("p h b -> p (h b)"), pn[:].rearrange("p h b -> p (h b)"), pD2[0:64, 0:4 * NS], ALU.add,
                   ["pn", pD2k], ["dn"])
                for h in range(4):
                    TS(dn[:, h, :], dn[:, h, :], esink[:, l * 4 + h:l * 4 + h + 1], ALU.add, ["dn", "esink"], ["dn"])
                RCP(dn[:].rearrange("p h b -> p (h b)"), dn[:].rearrange("p h b -> p (h b)"), ["dn"], ["dn"])
                for h in range(4):
                    TT(oT[0:64, 4 + h, t0:t0 + NS], num[:, h, :], dn[:, h, :], ALU.mult, ["num", "dn"], [("o", 4 + h, t0)])
            S.barrier()

        def conv(si, l, last):
            areset()
            wv3 = W["w_in"][l].rearrange("(k p) f -> p k f", p=128)
            Wg, gk = wg_slot()
            wc = wview(Wg, 0, 8, 512)
            wpw = wview(Wg, 4096, 2, 256)
            Dg = aalloc([2, 31, 128], BF16)
            uT = aalloc([2, 30 + TP], BF16)
            sig = aalloc([2, 512], F32)
            ulast = aalloc([2, 30], F32)
            yb = aalloc([2, 512], F32)
            ybf = aalloc([2, 512], BF16)
            ysq = aalloc([2, 512], BF16)
            msb = aalloc([1, 512], F32)[:, 0, :]
            var = aalloc([1, 512], F32)[:, 0, :]
            rb = aalloc([1, 512], F32)[:, 0, :]
            dd = aalloc([2, 512], F32)
            ysl = aalloc([2, 512], BF16)
            WDMA(wc[:], wv3[:, :, 2048:2560], gk)
            WDMA(wpw[:], W["conv_pw"][l].rearrange("(k p) o -> p k o", p=128), gk)
            if si == 0:
                wcv = wview(Wg, 4608, 1, 256, parts=30)[:, 0, :]
                WDMA(wcv, W["conv_w"][l, 0:30, :], gk)
            for c in range(2):
                for j in range(31):
                    TS(Dg[:, c, j, :], identb[:, :], convw[:, l, c, j:j + 1], ALU.mult, ["identb", "convw"], ["Dg"])
                CP(uT[:, c, 0:30], uh[:, l, c, :], ["uh"], [("uT", c, -1)])

            def ln_epilogue(n, t0):
                pm, pmk = pb()
                pq, pqk = pb()
                for c in range(2):
                    MM(pm[:, 0:n], onesv[:, 3, :], ybf[:, c, 0:n], c == 0, c == 1, [("ybf", c), "onesv"], [pmk])
                for c in range(2):
                    MM(pq[:, 0:n], onesv[:, 3, :], ysq[:, c, 0:n], c == 0, c == 1, [("ysq", c), "onesv"], [pqk])
                CP(msb[:, 0:n], pm[:, 0:n], [pmk], ["msb"], eng="act")
                STT(var[:, 0:n], msb[:, 0:n], -1.0, msb[:, 0:n], ALU.mult, ALU.mult, ["msb"], ["var"])
                TT(var[:, 0:n], var[:, 0:n], pq[:, 0:n], ALU.add, ["var", pqk], ["var"])
                ACT(rb[:, 0:n], var[:, 0:n], AF.Ln, ["var", "epsb"], ["rbC"], bias=epsb[:, 0:1])
                ACT(rb[:, 0:n], rb[:, 0:n], AF.Exp, ["rbC"], ["rbC"], scale=-0.5)
                for c in range(2):
                    TT(dd[:, c, 0:n], yb[:, c, 0:n], msb[:, 0:n], ALU.subtract, [("yb", c), "msb"], [("dd", c)])
                    TT(dd[:, c, 0:n], dd[:, c, 0:n], rb[:, 0:n], ALU.mult, [("dd", c), "rbC"], [("dd", c)])
                    ACT(ysl[:, c, 0:n], dd[:, c, 0:n], AF.Silu, [("dd", c), "lng", "lnb"], [("ysl", c)],
                        bias=lnb[:, l, c:c + 1], scale=lng[:, l, c:c + 1])
                for oc in range(2):
                    p, pk = pb()
                    for c in range(2):
                        MM(p[:, 0:n], wpw[:, c, oc * 128:(oc + 1) * 128], ysl[:, c, 0:n], c == 0, c == 1,
                           [gk, ("ysl", c)], [pk])
                    CP(oT[:, 8 + oc, t0:t0 + n], p[:, 0:n], [pk], [("o", 8 + oc, t0)], eng="act" if oc else "dve")

            uS = aalloc([2, NS], F32)
            for (t0, n) in tiles_of(si):
                for c in range(2):
                    pa, pak = pb()
                    pg, pgk = pb()
                    for k in range(8):
                        MM(pa[:, 0:n], wc[:, k, c * 128:(c + 1) * 128], hT[:, k, t0:t0 + n], k == 0, k == 7,
                           [gk, ("h", t0)], [pak])
                    for k in range(8):
                        MM(pg[:, 0:n], wc[:, k, 256 + c * 128:256 + (c + 1) * 128], hT[:, k, t0:t0 + n], k == 0, k == 7,
                           [gk, ("h", t0)], [pgk])
                    ACT(sig[:, c, 0:n], pg[:, 0:n], AF.Sigmoid, [pgk], [("sig", c)])
                    if t0 < TP:
                        TT(uT[:, c, 30 + t0:30 + t0 + n], pa[:, 0:n], sig[:, c, 0:n], ALU.mult, [pak, ("sig", c)],
                           [("uT", c, t0)])
                        if t0 == 512:
                            TT(ulast[:, c, :], pa[:, 482:512], sig[:, c, 482:512], ALU.mult, [pak, ("sig", c)],
                               [("ulast", c)])
                            if last:
                                DMA("sp", cvp[l, :, c * 128:(c + 1) * 128].rearrange("t p -> p t"), ulast[:, c, :],
                                    [("ulast", c)], (), "cvp", final=True, slow=True)
                            CP(uh[:, l, c, :], uT[:, c, TP:TP + 30], [("uT", c, 512)], ["uh"])
                    else:
                        TT(uS[:, c, :], pa[:, 0:n], sig[:, c, 0:n], ALU.mult, [pak, ("sig", c)], [("uS", c)])
            wst = wout_prepare(l)
            taps = {}
            for ti in range(2):
                t0 = ti * 512
                for c in range(2):
                    p, pk = pb()
                    rk = [("uT", c, t0), ("uT", c, t0 - 512 if t0 else -1), "Dg"]
                    for j in range(31):
                        MM(p[:, :], Dg[:, c, j, :], uT[:, c, t0 + j:t0 + j + 512], j == 0, j == 30, rk, [pk])
                    taps[(ti, c)] = (p, pk)
            def evac(ti):
                for c in range(2):
                    p, pk = taps[(ti, c)]
                    ACT(yb[:, c, :], p[:, :], AF.Identity, [pk, "convb"], [("yb", c)], bias=convb[:, l, c:c + 1])
                    ACT(ysq[:, c, :], p[:, :], AF.Square, [pk, "convb"], [("ysq", c)], bias=convb[:, l, c:c + 1])
                    CP(ybf[:, c, :], yb[:, c, :], [("yb", c)], [("ybf", c)])

            evac(0)
            ln_epilogue(512, 0)
            evac(1)
            wout_tile(wst, 0, 512)
            ln_epilogue(512, 512)
            if si == 0:
                t0 = TP
                cbb = aalloc([NS, 256], BF16, parts=30)
                msk = aalloc([4, 128], F32)
                y0 = aalloc([2, NS], F32)
                DMA("pool", cbb[:], sconv[l].rearrange("b j c -> j b c"), (), ["cbb"], "cbb")
                DMA("sp", cvs[l, :, 0:29, :], sconv[l, :, 1:30, :], (), (), "cvs", final=True)
                for c in range(2):
                    DMA("sp", cvs[l, :, 29, c * 128:(c + 1) * 128].rearrange("b p -> p b"), uS[:, c, :], [("uS", c)], (),
                        "cvs2", final=True, slow=True)
                    for q in range(NS // 4):
                        p, pk = pb()
                        MM(p[:, :], wcv[:, c * 128:(c + 1) * 128], cbb[:, 4 * q:4 * q + 4, c * 128:(c + 1) * 128], True, True,
                           [gk, "cbb"], [pk])
                        TT(msk[:], p[:, :].rearrange("p (a b) -> p a b", a=4),
                           ident[:, :].unsqueeze(1).to_broadcast([128, 4, 128]), ALU.mult, [pk, "ident"], ["msk"])
                        S.op("dve", (lambda o_, i_: (lambda e: e.reduce_sum(o_, i_, axis=AX.X)))(y0[:, c, 4 * q:4 * q + 4],
                                                                                                  msk[:]),
                             ["msk"], [("y0", c)])
                    STT(yb[:, c, 0:NS], uS[:, c, :], convw[:, l, c, 30:31], y0[:, c, :], ALU.mult, ALU.add,
                        [("uS", c), ("y0", c), "convw"], [("yb", c)])
                    TS(yb[:, c, 0:NS], yb[:, c, 0:NS], convb[:, l, c:c + 1], ALU.add, [("yb", c), "convb"], [("yb", c)])
                    CP(ybf[:, c, 0:NS], yb[:, c, 0:NS], [("yb", c)], [("ybf", c)])
                    ACT(ysq[:, c, 0:NS], yb[:, c, 0:NS], AF.Square, [("yb", c)], [("ysq", c)])
                ln_epilogue(NS, t0)
            S.barrier()
            wout_tile(wst, 512, 512)
            if si == 0:
                wout_tile(wst, TP, NS)

        def wout_prepare(l):
            Wg, gk = wg_slot()
            Wd, dk = wd_slot()
            wo8 = wview(Wg, 0, 8, D)
            wo2 = wview(Wd, 0, 2, D)
            wod = W["w_out"][l]
            WDMA(wo8[:, 0:4, :], wod[0:512, :].rearrange("(k p) d -> p k d", p=128), gk)
            WDMA(wo8[0:64, 4:8, :], wod[512:768, :].rearrange("(k p) d -> p k d", p=64), gk)
            WDMA(wo2[:, :, :], wod[768:1024, :].rearrange("(k p) d -> p k d", p=128), dk)
            return wo8, wo2, gk, dk

        def wout_tile(wst, t0, n):
            wo8, wo2, gk, dk = wst
            for c in range(8):
                p, pk = pb()
                for k in range(10):
                    kp = 64 if 4 <= k < 8 else 128
                    wsrc = wo8[0:kp, k, c * 128:(c + 1) * 128] if k < 8 else wo2[:, k - 8, c * 128:(c + 1) * 128]
                    MM(p[:, 0:n], wsrc, oT[0:kp, k, t0:t0 + n], k == 0, k == 9, [gk, dk, ("o", k, t0)], [pk])
                TT(xT[:, c, t0:t0 + n], xT[:, c, t0:t0 + n], p[:, 0:n], ALU.add, [pk, ("x", c, t0)], [("x", c, t0)])

        PH = DBG.get("phases")

        def on(name):
            return PH is None or name in PH

        for si in range(DBG.get("nseg", NSEG)):
            last = si == DBG.get("nseg", NSEG) - 1
            load_x(si)
            for l in range(DBG.get("depth", DEPTH)):
                if on("ffn1"):
                    ffn(si, l, 1)
                if on("mixnorm"):
                    areset()
                    rmsnorm(si, mng, l)
                    S.barrier()
                if on("ret"):
                    for pr in range(2):
                        retention(si, l, pr, last)
                if on("att"):
                    attention(si, l, last)
                if on("conv"):
                    conv(si, l, last)
                if on("ffn2"):
                    ffn(si, l, 2)
            store_y(si)
        S.emit()
        print("sched stats", S.stats, flush=True)
    return nc


_CACHE = {}


def kernel(**inputs):
    f32 = lambda a: np.ascontiguousarray(np.asarray(a), dtype=np.float32)
    inp = {k: f32(v) for k, v in inputs.items()}
    if "nc" not in _CACHE:
        _CACHE["nc"] = build()
    nc = _CACHE["nc"]
    ctab, _ = const_tables()
    wnames = ["ffn1_norm", "ffn1_wg", "ffn1_wu", "ffn1_wd", "mix_norm", "w_in", "ret_norm_g", "q_norm_g", "k_norm_g",
              "sinks", "conv_w", "conv_b", "conv_ln_g", "conv_ln_b", "conv_pw", "w_out", "ffn2_norm", "ffn2_wg",
              "ffn2_wu", "ffn2_wd"]
    in_maps = []
    zx = np.zeros((NSEG * TP, D), np.float32)
    for core in range(NCORES):
        m = {k: inp[k] for k in wnames}
        m.update(ctab)
        if core in ACT_CORES:
            c = ACT_CORES.index(core)
            sl = slice(c * NS, (c + 1) * NS)
            m["xp"] = inp["x_prompt"][c]
            m["xs"] = np.ascontiguousarray(inp["x_sample"][sl, 0, :])
            m["sret"] = np.ascontiguousarray(inp["state_ret"][:, sl])
            m["ck"] = np.ascontiguousarray(inp["cache_k_win"][:, sl].reshape(DEPTH, NS, 128, 128))
            m["cv"] = np.ascontiguousarray(inp["cache_v_win"][:, sl].reshape(DEPTH, NS, 128, 128))
            m["sconv"] = np.ascontiguousarray(inp["state_conv"][:, sl])
        else:
            m["xp"] = zx
            m["xs"] = np.zeros((NS, D), np.float32)
            m["sret"] = np.zeros((DEPTH, NS, 4, 64, 128), np.float32)
            m["ck"] = np.zeros((DEPTH, NS, 128, 128), np.float32)
            m["cv"] = np.zeros((DEPTH, NS, 128, 128), np.float32)
            m["sconv"] = np.zeros((DEPTH, NS, 30, 256), np.float32)
        in_maps.append(m)
    res = run_bass_kernel_spmd(nc, in_maps, core_ids=list(range(NCORES)))
    R = res.results
    A = ACT_CORES
    y_p = np.stack([R[c]["yp"] for c in A]).reshape(BATCH, SEQ, D)
    y_s = np.concatenate([R[c]["ys"] for c in A]).reshape(DECB, 1, D)
    ret_p = np.stack([R[c]["retp"] for c in A], axis=1)
    ret_s = np.concatenate([R[c]["rets"] for c in A], axis=1)
    kw_p = np.stack([R[c]["kwp"] for c in A], axis=1).reshape(DEPTH, BATCH, 128, 2, 64)
    kw_s = np.concatenate([R[c]["kws"] for c in A], axis=1).reshape(DEPTH, DECB, 128, 2, 64)
    vw_p = np.stack([R[c]["vwp"] for c in A], axis=1).reshape(DEPTH, BATCH, 128, 2, 64)
    vw_s = np.concatenate([R[c]["vws"] for c in A], axis=1).reshape(DEPTH, DECB, 128, 2, 64)
    cv_p = np.stack([R[c]["cvp"] for c in A], axis=1)
    cv_s = np.concatenate([R[c]["cvs"] for c in A], axis=1)
    outs = (y_p, y_s, ret_p, ret_s, kw_p, kw_s, vw_p, vw_s, cv_p, cv_s)
    return tuple(np.ascontiguousarray(o, dtype=np.float32) for o in outs)
```
